# Optimizing a Trainium2 kernel written in Bass

```python
import jax, jax.numpy as jnp
from jax import lax
import numpy as np

D_MODEL = 1024
BATCH = 8
SEQ = 4096
DEPTH = 1

GLA_HEADS = 4
GLA_DK = D_MODEL // 2 // GLA_HEADS
GLA_DV = D_MODEL // GLA_HEADS
GLA_QK_W = GLA_HEADS * GLA_DK
GLA_V_W = GLA_HEADS * GLA_DV
GLA_GATE_RANK = 16
GLA_GATE_NORMALIZER = 16.0
GLA_LOG_GATE_MIN = -0.5
GLA_CHUNK = 64
SWA_HEADS = 8
SWA_KV_HEADS = 2
SWA_HEAD_DIM = 128
SWA_Q_W = SWA_HEADS * SWA_HEAD_DIM
SWA_KV_W = SWA_KV_HEADS * SWA_HEAD_DIM
SWA_WINDOW = 128
SWA_BLOCK = 128
ROPE_THETA = 10000.0
D_FF = -(-8 * D_MODEL // (3 * 256)) * 256
NORM_EPS = 1e-6
IN_SPLITS = (GLA_QK_W, GLA_QK_W, GLA_V_W, GLA_V_W, GLA_GATE_RANK, GLA_GATE_RANK,
             SWA_Q_W, SWA_KV_W, SWA_KV_W, D_MODEL, D_MODEL)
D_IN = sum(IN_SPLITS)

kernel_name = "hybrid_gla_swa_gated_encoder_block"


def _rms_norm(x, g):
    xf = x.astype(jnp.float32)
    y = xf * lax.rsqrt(jnp.mean(xf * xf, axis=-1, keepdims=True) + NORM_EPS)
    return (y * g.astype(jnp.float32)).astype(x.dtype)


def _rope(x, pos):
    half = x.shape[-1] // 2
    inv_freq = ROPE_THETA ** (-jnp.arange(half, dtype=jnp.float32) / half)
    ang = pos.astype(jnp.float32)[:, None] * inv_freq[None, :]
    cos = jnp.cos(ang)[None, :, None, :]
    sin = jnp.sin(ang)[None, :, None, :]
    x1, x2 = x[..., :half], x[..., half:]
    return jnp.concatenate([x1 * cos - x2 * sin, x2 * cos + x1 * sin], axis=-1)


def _gla_direction(q, k, v, log_g, include_diag):
    bsz, nh, s, dk = q.shape
    dv = v.shape[-1]
    c = GLA_CHUNK
    n = s // c
    q = q.reshape(bsz, nh, n, c, dk)
    k = k.reshape(bsz, nh, n, c, dk)
    log_g = log_g.reshape(bsz, nh, n, c, dk)
    v = v.reshape(bsz, nh, n, c, dv)
    b = jnp.cumsum(log_g, axis=3)
    b_last = b[:, :, :, -1:, :]
    q_dec = q * jnp.exp(b)
    k_inv = k * jnp.exp(-b)
    k_tail = k * jnp.exp(b_last - b)
    mask = jnp.tril(jnp.ones((c, c), dtype=bool), k=0 if include_diag else -1)
    scores = jnp.where(mask, jnp.einsum("bhncd,bhnmd->bhncm", q_dec, k_inv), 0.0)
    o_intra = jnp.einsum("bhncm,bhnme->bhnce", scores, v)
    chunk_kv = jnp.einsum("bhnmd,bhnme->nbhde", k_tail, v)
    chunk_decay = jnp.exp(jnp.moveaxis(b_last[:, :, :, 0, :], 2, 0))

    def step(state, inp):
        kv_n, dec_n = inp
        return dec_n[..., None] * state + kv_n, state

    _, states = lax.scan(step, jnp.zeros((bsz, nh, dk, dv), q.dtype), (chunk_kv, chunk_decay))
    o_inter = jnp.einsum("bhncd,nbhde->bhnce", q_dec, states)
    return (o_intra + o_inter).reshape(bsz, nh, s, dv)


def _gla_branch(q, k, v, r, lr_f, lr_b, up_f, bias_f, up_b, bias_b, out_g):
    bsz, s, _ = q.shape
    f32 = jnp.float32

    def heads(t, d):
        return t.astype(f32).reshape(bsz, s, GLA_HEADS, d).transpose(0, 2, 1, 3)

    def log_gate(lr, up, bias):
        z = lr.astype(f32) @ up.astype(f32) + bias.astype(f32)
        lg = jnp.maximum(jax.nn.log_sigmoid(z) / GLA_GATE_NORMALIZER, GLA_LOG_GATE_MIN)
        return heads(lg, GLA_DK)

    qh = heads(q, GLA_DK) * (GLA_DK ** -0.5)
    kh = heads(k, GLA_DK)
    vh = heads(v, GLA_DV)
    o_fwd = _gla_direction(qh, kh, vh, log_gate(lr_f, up_f, bias_f), True)
    flip = lambda t: jnp.flip(t, axis=2)
    o_bwd = flip(_gla_direction(flip(qh), flip(kh), flip(vh),
                                flip(log_gate(lr_b, up_b, bias_b)), False))
    o = _rms_norm(o_fwd + o_bwd, out_g)
    o = o.transpose(0, 2, 1, 3).reshape(bsz, s, GLA_V_W)
    return (o * jax.nn.silu(r.astype(f32))).astype(q.dtype)


def _swa_branch(q, k, v, q_g, k_g, sinks):
    bsz, s, _ = q.shape
    f32 = jnp.float32
    hd, hkv, blk = SWA_HEAD_DIM, SWA_KV_HEADS, SWA_BLOCK
    grp = SWA_HEADS // SWA_KV_HEADS
    n = s // blk
    pos = jnp.arange(s)
    q = _rope(_rms_norm(q.astype(f32).reshape(bsz, s, SWA_HEADS, hd), q_g), pos)
    k = _rope(_rms_norm(k.astype(f32).reshape(bsz, s, hkv, hd), k_g), pos)
    v = v.astype(f32).reshape(bsz, s, hkv, hd)

    def context(t):
        tp = jnp.pad(t, ((0, 0), (blk, blk), (0, 0), (0, 0)))
        return jnp.concatenate(
            [tp[:, o * blk:o * blk + s].reshape(bsz, n, blk, hkv, hd) for o in range(3)], axis=2)

    kc, vc = context(k), context(v)
    qb = q.reshape(bsz, n, blk, hkv, grp, hd)
    scores = jnp.einsum("bnqhgd,bnkhd->bnhgqk", qb, kc) * (hd ** -0.5)
    q_off = jnp.arange(blk)
    k_off = jnp.arange(3 * blk) - blk
    in_window = jnp.abs(k_off[None, :] - q_off[:, None]) <= SWA_WINDOW
    k_abs = jnp.arange(n)[:, None] * blk + k_off[None, :]
    in_seq = (k_abs >= 0) & (k_abs < s)
    mask = in_window[None, :, :] & in_seq[:, None, :]
    scores = jnp.where(mask[None, :, None, None], scores, -jnp.inf)
    sink = sinks.astype(f32).reshape(1, 1, hkv, grp, 1, 1)
    m = jnp.maximum(scores.max(axis=-1, keepdims=True), sink)
    p = jnp.exp(scores - m)
    denom = p.sum(axis=-1, keepdims=True) + jnp.exp(sink - m)
    o = jnp.einsum("bnhgqk,bnkhd->bnqhgd", p / denom, vc)
    return o.reshape(bsz, s, SWA_Q_W)


def setup_inputs(seed: int = 0) -> dict:
    key = jax.random.key(seed)
    ks = jax.random.split(key, 20)
    f32 = jnp.float32
    nrm = lambda k, shape, scale: jax.random.normal(k, shape, f32) * scale
    gain = lambda k, d: 1.0 + 0.02 * jax.random.normal(k, (DEPTH, d), f32)
    return {
        "x": nrm(ks[0], (BATCH, SEQ, D_MODEL), 1.0),
        "norm_mix_g": gain(ks[1], D_MODEL),
        "w_in": nrm(ks[2], (DEPTH, D_MODEL, D_IN), D_MODEL ** -0.5),
        "gla_gate_up_fwd": nrm(ks[3], (DEPTH, GLA_GATE_RANK, GLA_QK_W), GLA_GATE_RANK ** -0.5),
        "gla_gate_bias_fwd": nrm(ks[4], (DEPTH, GLA_QK_W), 0.02),
        "gla_gate_up_bwd": nrm(ks[5], (DEPTH, GLA_GATE_RANK, GLA_QK_W), GLA_GATE_RANK ** -0.5),
        "gla_gate_bias_bwd": nrm(ks[6], (DEPTH, GLA_QK_W), 0.02),
        "gla_out_norm_g": gain(ks[7], GLA_DV),
        "w_o_gla": nrm(ks[8], (DEPTH, GLA_V_W, D_MODEL), GLA_V_W ** -0.5),
        "swa_q_norm_g": gain(ks[9], SWA_HEAD_DIM),
        "swa_k_norm_g": gain(ks[10], SWA_HEAD_DIM),
        "swa_sinks": nrm(ks[11], (DEPTH, SWA_HEADS), 0.5),
        "w_o_swa": nrm(ks[12], (DEPTH, SWA_Q_W, D_MODEL), SWA_Q_W ** -0.5),
        "w_out": nrm(ks[13], (DEPTH, D_MODEL, D_MODEL), D_MODEL ** -0.5),
        "norm_ffn_g": gain(ks[14], D_MODEL),
        "w_ffn_in": nrm(ks[15], (DEPTH, D_MODEL, 2 * D_FF), D_MODEL ** -0.5),
        "w_ffn_out": nrm(ks[16], (DEPTH, D_FF, D_MODEL), D_FF ** -0.5),
    }


def reference(x, norm_mix_g, w_in, gla_gate_up_fwd, gla_gate_bias_fwd, gla_gate_up_bwd,
              gla_gate_bias_bwd, gla_out_norm_g, w_o_gla, swa_q_norm_g, swa_k_norm_g,
              swa_sinks, w_o_swa, w_out, norm_ffn_g, w_ffn_in, w_ffn_out):
    split_at = tuple(int(i) for i in np.cumsum(IN_SPLITS)[:-1])
    for l in range(DEPTH):
        h = _rms_norm(x, norm_mix_g[l])
        proj = h @ w_in[l]
        (g_q, g_k, g_v, g_r, g_lr_f, g_lr_b,
         s_q, s_k, s_v, gate_a, gate_b) = jnp.split(proj, split_at, axis=-1)
        y_gla = _gla_branch(g_q, g_k, g_v, g_r, g_lr_f, g_lr_b,
                            gla_gate_up_fwd[l], gla_gate_bias_fwd[l],
                            gla_gate_up_bwd[l], gla_gate_bias_bwd[l],
                            gla_out_norm_g[l]) @ w_o_gla[l]
        y_swa = _swa_branch(s_q, s_k, s_v, swa_q_norm_g[l], swa_k_norm_g[l],
                            swa_sinks[l]).astype(x.dtype) @ w_o_swa[l]
        merged = jax.nn.sigmoid(gate_a) * y_gla + jax.nn.sigmoid(gate_b) * y_swa
        x = x + merged @ w_out[l]
        h2 = _rms_norm(x, norm_ffn_g[l])
        gu = h2 @ w_ffn_in[l]
        ff_gate, ff_up = gu[..., :D_FF], gu[..., D_FF:]
        x = x + (jax.nn.silu(ff_gate) * ff_up) @ w_ffn_out[l]
    return x
```

```python
import numpy as np
from contextlib import ExitStack
import concourse.bass as bass
import concourse.mybir as mybir
from concourse.bass_utils import run_bass_kernel_spmd

F32 = mybir.dt.float32
BF16 = mybir.dt.bfloat16
AF = mybir.ActivationFunctionType
ALU = mybir.AluOpType
AX = mybir.AxisListType

S = 4096
D = 1024
NT = 32
NG = 8
DFF = 2816
NJ = 22
EPS = 1e-6
SEM_LIMIT = 30000


class Buf:
    __slots__ = ("w", "r")

    def __init__(self):
        self.w = None
        self.r = []


class SemObj:
    __slots__ = ("h", "val", "id")
    _n = 0

    def __init__(self, h):
        self.h = h
        self.val = 0
        SemObj._n += 1
        self.id = SemObj._n


class Eng:
    def __init__(self, fw, name):
        self.fw = fw
        self.name = name
        self.ops = []
        self.sem = None
        self.waited = {}

    def cur_sem(self):
        if self.sem is None or self.sem.val >= SEM_LIMIT:
            self.sem = self.fw.new_sem(self.name)
        return self.sem


class FW:
    def __init__(self, nc, n_dma_sems=48):
        self.nc = nc
        self.engs = {n: Eng(self, n) for n in ("pe", "act", "dve", "pool", "sp")}
        self.dma_pool = []
        self.dma_pools = {}
        self.dma_rrs = {}
        self.n_dma_sems = n_dma_sems
        self.sems = []

    def new_sem(self, name):
        h = self.nc.alloc_semaphore(name=f"s_{name}_{len(self.sems)}")
        s = SemObj(h)
        self.sems.append(s)
        return s

    def _waits_for(self, eng, reads, writes):
        deps = {}

        def add(tok):
            if tok is None:
                return
            s, v = tok
            if deps.get(s.id, (None, -1))[1] < v:
                deps[s.id] = (s, v)
        for b in reads:
            add(b.w)
        for b in writes:
            add(b.w)
            for t in b.r:
                add(t)
        out = []
        for sid, (s, v) in deps.items():
            if eng.name == "pe" and eng.sem is not None and sid == eng.sem.id:
                continue
            if eng.waited.get(sid, -1) >= v:
                continue
            eng.waited[sid] = v
            out.append((s, v))
        return out

    def _commit(self, tok, reads, writes):
        for b in writes:
            b.w = tok
            b.r = []
        for b in reads:
            if b in writes:
                continue
            b.r.append(tok)
            if len(b.r) > 48:
                best = {}
                for s, v in b.r:
                    if best.get(s.id, (None, -1))[1] < v:
                        best[s.id] = (s, v)
                b.r = list(best.values())

    def op(self, engname, fns, reads=(), writes=()):
        eng = self.engs[engname]
        if not isinstance(fns, (list, tuple)):
            fns = [fns]
        waits = self._waits_for(eng, reads, writes)
        sem = eng.cur_sem()
        sem.val += 1
        tok = (sem, sem.val)
        fns = list(fns)

        def emit(e, waits=waits, fns=fns, sem=sem):
            for s, v in waits:
                e.wait_ge(s.h, v)
            ins = None
            for f in fns:
                ins = f(e)
            ins.then_inc(sem.h, 1)
        eng.ops.append(emit)
        self._commit(tok, reads, writes)
        return tok

    def dma(self, qname, fn, reads=(), writes=()):
        eng = self.engs[qname]
        waits = self._waits_for(eng, reads, writes)
        pool = self.dma_pools.setdefault(qname, [])
        npool = self.n_dma_sems if qname == "sp" else 16
        if len(pool) < npool:
            s = self.new_sem("dma" + qname)
            pool.append(s)
            self.dma_pool.append(s)
        else:
            k = self.dma_rrs.get(qname, 0)
            s = pool[k % npool]
            self.dma_rrs[qname] = k + 1
        prev = s.val
        pre = []
        if prev > 0 and eng.waited.get(s.id, -1) < prev:
            pre.append((s, prev))
            eng.waited[s.id] = prev
        s.val += 16
        tok = (s, s.val)

        def emit(e, waits=waits + pre, fn=fn, s=s):
            for ss, v in waits:
                e.wait_ge(ss.h, v)
            fn(e).then_inc(s.h, 16)
        eng.ops.append(emit)
        self._commit(tok, reads, writes)
        return tok

    def barrier(self):
        toks = []
        for n in ("pe", "act", "dve", "pool"):
            s = self.engs[n].sem
            if s is not None and s.val > 0:
                toks.append((s, s.val))
        for s in self.dma_pool:
            if s.val > 0:
                toks.append((s, s.val))
        for n, eng in self.engs.items():
            ws = []
            for s, v in toks:
                if eng.waited.get(s.id, -1) >= v:
                    continue
                if eng.sem is not None and s.id == eng.sem.id:
                    continue
                eng.waited[s.id] = v
                ws.append((s, v))

            def emit(e, ws=ws):
                for s, v in ws:
                    e.wait_ge(s.h, v)
            eng.ops.append(emit)

    def emit_all(self):
        nc = self.nc
        with nc.Block() as block:
            @block.tensor
            def _(e):
                for f in self.engs["pe"].ops:
                    f(e)

            @block.scalar
            def _(e):
                for f in self.engs["act"].ops:
                    f(e)

            @block.vector
            def _(e):
                for f in self.engs["dve"].ops:
                    f(e)

            @block.gpsimd
            def _(e):
                for f in self.engs["pool"].ops:
                    f(e)

            @block.sync
            def _(e):
                for f in self.engs["sp"].ops:
                    f(e)


ARENA_BF = 106400


class Arena:
    def __init__(self, ap):
        self.ap = ap
        self.off = 0
        self.top = ARENA_BF
        self.hw = 0

    def mark(self):
        return self.off

    def alloc_at(self, off, free_shape, dt):
        n = 1
        for s_ in free_shape:
            n *= s_
        units = n * (2 if dt == F32 else 1)
        assert off % 16 == 0 and off + units <= self.top
        v = self.ap[:, off:off + units]
        if dt == F32:
            v = v.bitcast(F32)
        if len(free_shape) == 2:
            v = v.rearrange("p (a b) -> p a b", b=free_shape[1])
        elif len(free_shape) == 3:
            v = v.rearrange("p (a b c) -> p a b c", b=free_shape[1], c=free_shape[2])
        return v

    def alloc_top(self, free_shape, dt):
        n = 1
        for s in free_shape:
            n *= s
        units = n * (2 if dt == F32 else 1)
        self.top = (self.top - units) // 16 * 16
        assert self.top >= self.off, "arena overflow (top)"
        o = self.top
        v = self.ap[:, o:o + units]
        if dt == F32:
            v = v.bitcast(F32)
        if len(free_shape) == 2:
            v = v.rearrange("p (a b) -> p a b", b=free_shape[1])
        return v

    def release(self, m):
        self.off = m

    def alloc(self, free_shape, dt):
        n = 1
        for s in free_shape:
            n *= s
        units = n * (2 if dt == F32 else 1)
        self.off = (self.off + 15) // 16 * 16
        o = self.off
        self.off += units
        assert self.off <= self.top, f"arena overflow {self.off} > {self.top}"
        self.hw = max(self.hw, self.off)
        v = self.ap[:, o:o + units]
        if dt == F32:
            v = v.bitcast(F32)
        if len(free_shape) == 2:
            v = v.rearrange("p (a b) -> p a b", b=free_shape[1])
        elif len(free_shape) == 3:
            v = v.rearrange("p (a b c) -> p a b c", b=free_shape[1], c=free_shape[2])
        return v


def build_program():
    nc = bass.Bass("TRN2", target_bir_lowering=False)

    def din(name, shape, dt=F32):
        return nc.dram_tensor(name, list(shape), dt, kind="ExternalInput").ap()

    x_d = din("x", [S, D])
    w_lr_d = din("w_lr", [128, 8 * 32])
    w_sq_d = din("w_sq", [128, 8 * 1024])
    w_skv_d = din("w_skv", [128, 8 * 512])
    w_gb_d = din("w_gb", [128, 8 * 1024])
    w_ga_d = din("w_ga", [128, 8 * 1024])
    w_os_d = din("w_os", [128, 8 * 1024])
    w_og_d = din("w_og", [128, 8 * 1024])
    w_out_d = din("w_out", [128, 8 * 1024])
    w_gh_d = [din(f"w_gh{h}", [128, 8 * 768]) for h in range(4)]
    w_fg_d = din("w_fg", [128, 8 * DFF])
    w_fu_d = din("w_fu", [128, 8 * DFF])
    w_fo_d = din("w_fo", [128, NJ * 1024])
    upf_d = din("upaug_f", [33, 512])
    upb_d = din("upaug_b", [33, 512])
    gmix_d = din("gmix_col", [128, 8])
    gffn_d = din("gffn_col", [128, 8])
    gout_d = din("gout", [1, 256])
    gq_d = din("gq", [1, 128])
    gk_d = din("gk", [1, 128])
    sinks_d = din("sinks", [1, 8])
    cos_d = din("cos_t", [128, NT * 64])
    sin_d = din("sin_t", [128, NT * 64])
    out_d = nc.dram_tensor("out", [S, D], F32, kind="ExternalOutput").ap()
    hT_d = nc.dram_tensor("hT_scr", [128, 8, S], BF16, kind="Internal").ap()
    ys_d = nc.dram_tensor("ys_scr", [128, 8, S], BF16, kind="Internal").ap()
    yg_d = nc.dram_tensor("yg_scr", [128, 8, S], BF16, kind="Internal").ap()
    h2_d = nc.dram_tensor("h2_scr", [128, 8, S], BF16, kind="Internal").ap()

    fw = FW(nc)
    es = ExitStack()
    with es:
        arena_t = es.enter_context(nc.sbuf_tensor("arena", [128, ARENA_BF], BF16))
        pp = es.enter_context(nc.psum_tensor("pp", [128, 8, 512], F32))
        ar = Arena(arena_t)
        PB = [Buf() for _ in range(8)]

        def bank(b):
            return pp[:, b, :]

        def bank16(b):
            return pp[:, b, :].bitcast(BF16)

        def mm(out_ap, pairs, reads, writes, extra=None):
            fns = []
            n = len(pairs)
            for i, (l, r) in enumerate(pairs):
                fns.append(lambda e, l=l, r=r, i=i, n=n, o=out_ap: e.matmul(
                    o, lhsT=l, rhs=r, start=(i == 0), stop=(i == n - 1)))
            if extra:
                fns = fns + extra
            return fw.op("pe", fns, reads, writes)

        def mmfns(out_ap, pairs):
            fns = []
            n = len(pairs)
            for i, (l, r) in enumerate(pairs):
                fns.append(lambda e, l=l, r=r, i=i, n=n, o=out_ap: e.matmul(
                    o, lhsT=l, rhs=r, start=(i == 0), stop=(i == n - 1)))
            return fns

        def act(out, in_, func, reads, writes, **kw):
            return fw.op("act", lambda e: e.activation(out=out, in_=in_, func=func, **kw), reads, writes)

        def tt(eng, out, in0, in1, op, reads, writes):
            return fw.op(eng, lambda e: e.tensor_tensor(out=out, in0=in0, in1=in1, op=op), reads, writes)

        def ts(eng, out, in0, s1, s2, op0, op1, reads, writes):
            if s2 is None:
                return fw.op(eng, lambda e: e.tensor_scalar(out=out, in0=in0, scalar1=s1, scalar2=None, op0=op0), reads, writes)
            return fw.op(eng, lambda e: e.tensor_scalar(out=out, in0=in0, scalar1=s1, scalar2=s2, op0=op0, op1=op1), reads, writes)

        def stt(eng, out, in0, scalar, in1, op0, op1, reads, writes):
            return fw.op(eng, lambda e: e.scalar_tensor_tensor(out=out, in0=in0, scalar=scalar, in1=in1, op0=op0, op1=op1), reads, writes)

        def copy(eng, out, in_, reads, writes):
            if eng == "act":
                return act(out, in_, AF.Copy, reads, writes)
            return fw.op(eng, lambda e: e.tensor_copy(out=out, in_=in_), reads, writes)

        def dma(q, out, in_, reads, writes):
            return fw.dma(q, lambda e: e.dma_start(out=out, in_=in_), reads, writes)

        def rstd_from_ss(ss, ms, rs, n, width, Bss, Bms, Brs, nhalf):
            ts("dve", ms, ss, 1.0 / n, EPS, ALU.mult, ALU.add, [Bss], [Bms])
            tt("pool", rs, ms, nhalf[:, 0:width], ALU.pow, [Bms, Bconst], [Brs])

        ident = ar.alloc([128], BF16)
        ones_bf = ar.alloc([128], BF16)
        mask2 = ar.alloc([256], BF16)
        mprev = ar.alloc([4, 128], BF16)
        mnext = ar.alloc([4, 128], BF16)
        gmix = ar.alloc([8], F32)
        gffn = ar.alloc([8], F32)
        nhalf = ar.alloc([16], F32)
        rm = ar.alloc([512], F32)
        Bconst = Buf()
        pool_ms = lambda ap, v: fw.op("pool", lambda e: e.memset(ap, v), [], [Bconst])

        def asel(ap, pattern, cm, cmp, fill=0.0):
            fw.op("pool", lambda e: e.affine_select(out=ap, in_=ap, pattern=pattern, compare_op=cmp,
                                                    fill=fill, base=0, channel_multiplier=cm), [Bconst], [Bconst])
        pool_ms(ident, 1.0)
        asel(ident, [[-1, 128]], 1, ALU.is_equal)
        pool_ms(nhalf, -0.5)
        dma("sp", gmix, gmix_d, [], [Bconst])
        dma("sp", gffn, gffn_d, [], [Bconst])

        lrT = ar.alloc([S], BF16)
        BlrT = Buf()
        fw.op("pool", lambda e: e.memset(lrT[32:33, :], 1.0), [], [BlrT])

        wh0_top = ar.alloc_top([8, 768], BF16)
        upf = ar.alloc_top([512], BF16)
        upb = ar.alloc_top([512], BF16)
        gout = ar.alloc_top([256], F32)
        Bc3 = Buf()
        dma("pool", upf[0:33, :], upf_d, [], [Bc3])
        dma("pool", upb[0:33, :], upb_d, [], [Bc3])
        dma("sp", gout, gout_d.partition_broadcast(128), [], [Bc3])
        ts("dve", gout, gout, 0.5, None, ALU.mult, None, [Bc3], [Bc3])
        top_after_wh0 = ar.top
        wsq = ar.alloc_top([8, 1024], BF16)
        wskv = ar.alloc_top([8, 512], BF16)
        Bw2 = Buf()
        wlr_top = ar.alloc_top([8, 32], BF16)
        Bwlr = Buf()
        dma("pool", wlr_top.rearrange("p a b -> p (a b)"), w_lr_d, [], [Bwlr])
        dma("pool", wskv.rearrange("p a b -> p (a b)"), w_skv_d, [], [Bw2])
        for kc in range(0, 8, 2):
            dma("pool", wsq[:, kc:kc + 2, :].rearrange("p a b -> p (a b)"),
                w_sq_d[:, kc * 1024:(kc + 2) * 1024], [], [Bw2])
        pool_ms(ones_bf, 1.0)
        pool_ms(mask2, 1.0)
        asel(mask2[:, 0:128], [[1, 128]], -1, ALU.is_ge)
        asel(mask2[:, 128:256], [[-1, 128]], 1, ALU.is_gt)
        pool_ms(mprev, 0.0)
        asel(mprev, [[0, 4], [-1, 128]], 1, ALU.is_ge, fill=-30000.0)
        pool_ms(mnext, 0.0)
        asel(mnext, [[0, 4], [1, 128]], -1, ALU.is_ge, fill=-30000.0)
        pool_ms(rm, 1.0)
        pool_ms(rm.rearrange("p (c t) -> p c t", t=128)[:, :, 0:1], 0.0)
        cos_t = ar.alloc_top([NT, 64], F32)
        sin_t = ar.alloc_top([NT, 64], F32)
        gqk = ar.alloc_top([2, 128], F32)
        sk8 = ar.alloc_top([8], F32)
        se8 = ar.alloc_top([8], F32)
        sinkrow = ar.alloc_top([8, 128], BF16)
        negshift = ar.alloc_top([1], F32)
        tmpc = ar.alloc_top([128], F32)
        mx = ar.alloc_top([4], F32)
        Bc2 = Buf()
        dma("sp", cos_t.rearrange("p a b -> p (a b)"), cos_d, [], [Bc2])
        dma("sp", sin_t.rearrange("p a b -> p (a b)"), sin_d, [], [Bc2])
        dma("sp", gqk[:, 0, :], gq_d.partition_broadcast(128), [], [Bc2])
        dma("sp", gqk[:, 1, :], gk_d.partition_broadcast(128), [], [Bc2])
        dma("sp", sk8, sinks_d.partition_broadcast(128), [], [Bc2])
        tt("dve", tmpc, gqk[:, 0, :], gqk[:, 0, :], ALU.mult, [Bc2], [Bc2])
        fw.op("dve", lambda e: e.tensor_reduce(out=mx[:, 0:1], in_=tmpc, axis=AX.X, op=ALU.max), [Bc2], [Bc2])
        tt("dve", tmpc, gqk[:, 1, :], gqk[:, 1, :], ALU.mult, [Bc2], [Bc2])
        fw.op("dve", lambda e: e.tensor_reduce(out=mx[:, 1:2], in_=tmpc, axis=AX.X, op=ALU.max), [Bc2], [Bc2])
        tt("dve", mx[:, 2:3], mx[:, 0:1], mx[:, 1:2], ALU.mult, [Bc2], [Bc2])
        tt("pool", mx[:, 3:4], mx[:, 2:3], nhalf[:, 0:1], ALU.pow, [Bc2, Bconst], [Bc2])
        fw.op("dve", lambda e: e.reciprocal(out=mx[:, 2:3], in_=mx[:, 3:4]), [Bc2], [Bc2])
        ts("dve", negshift, mx[:, 2:3], -(128.0 ** 0.5), None, ALU.mult, None, [Bc2], [Bc2])
        act(se8, sk8, AF.Exp, [Bc2], [Bc2], bias=negshift)
        copy("dve", sinkrow[0:1, :, :], se8[0:1, :].unsqueeze(2).to_broadcast([1, 8, 128]), [Bc2], [Bc2])


        def rstd_act(ss, ms, rs, n, Bss, Bms, Brs):
            act(ms, ss, AF.Ln, [Bss], [Bms], scale=1.0 / n, bias=EPS)
            act(rs, ms, AF.Exp, [Bms], [Brs], scale=-0.5)

        def norm_a(xt, Bxt, junk, Bjunk, st, Bst, use_act=False):
            act(junk, xt, AF.Square, [Bxt], [Bjunk, Bst[0]], accum_out=st[0])
            if use_act:
                rstd_act(st[0], st[1], st[2], D, Bst[0], Bst[1], Bst[2])
            else:
                rstd_from_ss(st[0], st[1], st[2], D, 1, Bst[0], Bst[1], Bst[2], nhalf)

        def norm_b(xt, Bxt, st, Bst, xs, Bxs, tb_bank, stage, Bstage, slot, gcol):
            act(xs, xt, AF.Copy, [Bxt, Bst[2]], [Bxs], scale=st[2])
            tp = bank16(tb_bank).rearrange("p (c t) -> p c t", t=128)
            fns = [lambda e, c=c: e.transpose(out=tp[:, c, :], in_=xs[:, c * 128:(c + 1) * 128], identity=ident)
                   for c in range(8)]
            fw.op("pe", fns, [Bxs, Bconst], [PB[tb_bank]])
            tt("dve", stage[:, :, slot * 128:(slot + 1) * 128], tp,
               gcol.unsqueeze(2).to_broadcast([128, 8, 128]), ALU.mult, [PB[tb_bank], Bconst], [Bstage])

        m0 = ar.mark()
        wlr = wlr_top
        NX = 6
        xts = [ar.alloc([D], F32) for _ in range(NX)]
        Bxts = [Buf() for _ in range(NX)]
        xss = [ar.alloc([D], BF16) for _ in range(2)]
        Bxss = [Buf() for _ in range(2)]
        junk = ar.alloc([D], BF16)
        Bjunk = Buf()
        stats = [[ar.alloc([1], F32) for _ in range(3)] for _ in range(NX)]
        Bstats = [[Buf() for _ in range(3)] for _ in range(NX)]
        hst = [ar.alloc([8, 512], BF16) for _ in range(3)]
        Bhst = [Buf() for _ in range(3)]
        BhT = [Buf() for _ in range(NG)]

        def p1_load(t):
            dma("sp", xts[t % NX], x_d[t * 128:(t + 1) * 128, :], [], [Bxts[t % NX]])

        def p1_a(t):
            if t + 3 < NT:
                p1_load(t + 3)
            norm_a(xts[t % NX], Bxts[t % NX], junk, Bjunk, stats[t % NX], Bstats[t % NX], use_act=True)

        def p1_b(t):
            g, sl = divmod(t, 4)
            norm_b(xts[t % NX], Bxts[t % NX], stats[t % NX], Bstats[t % NX], xss[t % 2], Bxss[t % 2],
                   t % 2, hst[g % 3], Bhst[g % 3], sl, gmix)

        def p1_store(g):
            hs, Bhs = hst[g % 3], Bhst[g % 3]
            dma("sp", hT_d[:, :, g * 512:(g + 1) * 512], hs, [Bhs], [BhT[g]])
            mm(bank(2)[0:32, :], [(wlr[:, kc, :], hs[:, kc, :]) for kc in range(8)], [Bwlr, Bhs], [PB[2]])

        def p1_lrcopy(g):
            copy("act", lrT[0:32, g * 512:(g + 1) * 512], bank(2)[0:32, :], [PB[2]], [BlrT])
        for t_ in range(3):
            p1_load(t_)
        for t in range(NT + 8):
            if t < NT:
                p1_a(t)
            if 0 <= t - 2 < NT:
                p1_b(t - 2)
            if t >= 5 and (t - 5) % 4 == 3 and (t - 5) // 4 < NG:
                p1_store((t - 5) // 4)
            if t >= 7 and (t - 7) % 4 == 3 and (t - 7) // 4 < NG:
                p1_lrcopy((t - 7) // 4)
        fw.barrier()
        ar.release(m0)
        print("arena hw phase1", ar.hw, "top", ar.top)

        m0 = ar.mark()
        Bwhs = [Buf() for _ in range(2)]
        dma("pool", wh0_top.rearrange("p a b -> p (a b)"), w_gh_d[0], [], [Bwhs[0]])
        hgs = [ar.alloc([8, 512], BF16) for _ in range(2)]
        Bhgs = [Buf() for _ in range(2)]
        kT_all = ar.alloc([2, 8 * 128], BF16)
        BkT = [Buf() for _ in range(8)]
        v_all = ar.alloc([8, 256], BF16)
        Bv = [Buf() for _ in range(8)]
        qsb = [ar.alloc([10, 128], F32) for _ in range(2)]
        Bqsb = [Buf() for _ in range(2)]
        qT = ar.alloc([8, 8, 128], BF16)
        BqT = [Buf() for _ in range(8)]
        yst = [ar.alloc([8, 512], BF16) for _ in range(2)]
        Byst = [Buf() for _ in range(2)]
        Pt = [[[ar.alloc([512], BF16) for _ in range(3)] for _ in range(2)] for _ in range(2)]
        BPt = [[[Buf() for _ in range(3)] for _ in range(2)] for _ in range(2)]
        sq1 = ar.alloc([10, 128], F32)
        sq = [sq1, sq1]
        qn = [ar.alloc([10, 128], F32) for _ in range(2)]
        rA = [ar.alloc([10, 128], F32) for _ in range(2)]
        rB = [ar.alloc([10, 128], F32) for _ in range(2)]
        qr = [ar.alloc([10, 128], BF16) for _ in range(3)]
        cg = [ar.alloc([2, 128], F32) for _ in range(2)]
        sgn = [ar.alloc([2, 128], F32) for _ in range(2)]
        Bsq1 = Buf()
        Bsq = [Bsq1, Bsq1]
        Bqn = [Buf() for _ in range(2)]
        BrA = [Buf() for _ in range(2)]
        BrB = [Buf() for _ in range(2)]
        Bqr = [Buf() for _ in range(3)]
        Bcg = [Buf() for _ in range(2)]
        st10 = [[ar.alloc([10], F32) for _ in range(3)] for _ in range(2)]
        Bst10 = [[Buf() for _ in range(3)] for _ in range(2)]
        lnden = [ar.alloc([512], F32) for _ in range(2)]
        rden = lnden
        Blnden = [Buf() for _ in range(2)]
        Brden = Blnden
        Bys = [Buf() for _ in range(NG)]
        inv_sqrt_hd = 128.0 ** -0.5

        def swa_proj_a(t):
            g, sl = divmod(t, 4)
            p = t % 2
            hg, Bhg = hgs[g % 2], Bhgs[g % 2]
            if sl == 0:
                dma("sp", hg, hT_d[:, :, g * 512:(g + 1) * 512], [BhT[g]], [Bhg])
            lhs = [hg[:, kc, sl * 128:(sl + 1) * 128] for kc in range(8)]
            qf = qsb[p].rearrange("p a b -> p (a b)")
            mm(bank(0), [(lhs[kc], wsq[:, kc, 0:512]) for kc in range(8)], [Bhg, Bw2], [PB[0]])
            copy("act", qf[:, 0:512], bank(0), [PB[0]], [Bqsb[p]])
            mm(bank(1), [(lhs[kc], wsq[:, kc, 512:1024]) for kc in range(8)], [Bhg, Bw2], [PB[1]])
            copy("act", qf[:, 512:1024], bank(1), [PB[1]], [Bqsb[p]])
            mm(bank(0), [(lhs[kc], wskv[:, kc, :]) for kc in range(8)], [Bhg, Bw2], [PB[0]])
            copy("act", qf[:, 1024:1280], bank(0)[:, 0:256], [PB[0]], [Bqsb[p]])
            copy("act", v_all[:, t % 8, :], bank(0)[:, 256:512], [PB[0]], [Bv[t % 8]])
            act(sq[p], qsb[p], AF.Square, [Bqsb[p]], [Bsq[p]])
            st, Bst = st10[p], Bst10[p]
            fw.op("dve", lambda e: e.tensor_reduce(out=st[0], in_=sq[p], axis=AX.X, op=ALU.add), [Bsq[p]], [Bst[0]])
            c2 = cos_t[:, t, :].unsqueeze(1).unsqueeze(1).to_broadcast([128, 2, 2, 64])
            s2_ = sin_t[:, t, :].unsqueeze(1).unsqueeze(1).to_broadcast([128, 2, 2, 64])
            tt("pool", cg[p].rearrange("p a (h j) -> p a h j", h=2), gqk.rearrange("p a (h j) -> p a h j", h=2), c2,
               ALU.mult, [Bc2], [Bcg[p]])
            tt("pool", sgn[p].rearrange("p a (h j) -> p a h j", h=2), gqk.rearrange("p a (h j) -> p a h j", h=2), s2_,
               ALU.mult, [Bc2], [Bcg[p]])

        def swa_proj_b(t):
            p = t % 2
            rs = st10[p][2]
            Brs = Bst10[p][2]
            rstd_act(st10[p][0], st10[p][1], rs, 128, Bst10[p][0], Bst10[p][1], Brs)
            tt("dve", qn[p], qsb[p], rs.unsqueeze(2).to_broadcast([128, 10, 128]), ALU.mult,
               [Bqsb[p], Brs], [Bqn[p]])
            tt("pool", rA[p][:, 0:8, :], qn[p][:, 0:8, :], cg[p][:, 0:1, :].to_broadcast([128, 8, 128]),
               ALU.mult, [Bqn[p], Bcg[p]], [BrA[p]])
            tt("pool", rA[p][:, 8:10, :], qn[p][:, 8:10, :], cg[p][:, 1:2, :].to_broadcast([128, 2, 128]),
               ALU.mult, [Bqn[p], Bcg[p]], [BrA[p]])
            tt("dve", rB[p][:, 0:8, :], qn[p][:, 0:8, :], sgn[p][:, 0:1, :].to_broadcast([128, 8, 128]),
               ALU.mult, [Bqn[p], Bcg[p]], [BrB[p]])
            tt("dve", rB[p][:, 8:10, :], qn[p][:, 8:10, :], sgn[p][:, 1:2, :].to_broadcast([128, 2, 128]),
               ALU.mult, [Bqn[p], Bcg[p]], [BrB[p]])
            tt("dve", qr[t % 3][:, :, 0:64], rA[p][:, :, 0:64], rB[p][:, :, 64:128], ALU.subtract,
               [BrA[p], BrB[p]], [Bqr[t % 3]])
            tt("pool", qr[t % 3][:, :, 64:128], rA[p][:, :, 64:128], rB[p][:, :, 0:64], ALU.add,
               [BrA[p], BrB[p]], [Bqr[t % 3]])

        def swa_transpose_a(t):
            slot = t % 8
            p = t % 3
            tp = bank16(3).rearrange("p (c t) -> p c t", t=128)
            fns = [lambda e, c=c: e.transpose(out=tp[:, c, :], in_=qr[p][:, 8 + c, :], identity=ident) for c in range(2)]
            fns += [lambda e, c=c: e.transpose(out=tp[:, 2 + c, :], in_=qr[p][:, c, :], identity=ident) for c in range(6)]
            fw.op("pe", fns, [Bqr[p], Bconst], [PB[3]])
            copy("dve", kT_all[:, :, slot * 128:(slot + 1) * 128], tp[:, 0:2, :], [PB[3]], [BkT[slot]])
            copy("dve", qT[:, slot, 0:6, :], tp[:, 2:8, :], [PB[3]], [BqT[slot]])

        def swa_transpose_b(t):
            slot = t % 8
            p = t % 3
            tp = bank16(3).rearrange("p (c t) -> p c t", t=128)
            fns = [lambda e, c=c: e.transpose(out=tp[:, c, :], in_=qr[p][:, 6 + c, :], identity=ident) for c in range(2)]
            fw.op("pe", fns, [Bqr[p], Bconst], [PB[3]])
            copy("dve", qT[:, slot, 6:8, :], tp[:, 0:2, :], [PB[3]], [BqT[slot]])

        sbank = [5, 6]
        scnt = [0]

        def swa_scores(b):
            slot = b % 8
            for kvh in range(2):
                for oi, o in enumerate((b - 1, b, b + 1)):
                    if o < 0 or o >= NT:
                        continue
                    sbk = sbank[scnt[0] % 2]
                    scnt[0] += 1
                    pairs = [(kT_all[:, kvh, (o % 8) * 128:(o % 8 + 1) * 128],
                              qT[:, slot, kvh * 4:(kvh + 1) * 4, :].rearrange("p a b -> p (a b)"))]
                    if o != b:
                        m = mprev if o < b else mnext
                        pairs.append((ident, m.rearrange("p a b -> p (a b)")))
                    mm(bank(sbk), pairs, [BkT[o % 8], BqT[slot], Bconst], [PB[sbk]])
                    P, BP = Pt[b % 2][kvh][oi], BPt[b % 2][kvh][oi]
                    act(P, bank(sbk), AF.Exp, [PB[sbk], Bc2], [BP], bias=negshift, scale=inv_sqrt_hd)

        def swa_pv(b):
            g, sl = divmod(b, 4)
            ys, Bys_ = yst[g % 2], Byst[g % 2]
            for kvh in range(2):
                ob = 7 if kvh == 0 else 2
                valid = [(oi, o) for oi, o in enumerate((b - 1, b, b + 1)) if 0 <= o < NT]
                pairs = [(v_all[:, o % 8, kvh * 128:(kvh + 1) * 128], Pt[b % 2][kvh][oi]) for oi, o in valid]
                rd = [Bv[o % 8] for _, o in valid] + [BPt[b % 2][kvh][oi] for oi, _ in valid]
                mm(bank(ob), pairs, rd, [PB[ob]])
                pairs2 = [(ones_bf, Pt[b % 2][kvh][oi]) for oi, _ in valid]
                pairs2.append((ones_bf[0:1, :], sinkrow[0:1, kvh * 4:(kvh + 1) * 4, :].rearrange("p a b -> p (a b)")))
                mm(bank(4), pairs2, rd + [Bconst, Bc2], [PB[4]])
                act(lnden[kvh], bank(4), AF.Ln, [PB[4]], [Blnden[kvh]])
                act(rden[kvh], lnden[kvh], AF.Exp, [Blnden[kvh]], [Brden[kvh]], scale=-1.0)
                tt("dve", ys[:, kvh * 4:(kvh + 1) * 4, sl * 128:(sl + 1) * 128],
                   bank(ob).rearrange("p (a b) -> p a b", b=128), rden[kvh].rearrange("p (a b) -> p a b", b=128),
                   ALU.mult, [PB[ob], Brden[kvh]], [Bys_])
            if sl == 3:
                dma("sp", ys_d[:, :, g * 512:(g + 1) * 512], ys, [Bys_], [Bys[g]])

        for it in range(NT + 6):
            if it < NT:
                swa_proj_a(it)
            if 0 <= it - 3 < NT:
                swa_transpose_b(it - 3)
            if 0 <= it - 6 < NT:
                swa_pv(it - 6)
            if 0 <= it - 5 < NT:
                swa_scores(it - 5)
            if 0 <= it - 2 < NT:
                swa_transpose_a(it - 2)
            if it < NT:
                swa_proj_b(it)
        fw.barrier()
        ar.release(m0)
        ar.top = top_after_wh0
        print("arena hw phase2", ar.hw)

        m0 = ar.mark()
        whs = [wh0_top, ar.alloc([8, 768], BF16)]
        hgs = [ar.alloc([8, 512], BF16) for _ in range(2)]
        Bhgs = [Buf() for _ in range(2)]
        QD = [ar.alloc([S], BF16) for _ in range(2)]
        KI = [ar.alloc([S], BF16) for _ in range(2)]
        BQK = [[Buf() for _ in range(NG)] for _ in range(2)]
        KIt = [ar.alloc([NT, 128], BF16) for _ in range(2)]
        BKIt = [[Buf() for _ in range(NG)] for _ in range(2)]
        Vh = ar.alloc([NT, 256], BF16)
        BVh = [Buf() for _ in range(NT)]
        S2 = ar.alloc([NT, 256], BF16)
        BS2 = [Buf() for _ in range(NT)]
        SBh = ar.alloc([NT, 256], BF16)
        BSB = [Buf() for _ in range(NT)]
        eT = [ar.alloc([NT], F32) for _ in range(2)]
        BeT = [[Buf() for _ in range(NG)] for _ in range(2)]
        ar.off = (ar.off + 15) // 16 * 16
        tmp_off = ar.off
        Lg_ = [[ar.alloc([512], F32) for _ in range(2)] for _ in range(2)]
        Pp = [[ar.alloc([512], F32) for _ in range(2)] for _ in range(2)]
        Pex = [ar.alloc([512], F32) for _ in range(2)]
        Ep = [[ar.alloc([512], F32) for _ in range(2)] for _ in range(2)]
        En = [[ar.alloc([512], F32) for _ in range(2)] for _ in range(2)]
        BLg = [[Buf() for _ in range(2)] for _ in range(2)]
        BPp = [[Buf() for _ in range(2)] for _ in range(2)]
        BPex = [Buf() for _ in range(2)]
        BEp = [[Buf() for _ in range(2)] for _ in range(2)]
        BEn = [[Buf() for _ in range(2)] for _ in range(2)]
        tnh = [ar.alloc([256], F32) for _ in range(2)]
        Btnh = [Buf() for _ in range(2)]
        Sf = ar.alloc([256], F32)
        Sb = ar.alloc([256], F32)
        Sp = [ar.alloc([256], F32) for _ in range(2)]
        Se = [ar.alloc([256], F32) for _ in range(2)]
        BSf, BSb = Buf(), Buf()
        BSp = [Buf() for _ in range(2)]
        BSe = [Buf() for _ in range(2)]
        Sfb = [ar.alloc([256], BF16) for _ in range(2)]
        BSfb = [Buf() for _ in range(2)]
        Amat = [ar.alloc([256], BF16) for _ in range(2)]
        BA = [Buf() for _ in range(2)]
        on = [ar.alloc([256], F32) for _ in range(2)]
        Bon = [Buf() for _ in range(2)]
        yb = [ar.alloc([256], BF16) for _ in range(2)]
        Byb = [Buf() for _ in range(2)]
        ost = [[ar.alloc([1], F32) for _ in range(3)] for _ in range(3)]
        Bost = [[Buf() for _ in range(3)] for _ in range(3)]
        OBS = [7, 0, 3]
        ojunk = ar.alloc([256], BF16)
        Bojunk = Buf()
        ygst = [ar.alloc([2, 512], BF16) for _ in range(2)]
        Bygst = [Buf() for _ in range(2)]
        Byg = [[Buf() for _ in range(NG)] for _ in range(4)]
        dk_scale = 128.0 ** -0.5
        PB7h = [Buf(), Buf()]

        def load_wh(h):
            dma("pool", whs[h % 2].rearrange("p a b -> p (a b)"), w_gh_d[h], [], [Bwhs[h % 2]])

        for h in range(4):
            wh, Bwh = whs[h % 2], Bwhs[h % 2]
            if h + 1 < 4:
                load_wh(h + 1)

            def g1(g):
                p = g % 2
                for d in range(2):
                    up = upf if d == 0 else upb
                    mm(bank(0), [(up[0:33, h * 128:(h + 1) * 128], lrT[0:33, g * 512:(g + 1) * 512])],
                       [Bc3, BlrT], [PB[0]])
                    L = Lg_[p][d]
                    act(L, bank(0), AF.Exp, [PB[0]], [BLg[p][d]], scale=-1.0)
                    act(L, L, AF.Ln, [BLg[p][d]], [BLg[p][d]], bias=1.0)
                    ts("dve", L, L, -1.0 / 16.0, -0.5, ALU.mult, ALU.max, [BLg[p][d]], [BLg[p][d]])
                    fw.op("dve", lambda e, L=L, P=Pp[p][d]: e.tensor_tensor_scan(
                        out=P, data0=rm, data1=L, initial=0.0, op0=ALU.mult, op1=ALU.add),
                        [BLg[p][d], Bconst], [BPp[p][d]])
                tt("dve", Pex[p], Pp[p][1], Lg_[p][1], ALU.subtract, [BPp[p][1], BLg[p][1]], [BPex[p]])
                for d in range(2):
                    P4 = Pp[p][d].rearrange("p (c t) -> p c t", t=128)
                    act(eT[d][:, g * 4:(g + 1) * 4], P4[:, :, 127], AF.Exp, [BPp[p][d]], [BeT[d][g]])
                    src, Bsrc = (Pp[p][0], BPp[p][0]) if d == 0 else (Pex[p], BPex[p])
                    act(Ep[p][d], src, AF.Exp, [Bsrc], [BEp[p][d]])
                    act(En[p][d], src, AF.Exp, [Bsrc], [BEn[p][d]], scale=-1.0)

            def g2(g):
                p = g % 2
                hg, Bhg = hgs[p], Bhgs[p]
                if not (h > 0 and g < 2):
                    dma("sp", hg, hT_d[:, :, g * 512:(g + 1) * 512], [BhT[g]], [Bhg])
                mm(bank(1 + p), [(wh[:, kc, 0:128], hg[:, kc, :]) for kc in range(8)], [Bwh, Bhg], [PB[1 + p]])
                mm(bank(3 + p), [(wh[:, kc, 128:256], hg[:, kc, :]) for kc in range(8)], [Bwh, Bhg], [PB[3 + p]])

            def g3(g):
                p = g % 2
                gs = slice(g * 512, (g + 1) * 512)
                stt("dve", QD[0][:, gs], bank(1 + p), dk_scale, Ep[p][0], ALU.mult, ALU.mult,
                    [PB[1 + p], BEp[p][0]], [BQK[0][g]])
                stt("dve", QD[1][:, gs], bank(1 + p), dk_scale, En[p][1], ALU.mult, ALU.mult,
                    [PB[1 + p], BEn[p][1]], [BQK[1][g]])
                tt("dve", KI[0][:, gs], bank(3 + p), En[p][0], ALU.mult, [PB[3 + p], BEn[p][0]], [BQK[0][g]])
                tt("dve", KI[1][:, gs], bank(3 + p), Ep[p][1], ALU.mult, [PB[3 + p], BEp[p][1]], [BQK[1][g]])

            def g4(g):
                tp = bank16(5 + (g % 2)).rearrange("p (c t) -> p c t", t=128)
                fns = [lambda e, c=c, d=d: e.transpose(out=tp[:, d * 4 + c, :],
                                                       in_=KI[d][:, (g * 4 + c) * 128:(g * 4 + c + 1) * 128],
                                                       identity=ident) for d in range(2) for c in range(4)]
                fw.op("pe", fns, [BQK[0][g], BQK[1][g], Bconst], [PB[5 + (g % 2)]])
                for d in range(2):
                    copy("act", KIt[d][:, g * 4:(g + 1) * 4, :], tp[:, d * 4:(d + 1) * 4, :], [PB[5 + (g % 2)]],
                         [BKIt[d][g]])
            for it in range(NG + 2):
                if it < NG:
                    g1(it)
                    g2(it)
                if 0 <= it - 1 < NG:
                    g3(it - 1)
                if 0 <= it - 2 < NG:
                    g4(it - 2)

            if h == 3:
                tmp_bufs = [b for row in BLg for b in row] + [b for row in BPp for b in row] + BPex + \
                           [b for row in BEp for b in row] + [b for row in BEn for b in row]
                w4_pref = [ar.alloc_at(tmp_off + i * 8192, [4, 8, 256], BF16) for i in range(2)]
                Bw4_pref = [[Buf() for _ in range(4)] for _ in range(2)]
                for ob4 in range(2):
                    for wi, wd in enumerate((w_og_d, w_ga_d, w_os_d, w_gb_d)):
                        src = wd[:, ob4 * 2048:(ob4 + 1) * 2048].rearrange("p (a b) -> p a b", b=256)
                        extra = tmp_bufs if (ob4 == 0 and wi == 0) else []
                        dma("pool", w4_pref[ob4][:, wi, :, :], src, [], [Bw4_pref[ob4][wi]] + extra)

            fw.op("pool", lambda e: e.memset(Sb, 0.0), [], [BSb])

            def bwd_step(n):
                g = n // 4
                p = n % 2
                act(Sp[p], Sb, AF.Copy, [BSb, BeT[1][g]], [BSp[p]], scale=eT[1][:, n:n + 1])
                act(SBh[:, n, :], Sb, AF.Copy, [BSb, BeT[1][g]], [BSB[n]], scale=eT[1][:, n:n + 1])
                mm(bank(5 + p)[:, 0:256], [(KIt[1][:, n, :], Vh[:, n, :])],
                   [BKIt[1][g], BVh[n]], [PB[5 + p]])
                tt("dve", Sb, bank(5 + p)[:, 0:256], Sp[p], ALU.add, [PB[5 + p], BSp[p]], [BSb])
            pending = []
            for g in range(NG - 1, -1, -1):
                hg, Bhg = hgs[g % 2], Bhgs[g % 2]
                dma("sp", hg, hT_d[:, :, g * 512:(g + 1) * 512], [BhT[g]], [Bhg])
                for sl in range(3, -1, -1):
                    t = g * 4 + sl
                    lhs = [hg[:, kc, sl * 128:(sl + 1) * 128] for kc in range(8)]
                    bk = 1 + (t % 4)
                    fns = mmfns(bank(bk)[:, 0:256], [(lhs[kc], wh[:, kc, 256:512]) for kc in range(8)])
                    fns += mmfns(bank(bk)[:, 256:512], [(lhs[kc], wh[:, kc, 512:768]) for kc in range(8)])
                    fw.op("pe", fns, [Bhg, Bwh], [PB[bk]])
                    copy("act", Vh[:, t, :], bank(bk)[:, 0:256], [PB[bk]], [BVh[t]])
                    act(tnh[t % 2], bank(bk)[:, 256:512], AF.Tanh, [PB[bk]], [Btnh[t % 2]], scale=0.5)
                    stt("dve", S2[:, t, :], tnh[t % 2], 1.0, bank(bk)[:, 256:512], ALU.add, ALU.mult,
                        [Btnh[t % 2], PB[bk]], [BS2[t]])
                    pending.append(t)
                    if len(pending) > 2:
                        bwd_step(pending.pop(0))
            while pending:
                bwd_step(pending.pop(0))

            fw.op("pool", lambda e: e.memset(Sf, 0.0), [], [BSf])
            fw.op("pool", lambda e: e.memset(Sfb[0], 0.0), [], [BSfb[0]])

            def f_main(n):
                g, sl = divmod(n, 4)
                p = n % 2
                q3 = n % 3
                cs = slice(n * 128, (n + 1) * 128)
                A, BA_ = Amat[p], BA[p]
                sb_ = 5 + p
                fns = mmfns(bank(sb_)[:, 0:128], [(KI[0][:, cs], QD[0][:, cs])])
                fns += mmfns(bank(sb_)[:, 128:256], [(KI[1][:, cs], QD[1][:, cs])])
                fw.op("pe", fns, [BQK[0][g], BQK[1][g]], [PB[sb_]])
                tt("dve", A, bank(sb_)[:, 0:256], mask2, ALU.mult, [PB[sb_], Bconst], [BA_])
                bk = 1 + p
                mm(bank(bk)[:, 0:256], [(KIt[0][:, n, :], Vh[:, n, :])], [BKIt[0][g], BVh[n]], [PB[bk]])
                cur, nxt = Sfb[p], Sfb[1 - p]
                Bcur, Bnxt = BSfb[p], BSfb[1 - p]
                ob = OBS[q3]
                mm(bank(ob)[:, 0:256],
                   [(A[:, 0:128], Vh[:, n, :]), (A[:, 128:256], Vh[:, n, :]),
                    (QD[0][:, cs], cur), (QD[1][:, cs], SBh[:, n, :])],
                   [BA_, BVh[n], BQK[0][g], BQK[1][g], Bcur, BSB[n]], [PB[ob]])
                act(Se[p], Sf, AF.Copy, [BSf, BeT[0][g]], [BSe[p]], scale=eT[0][:, n:n + 1])
                stt("dve", Sf, bank(bk)[:, 0:256], eT[0][:, n:n + 1], Se[p], ALU.mult, ALU.add,
                    [PB[bk], BeT[0][g], BSe[p]], [BSf])
                copy("dve", nxt, Sf, [BSf], [Bnxt])

            def f_sq(n):
                q3 = n % 3
                ob = OBS[q3]
                act(ojunk, bank(ob)[:, 0:256], AF.Square, [PB[ob]], [Bojunk, Bost[q3][0]], accum_out=ost[q3][0])

            def f_out(n):
                g, sl = divmod(n, 4)
                p = n % 2
                q3 = n % 3
                ob = OBS[q3]
                rstd_act(ost[q3][0], ost[q3][1], ost[q3][2], 256, Bost[q3][0], Bost[q3][1], Bost[q3][2])
                stt("dve", on[p], bank(ob)[:, 0:256], ost[q3][2], gout, ALU.mult, ALU.mult,
                    [PB[ob], Bost[q3][2], Bc3], [Bon[p]])
                tt("pool", yb[p], on[p], S2[:, n, :], ALU.mult, [Bon[p], BS2[n]], [Byb[p]])

            def f_tr(n):
                g, sl = divmod(n, 4)
                p = n % 2
                tp = bank16(4).rearrange("p (c t) -> p c t", t=128)
                fns = [lambda e, c=c: e.transpose(out=tp[:, c, :], in_=yb[p][:, c * 128:(c + 1) * 128],
                                                  identity=ident) for c in range(2)]
                fw.op("pe", fns, [Byb[p], Bconst], [PB[4]])
                ygs, Bygs = ygst[g % 2], Bygst[g % 2]
                copy("act", ygs[:, :, sl * 128:(sl + 1) * 128], tp[:, 0:2, :], [PB[4]], [Bygs])
                if sl == 3:
                    dma("sp", yg_d[:, 2 * h:2 * h + 2, g * 512:(g + 1) * 512], ygs, [Bygs], [Byg[h][g]])
            for it in range(NT + 3):
                if 0 <= it - 3 < NT:
                    f_tr(it - 3)
                if it < NT:
                    f_main(it)
                if 0 <= it - 2 < NT:
                    f_out(it - 2)
                if 0 <= it - 1 < NT:
                    f_sq(it - 1)
        fw.barrier()
        ar.release(m0)
        ar.top = ARENA_BF
        print("arena hw phase3", ar.hw)

        m0 = ar.mark()
        wblk = [None] * 4
        Bw4 = [None] * 4
        wblk[0], wblk[1] = w4_pref
        Bw4[0], Bw4[1] = Bw4_pref
        wblk[2] = ar.alloc([4, 8, 256], BF16)
        wblk[3] = ar.alloc([4, 8, 256], BF16)
        Bw4[2] = [Buf() for _ in range(4)]
        Bw4[3] = [Buf() for _ in range(4)]
        wout = ar.alloc([8, 1024], BF16)
        Bwout = Buf()
        for ob4 in range(2, 4):
            for wi, wd in enumerate((w_og_d, w_ga_d, w_os_d, w_gb_d)):
                src = wd[:, ob4 * 2048:(ob4 + 1) * 2048].rearrange("p (a b) -> p a b", b=256)
                dep = Bw4[2] if ob4 == 3 else []
                dma("pool", wblk[ob4][:, wi, :, :], src, dep, [Bw4[ob4][wi]])
        for kc in range(0, 8, 2):
            dma("pool", wout[:, kc:kc + 2, :].rearrange("p a b -> p (a b)"),
                w_out_d[:, kc * 1024:(kc + 2) * 1024], Bw4[3], [Bwout])
        ygg = [ar.alloc([8, 512], BF16) for _ in range(2)]
        ysg = [ar.alloc([8, 512], BF16) for _ in range(2)]
        hgg = [ar.alloc([8, 512], BF16) for _ in range(2)]
        Bygg = [Buf() for _ in range(2)]
        Bysg = [Buf() for _ in range(2)]
        Bhgg = [Buf() for _ in range(2)]
        Mg = [ar.alloc([8, 512], BF16)]
        BMg = [Buf()]
        tg = [ar.alloc([512], F32) for _ in range(2)]
        Btg = [Buf() for _ in range(2)]
        Ag = [ar.alloc([512], F32) for _ in range(2)]
        BAg = [Buf() for _ in range(2)]
        xts = [ar.alloc([D], F32) for _ in range(3)]
        Bxts = [Buf() for _ in range(3)]
        assert ar.off <= tmp_off, (ar.off, tmp_off)
        ar.off = tmp_off + 2 * 8192
        x1s = [ar.alloc([D], F32) for _ in range(3)]
        Bx1s = [Buf() for _ in range(3)]
        xss = [ar.alloc([D], BF16) for _ in range(2)]
        Bxss = [Buf() for _ in range(2)]
        junk = ar.alloc([D], BF16)
        Bjunk = Buf()
        stats = [[ar.alloc([1], F32) for _ in range(3)] for _ in range(3)]
        Bstats = [[Buf() for _ in range(3)] for _ in range(3)]
        h2st = [ar.alloc([8, 512], BF16) for _ in range(2)]
        Bh2st = [Buf() for _ in range(2)]
        Bx1 = [Buf() for _ in range(NT)]
        Bh2 = [Buf() for _ in range(NG)]

        def p4_load(g):
            gs = slice(g * 512, (g + 1) * 512)
            dma("sp", ygg[g % 2], yg_d[:, :, gs], [Byg[h_][g] for h_ in range(4)], [Bygg[g % 2]])
            dma("sp", ysg[g % 2], ys_d[:, :, gs], [Bys[g]], [Bysg[g % 2]])
            dma("sp", hgg[g % 2], hT_d[:, :, gs], [BhT[g]], [Bhgg[g % 2]])

        def p4_merge(g):
            yg_, ys_, hg_ = ygg[g % 2], ysg[g % 2], hgg[g % 2]
            M, BM = Mg[0], BMg[0]
            for oc in range(8):
                wb = oc // 2
                cs = slice((oc % 2) * 128, (oc % 2) * 128 + 128)
                W, BW = wblk[wb], Bw4[wb]
                mm(bank(0), [(W[:, 0, kc, cs], yg_[:, kc, :]) for kc in range(8)], [BW[0], Bygg[g % 2]], [PB[0]])
                mm(bank(1), [(W[:, 1, kc, cs], hg_[:, kc, :]) for kc in range(8)], [BW[1], Bhgg[g % 2]], [PB[1]])
                act(tg[0], bank(1), AF.Tanh, [PB[1]], [Btg[0]], scale=0.5)
                stt("dve", Ag[0], tg[0], 1.0, bank(0), ALU.add, ALU.mult, [Btg[0], PB[0]], [BAg[0]])
                mm(bank(2), [(W[:, 2, kc, cs], ys_[:, kc, :]) for kc in range(8)], [BW[2], Bysg[g % 2]], [PB[2]])
                mm(bank(3), [(W[:, 3, kc, cs], hg_[:, kc, :]) for kc in range(8)], [BW[3], Bhgg[g % 2]], [PB[3]])
                act(tg[1], bank(3), AF.Tanh, [PB[3]], [Btg[1]], scale=0.5)
                stt("dve", Ag[1], tg[1], 1.0, bank(2), ALU.add, ALU.mult, [Btg[1], PB[2]], [BAg[1]])
                tt("pool", M[:, oc, :], Ag[0], Ag[1], ALU.add, [BAg[0], BAg[1]], [BM])

        def p4_out_a(t):
            g, sl = divmod(t, 4)
            M, BM = Mg[0], BMg[0]
            xt, Bxt = xts[t % 3], Bxts[t % 3]
            x1, Bx1_ = x1s[t % 3], Bx1s[t % 3]
            if t + 2 < NT:
                dma("sp", xts[(t + 2) % 3], x_d[(t + 2) * 128:(t + 3) * 128, :], [], [Bxts[(t + 2) % 3]])
            lhs = [M[:, kc, sl * 128:(sl + 1) * 128] for kc in range(8)]
            for hf in range(2):
                bk = 4 + hf
                mm(bank(bk), [(lhs[kc], wout[:, kc, hf * 512:(hf + 1) * 512]) for kc in range(8)],
                   [BM, Bwout], [PB[bk]])
                stt("dve", x1[:, hf * 512:(hf + 1) * 512], bank(bk), 0.5, xt[:, hf * 512:(hf + 1) * 512],
                    ALU.mult, ALU.add, [PB[bk], Bxt], [Bx1_])
            norm_a(x1, Bx1_, junk, Bjunk, stats[t % 3], Bstats[t % 3])

        def p4_out_b1(t):
            dma("sp", out_d[t * 128:(t + 1) * 128, :], x1s[t % 3], [Bx1s[t % 3]], [Bx1[t]])
            act(xss[t % 2], x1s[t % 3], AF.Copy, [Bx1s[t % 3], Bstats[t % 3][2]], [Bxss[t % 2]],
                scale=stats[t % 3][2])

        def p4_out_b2(t):
            g, sl = divmod(t, 4)
            xs = xss[t % 2]
            tb_bank = 6 + (t % 2)
            tp = bank16(tb_bank).rearrange("p (c t) -> p c t", t=128)
            fns = [lambda e, c=c: e.transpose(out=tp[:, c, :], in_=xs[:, c * 128:(c + 1) * 128], identity=ident)
                   for c in range(8)]
            fw.op("pe", fns, [Bxss[t % 2], Bconst], [PB[tb_bank]])
            tt("dve", h2st[g % 2][:, :, sl * 128:(sl + 1) * 128], tp,
               gffn.unsqueeze(2).to_broadcast([128, 8, 128]), ALU.mult, [PB[tb_bank], Bconst], [Bh2st[g % 2]])
            if sl == 3:
                dma("sp", h2_d[:, :, g * 512:(g + 1) * 512], h2st[g % 2], [Bh2st[g % 2]], [Bh2[g]])
        p4_load(0)
        for t_ in range(2):
            dma("sp", xts[t_], x_d[t_ * 128:(t_ + 1) * 128, :], [], [Bxts[t_]])
        for g in range(NG):
            if g + 1 < NG:
                p4_load(g + 1)
            p4_merge(g)
            for sl in range(4):
                t = g * 4 + sl
                if t - 2 >= 0:
                    p4_out_b1(t - 2)
                p4_out_a(t)
                if t - 2 >= 0:
                    p4_out_b2(t - 2)
        JB = 2
        NJB = NJ // JB
        wfg_blk = lambda jb: w_fg_d[:, jb * 2048:(jb + 1) * 2048].rearrange("p (a b) -> p a b", b=256)
        wfu_blk = lambda jb: w_fu_d[:, jb * 2048:(jb + 1) * 2048].rearrange("p (a b) -> p a b", b=256)
        wfgb = [None] * NJB
        wfub = [None] * NJB
        Bwg = [Buf() for _ in range(NJB)]
        Bwu = [Buf() for _ in range(NJB)]
        NPRE = 4
        for jb in range(NPRE):
            wfgb[jb] = ar.alloc_at(m0 + jb * 4096, [8, 256], BF16)
            wfub[jb] = ar.alloc_at(m0 + jb * 4096 + 2048, [8, 256], BF16)
            cs = slice(jb * JB * 128, (jb + 1) * JB * 128)
            extra = (Bw4[2] + Bw4[3]) if jb == 0 else []
            dma("pool", wfgb[jb], wfg_blk(jb), [], [Bwg[jb]] + extra)
            dma("pool", wfub[jb], wfu_blk(jb), [], [Bwu[jb]])
        for t_ in (NT - 2, NT - 1):
            p4_out_b1(t_)
            p4_out_b2(t_)
        fw.barrier()
        ar.release(m0)
        print("arena hw phase4", ar.hw)

        m0 = ar.mark()
        ar.off = m0 + NPRE * 4096
        for jb in range(NPRE, NJB):
            wfgb[jb] = ar.alloc([8, 256], BF16)
            wfub[jb] = ar.alloc([8, 256], BF16)
        wfo = ar.alloc([NJ, 1024], BF16)
        Bwo = [Buf() for _ in range(NJB)]
        for jb in range(NPRE, NJB):
            cs = slice(jb * JB * 128, (jb + 1) * JB * 128)
            dep = [Bwu[jb - 3]] if jb - 3 >= NPRE else []
            dma("pool", wfgb[jb], wfg_blk(jb), dep, [Bwg[jb]])
            dma("pool", wfub[jb], wfu_blk(jb), [], [Bwu[jb]])
        for jb in range(NJB):
            dep = [Bwu[NJB - 1]] if jb == 0 else ([Bwo[jb - 3]] if jb >= 3 else [])
            dma("pool", wfo[:, jb * JB:(jb + 1) * JB, :].rearrange("p a b -> p (a b)"),
                w_fo_d[:, jb * JB * 1024:(jb + 1) * JB * 1024], dep, [Bwo[jb]])
        h2g = [ar.alloc([8, 512], BF16) for _ in range(2)]
        Bh2g = [Buf() for _ in range(2)]
        actT = ar.alloc([NJ, 512], BF16)
        Bact = [Buf() for _ in range(NJ)]
        sg = [ar.alloc([512], F32) for _ in range(2)]
        Bsg = [Buf() for _ in range(2)]
        x1r = [ar.alloc([D], F32) for _ in range(3)]
        Bx1r = [Buf() for _ in range(3)]
        fin = [ar.alloc([D], F32) for _ in range(2)]
        Bfin = [Buf() for _ in range(2)]
        Bout = [Buf() for _ in range(NT)]
        for g in range(NG):
            gs = slice(g * 512, (g + 1) * 512)
            h2, Bh2_ = h2g[g % 2], Bh2g[g % 2]
            dma("sp", h2, h2_d[:, :, gs], [Bh2[g]], [Bh2_])
            for j in range(NJ):
                js = slice((j % JB) * 128, (j % JB + 1) * 128)
                p = j % 2
                mm(bank(p), [(wfgb[j // JB][:, kc, js], h2[:, kc, :]) for kc in range(8)], [Bwg[j // JB], Bh2_], [PB[p]])
                mm(bank(2 + p), [(wfub[j // JB][:, kc, js], h2[:, kc, :]) for kc in range(8)], [Bwu[j // JB], Bh2_], [PB[2 + p]])
                act(sg[p], bank(p), AF.Silu, [PB[p]], [Bsg[p]])
                tt("dve", actT[:, j, :], sg[p], bank(2 + p), ALU.mult, [Bsg[p], PB[2 + p]], [Bact[j]])
            for sl in range(4):
                t = g * 4 + sl
                xr, Bxr = x1r[t % 3], Bx1r[t % 3]
                fo, Bfo = fin[t % 2], Bfin[t % 2]
                if t == 0:
                    for t_ in range(2):
                        dma("sp", x1r[t_], out_d[t_ * 128:(t_ + 1) * 128, :], [Bx1[t_]], [Bx1r[t_]])
                if t + 2 < NT:
                    dma("sp", x1r[(t + 2) % 3], out_d[(t + 2) * 128:(t + 3) * 128, :], [Bx1[t + 2]], [Bx1r[(t + 2) % 3]])
                if t >= 1:
                    dma("sp", out_d[(t - 1) * 128:t * 128, :], fin[(t - 1) % 2], [Bfin[(t - 1) % 2], Bx1[t - 1]], [Bout[t - 1]])
                for hf in range(2):
                    bk = 4 + 2 * (t % 2) + hf
                    mm(bank(bk), [(actT[:, j, sl * 128:(sl + 1) * 128], wfo[:, j, hf * 512:(hf + 1) * 512])
                                  for j in range(NJ)], Bact + Bwo, [PB[bk]])
                    tt("dve", fo[:, hf * 512:(hf + 1) * 512], bank(bk), xr[:, hf * 512:(hf + 1) * 512], ALU.add,
                       [PB[bk], Bxr], [Bfo])
        dma("sp", out_d[(NT - 1) * 128:NT * 128, :], fin[(NT - 1) % 2], [Bfin[(NT - 1) % 2], Bx1[NT - 1]], [Bout[NT - 1]])
        print("arena hw phase5", ar.hw)
        fw.barrier()
        fw.emit_all()
    return nc


def _kc_layout(w):
    k, n = w.shape
    c = k // 128
    return np.ascontiguousarray(w.reshape(c, 128, n).transpose(1, 0, 2).reshape(128, c * n))


def _kc_blocks(w, bc):
    k, n = w.shape
    c = k // 128
    nb = n // bc
    return np.ascontiguousarray(w.reshape(c, 128, nb, bc).transpose(1, 2, 0, 3).reshape(128, nb * c * bc))


_NC_CACHE = {}


def kernel(x, norm_mix_g, w_in, gla_gate_up_fwd, gla_gate_bias_fwd, gla_gate_up_bwd,
           gla_gate_bias_bwd, gla_out_norm_g, w_o_gla, swa_q_norm_g, swa_k_norm_g,
           swa_sinks, w_o_swa, w_out, norm_ffn_g, w_ffn_in, w_ffn_out):
    f32 = np.float32
    x = np.asarray(x, f32)
    w_in = np.asarray(w_in, f32)[0]
    shared = {}
    shared["w_lr"] = _kc_layout(w_in[:, 3072:3104])
    shared["w_sq"] = _kc_layout(w_in[:, 3104:4128])
    shared["w_skv"] = _kc_layout(w_in[:, 4128:4640])
    shared["w_ga"] = _kc_blocks(w_in[:, 4640:5664], 256)
    shared["w_gb"] = _kc_blocks(w_in[:, 5664:6688], 256)
    shared["w_os"] = _kc_blocks(np.asarray(w_o_swa, f32)[0], 256)
    shared["w_og"] = _kc_blocks(np.asarray(w_o_gla, f32)[0], 256)
    shared["w_out"] = _kc_layout(np.asarray(w_out, f32)[0])
    for h in range(4):
        blk = np.concatenate([w_in[:, h * 128:(h + 1) * 128], w_in[:, 512 + h * 128:512 + (h + 1) * 128],
                              w_in[:, 1024 + h * 256:1024 + (h + 1) * 256],
                              w_in[:, 2048 + h * 256:2048 + (h + 1) * 256]], axis=1)
        shared[f"w_gh{h}"] = _kc_layout(blk)
    wfi = np.asarray(w_ffn_in, f32)[0]
    shared["w_fg"] = _kc_blocks(wfi[:, :DFF], 256)
    shared["w_fu"] = _kc_blocks(wfi[:, DFF:], 256)
    shared["w_fo"] = _kc_layout(np.asarray(w_ffn_out, f32)[0])
    z16 = np.zeros((16, 512), f32)
    shared["upaug_f"] = np.ascontiguousarray(np.concatenate(
        [np.asarray(gla_gate_up_fwd, f32)[0], z16, np.asarray(gla_gate_bias_fwd, f32)[0][None, :]], axis=0))
    shared["upaug_b"] = np.ascontiguousarray(np.concatenate(
        [z16, np.asarray(gla_gate_up_bwd, f32)[0], np.asarray(gla_gate_bias_bwd, f32)[0][None, :]], axis=0))
    shared["gmix_col"] = np.ascontiguousarray(np.asarray(norm_mix_g, f32)[0].reshape(8, 128).T)
    shared["gffn_col"] = np.ascontiguousarray(np.asarray(norm_ffn_g, f32)[0].reshape(8, 128).T)
    shared["gout"] = np.asarray(gla_out_norm_g, f32).reshape(1, 256)
    shared["gq"] = np.asarray(swa_q_norm_g, f32).reshape(1, 128)
    shared["gk"] = np.asarray(swa_k_norm_g, f32).reshape(1, 128)
    shared["sinks"] = np.asarray(swa_sinks, f32).reshape(1, 8)
    half = 64
    inv_freq = (np.float32(10000.0) ** (-np.arange(half, dtype=f32) / f32(half))).astype(f32)
    ang = (np.arange(S, dtype=f32)[:, None] * inv_freq[None, :]).astype(f32)
    cos = np.cos(ang).astype(f32).reshape(NT, 128, half).transpose(1, 0, 2).reshape(128, NT * half)
    sin = np.sin(ang).astype(f32).reshape(NT, 128, half).transpose(1, 0, 2).reshape(128, NT * half)
    shared["cos_t"] = np.ascontiguousarray(cos)
    shared["sin_t"] = np.ascontiguousarray(sin)

    if "nc" not in _NC_CACHE:
        _NC_CACHE["nc"] = build_program()
    nc = _NC_CACHE["nc"]
    in_maps = []
    for c in range(8):
        m = dict(shared)
        m["x"] = np.ascontiguousarray(x[c])
        in_maps.append(m)
    res = run_bass_kernel_spmd(nc, in_maps, core_ids=list(range(8)))
    return np.stack([np.asarray(r["out"], dtype=f32) for r in res.results], axis=0)
```

```python
import numpy as np
from contextlib import ExitStack
import concourse.bass as bass
import concourse.mybir as mybir
from concourse.bass_utils import run_bass_kernel_spmd

F32 = mybir.dt.float32
BF16 = mybir.dt.bfloat16
AF = mybir.ActivationFunctionType
ALU = mybir.AluOpType
AX = mybir.AxisListType

S = 4096
D = 1024
NT = 32
NG = 8
DFF = 2816
NJ = 22
EPS = 1e-6
SEM_LIMIT = 30000


class Buf:
    __slots__ = ("w", "r")

    def __init__(self):
        self.w = None
        self.r = []


class SemObj:
    __slots__ = ("h", "val", "id")
    _n = 0

    def __init__(self, h):
        self.h = h
        self.val = 0
        SemObj._n += 1
        self.id = SemObj._n


class Eng:
    def __init__(self, fw, name):
        self.fw = fw
        self.name = name
        self.ops = []
        self.sem = None
        self.waited = {}

    def cur_sem(self):
        if self.sem is None or self.sem.val >= SEM_LIMIT:
            self.sem = self.fw.new_sem(self.name)
        return self.sem


class FW:
    def __init__(self, nc, n_dma_sems=48):
        self.nc = nc
        self.engs = {n: Eng(self, n) for n in ("pe", "act", "dve", "pool", "sp")}
        self.dma_pool = []
        self.dma_pools = {}
        self.dma_rrs = {}
        self.n_dma_sems = n_dma_sems
        self.sems = []

    def new_sem(self, name):
        h = self.nc.alloc_semaphore(name=f"s_{name}_{len(self.sems)}")
        s = SemObj(h)
        self.sems.append(s)
        return s

    def _waits_for(self, eng, reads, writes):
        deps = {}

        def add(tok):
            if tok is None:
                return
            s, v = tok
            if deps.get(s.id, (None, -1))[1] < v:
                deps[s.id] = (s, v)
        for b in reads:
            add(b.w)
        for b in writes:
            add(b.w)
            for t in b.r:
                add(t)
        out = []
        for sid, (s, v) in deps.items():
            if eng.name == "pe" and eng.sem is not None and sid == eng.sem.id:
                continue
            if eng.waited.get(sid, -1) >= v:
                continue
            eng.waited[sid] = v
            out.append((s, v))
        return out

    def _commit(self, tok, reads, writes):
        for b in writes:
            b.w = tok
            b.r = []
        for b in reads:
            if b in writes:
                continue
            b.r.append(tok)
            if len(b.r) > 48:
                best = {}
                for s, v in b.r:
                    if best.get(s.id, (None, -1))[1] < v:
                        best[s.id] = (s, v)
                b.r = list(best.values())

    def op(self, engname, fns, reads=(), writes=()):
        eng = self.engs[engname]
        if not isinstance(fns, (list, tuple)):
            fns = [fns]
        waits = self._waits_for(eng, reads, writes)
        sem = eng.cur_sem()
        sem.val += 1
        tok = (sem, sem.val)
        fns = list(fns)

        def emit(e, waits=waits, fns=fns, sem=sem):
            for s, v in waits:
                e.wait_ge(s.h, v)
            ins = None
            for f in fns:
                ins = f(e)
            ins.then_inc(sem.h, 1)
        eng.ops.append(emit)
        self._commit(tok, reads, writes)
        return tok

    def dma(self, qname, fn, reads=(), writes=()):
        eng = self.engs[qname]
        waits = self._waits_for(eng, reads, writes)
        pool = self.dma_pools.setdefault(qname, [])
        npool = self.n_dma_sems if qname == "sp" else 16
        if len(pool) < npool:
            s = self.new_sem("dma" + qname)
            pool.append(s)
            self.dma_pool.append(s)
        else:
            k = self.dma_rrs.get(qname, 0)
            s = pool[k % npool]
            self.dma_rrs[qname] = k + 1
        prev = s.val
        pre = []
        if prev > 0 and eng.waited.get(s.id, -1) < prev:
            pre.append((s, prev))
            eng.waited[s.id] = prev
        s.val += 16
        tok = (s, s.val)

        def emit(e, waits=waits + pre, fn=fn, s=s):
            for ss, v in waits:
                e.wait_ge(ss.h, v)
            fn(e).then_inc(s.h, 16)
        eng.ops.append(emit)
        self._commit(tok, reads, writes)
        return tok

    def barrier(self):
        toks = []
        for n in ("pe", "act", "dve", "pool"):
            s = self.engs[n].sem
            if s is not None and s.val > 0:
                toks.append((s, s.val))
        for s in self.dma_pool:
            if s.val > 0:
                toks.append((s, s.val))
        for n, eng in self.engs.items():
            ws = []
            for s, v in toks:
                if eng.waited.get(s.id, -1) >= v:
                    continue
                if eng.sem is not None and s.id == eng.sem.id:
                    continue
                eng.waited[s.id] = v
                ws.append((s, v))

            def emit(e, ws=ws):
                for s, v in ws:
                    e.wait_ge(s.h, v)
            eng.ops.append(emit)

    def emit_all(self):
        nc = self.nc
        with nc.Block() as block:
            @block.tensor
            def _(e):
                for f in self.engs["pe"].ops:
                    f(e)

            @block.scalar
            def _(e):
                for f in self.engs["act"].ops:
                    f(e)

            @block.vector
            def _(e):
                for f in self.engs["dve"].ops:
                    f(e)

            @block.gpsimd
            def _(e):
                for f in self.engs["pool"].ops:
                    f(e)

            @block.sync
            def _(e):
                for f in self.engs["sp"].ops:
                    f(e)


ARENA_BF = 106400


class Arena:
    def __init__(self, ap):
        self.ap = ap
        self.off = 0
        self.top = ARENA_BF
        self.hw = 0

    def mark(self):
        return self.off

    def alloc_at(self, off, free_shape, dt):
        n = 1
        for s_ in free_shape:
            n *= s_
        units = n * (2 if dt == F32 else 1)
        assert off % 16 == 0 and off + units <= self.top
        v = self.ap[:, off:off + units]
        if dt == F32:
            v = v.bitcast(F32)
        if len(free_shape) == 2:
            v = v.rearrange("p (a b) -> p a b", b=free_shape[1])
        elif len(free_shape) == 3:
            v = v.rearrange("p (a b c) -> p a b c", b=free_shape[1], c=free_shape[2])
        return v

    def alloc_top(self, free_shape, dt):
        n = 1
        for s in free_shape:
            n *= s
        units = n * (2 if dt == F32 else 1)
        self.top = (self.top - units) // 16 * 16
        assert self.top >= self.off, "arena overflow (top)"
        o = self.top
        v = self.ap[:, o:o + units]
        if dt == F32:
            v = v.bitcast(F32)
        if len(free_shape) == 2:
            v = v.rearrange("p (a b) -> p a b", b=free_shape[1])
        return v

    def release(self, m):
        self.off = m

    def alloc(self, free_shape, dt):
        n = 1
        for s in free_shape:
            n *= s
        units = n * (2 if dt == F32 else 1)
        self.off = (self.off + 15) // 16 * 16
        o = self.off
        self.off += units
        assert self.off <= self.top, f"arena overflow {self.off} > {self.top}"
        self.hw = max(self.hw, self.off)
        v = self.ap[:, o:o + units]
        if dt == F32:
            v = v.bitcast(F32)
        if len(free_shape) == 2:
            v = v.rearrange("p (a b) -> p a b", b=free_shape[1])
        elif len(free_shape) == 3:
            v = v.rearrange("p (a b c) -> p a b c", b=free_shape[1], c=free_shape[2])
        return v


def build_program():
    nc = bass.Bass("TRN2", target_bir_lowering=False)

    def din(name, shape, dt=F32):
        return nc.dram_tensor(name, list(shape), dt, kind="ExternalInput").ap()

    x_d = din("x", [S, D])
    w_lr_d = din("w_lr", [128, 8 * 32])
    w_sq_d = din("w_sq", [128, 8 * 1024])
    w_skv_d = din("w_skv", [128, 8 * 512])
    w_gb_d = din("w_gb", [128, 8 * 1024])
    w_ga_d = din("w_ga", [128, 8 * 1024])
    w_os_d = din("w_os", [128, 8 * 1024])
    w_og_d = din("w_og", [128, 8 * 1024])
    w_out_d = din("w_out", [128, 8 * 1024])
    w_gh_d = [din(f"w_gh{h}", [128, 8 * 768]) for h in range(4)]
    w_fg_d = din("w_fg", [128, 8 * DFF])
    w_fu_d = din("w_fu", [128, 8 * DFF])
    w_fo_d = din("w_fo", [128, NJ * 1024])
    upf_d = din("upaug_f", [33, 512])
    upb_d = din("upaug_b", [33, 512])
    gmix_d = din("gmix_col", [128, 8])
    gffn_d = din("gffn_col", [128, 8])
    gout_d = din("gout", [1, 256])
    gq_d = din("gq", [1, 128])
    gk_d = din("gk", [1, 128])
    sinks_d = din("sinks", [1, 8])
    cos_d = din("cos_t", [128, NT * 64])
    sin_d = din("sin_t", [128, NT * 64])
    out_d = nc.dram_tensor("out", [S, D], F32, kind="ExternalOutput").ap()
    hT_d = nc.dram_tensor("hT_scr", [128, 8, S], BF16, kind="Internal").ap()
    ys_d = nc.dram_tensor("ys_scr", [128, 8, S], BF16, kind="Internal").ap()
    yg_d = nc.dram_tensor("yg_scr", [128, 8, S], BF16, kind="Internal").ap()
    h2_d = nc.dram_tensor("h2_scr", [128, 8, S], BF16, kind="Internal").ap()

    fw = FW(nc)
    es = ExitStack()
    with es:
        arena_t = es.enter_context(nc.sbuf_tensor("arena", [128, ARENA_BF], BF16))
        pp = es.enter_context(nc.psum_tensor("pp", [128, 8, 512], F32))
        ar = Arena(arena_t)
        PB = [Buf() for _ in range(8)]

        def bank(b):
            return pp[:, b, :]

        def bank16(b):
            return pp[:, b, :].bitcast(BF16)

        def mm(out_ap, pairs, reads, writes, extra=None):
            fns = []
            n = len(pairs)
            for i, (l, r) in enumerate(pairs):
                fns.append(lambda e, l=l, r=r, i=i, n=n, o=out_ap: e.matmul(
                    o, lhsT=l, rhs=r, start=(i == 0), stop=(i == n - 1)))
            if extra:
                fns = fns + extra
            return fw.op("pe", fns, reads, writes)

        def mmfns(out_ap, pairs):
            fns = []
            n = len(pairs)
            for i, (l, r) in enumerate(pairs):
                fns.append(lambda e, l=l, r=r, i=i, n=n, o=out_ap: e.matmul(
                    o, lhsT=l, rhs=r, start=(i == 0), stop=(i == n - 1)))
            return fns

        def act(out, in_, func, reads, writes, **kw):
            return fw.op("act", lambda e: e.activation(out=out, in_=in_, func=func, **kw), reads, writes)

        def tt(eng, out, in0, in1, op, reads, writes):
            return fw.op(eng, lambda e: e.tensor_tensor(out=out, in0=in0, in1=in1, op=op), reads, writes)

        def ts(eng, out, in0, s1, s2, op0, op1, reads, writes):
            if s2 is None:
                return fw.op(eng, lambda e: e.tensor_scalar(out=out, in0=in0, scalar1=s1, scalar2=None, op0=op0), reads, writes)
            return fw.op(eng, lambda e: e.tensor_scalar(out=out, in0=in0, scalar1=s1, scalar2=s2, op0=op0, op1=op1), reads, writes)

        def stt(eng, out, in0, scalar, in1, op0, op1, reads, writes):
            return fw.op(eng, lambda e: e.scalar_tensor_tensor(out=out, in0=in0, scalar=scalar, in1=in1, op0=op0, op1=op1), reads, writes)

        def copy(eng, out, in_, reads, writes):
            if eng == "act":
                return act(out, in_, AF.Copy, reads, writes)
            return fw.op(eng, lambda e: e.tensor_copy(out=out, in_=in_), reads, writes)

        def dma(q, out, in_, reads, writes):
            return fw.dma(q, lambda e: e.dma_start(out=out, in_=in_), reads, writes)

        def rstd_from_ss(ss, ms, rs, n, width, Bss, Bms, Brs, nhalf):
            ts("dve", ms, ss, 1.0 / n, EPS, ALU.mult, ALU.add, [Bss], [Bms])
            tt("pool", rs, ms, nhalf[:, 0:width], ALU.pow, [Bms, Bconst], [Brs])

        ident = ar.alloc([128], BF16)
        ones_bf = ar.alloc([128], BF16)
        mask2 = ar.alloc([256], BF16)
        mprev = ar.alloc([4, 128], BF16)
        mnext = ar.alloc([4, 128], BF16)
        gmix = ar.alloc([8], F32)
        gffn = ar.alloc([8], F32)
        nhalf = ar.alloc([16], F32)
        rm = ar.alloc([512], F32)
        Bconst = Buf()
        pool_ms = lambda ap, v: fw.op("pool", lambda e: e.memset(ap, v), [], [Bconst])

        def asel(ap, pattern, cm, cmp, fill=0.0):
            fw.op("pool", lambda e: e.affine_select(out=ap, in_=ap, pattern=pattern, compare_op=cmp,
                                                    fill=fill, base=0, channel_multiplier=cm), [Bconst], [Bconst])
        pool_ms(ident, 1.0)
        asel(ident, [[-1, 128]], 1, ALU.is_equal)
        pool_ms(nhalf, -0.5)
        dma("sp", gmix, gmix_d, [], [Bconst])
        dma("sp", gffn, gffn_d, [], [Bconst])

        lrT = ar.alloc([S], BF16)
        BlrT = Buf()
        fw.op("pool", lambda e: e.memset(lrT[32:33, :], 1.0), [], [BlrT])

        wh0_top = ar.alloc_top([8, 768], BF16)
        upf = ar.alloc_top([512], BF16)
        upb = ar.alloc_top([512], BF16)
        gout = ar.alloc_top([256], F32)
        Bc3 = Buf()
        dma("pool", upf[0:33, :], upf_d, [], [Bc3])
        dma("pool", upb[0:33, :], upb_d, [], [Bc3])
        dma("sp", gout, gout_d.partition_broadcast(128), [], [Bc3])
        ts("dve", gout, gout, 0.5, None, ALU.mult, None, [Bc3], [Bc3])
        top_after_wh0 = ar.top
        wsq = ar.alloc_top([8, 1024], BF16)
        wskv = ar.alloc_top([8, 512], BF16)
        Bw2 = Buf()
        wlr_top = ar.alloc_top([8, 32], BF16)
        Bwlr = Buf()
        dma("pool", wlr_top.rearrange("p a b -> p (a b)"), w_lr_d, [], [Bwlr])
        dma("pool", wskv.rearrange("p a b -> p (a b)"), w_skv_d, [], [Bw2])
        for kc in range(0, 8, 2):
            dma("pool", wsq[:, kc:kc + 2, :].rearrange("p a b -> p (a b)"),
                w_sq_d[:, kc * 1024:(kc + 2) * 1024], [], [Bw2])
        pool_ms(ones_bf, 1.0)
        pool_ms(mask2, 1.0)
        asel(mask2[:, 0:128], [[1, 128]], -1, ALU.is_ge)
        asel(mask2[:, 128:256], [[-1, 128]], 1, ALU.is_gt)
        pool_ms(mprev, 0.0)
        asel(mprev, [[0, 4], [-1, 128]], 1, ALU.is_ge, fill=-30000.0)
        pool_ms(mnext, 0.0)
        asel(mnext, [[0, 4], [1, 128]], -1, ALU.is_ge, fill=-30000.0)
        pool_ms(rm, 1.0)
        pool_ms(rm.rearrange("p (c t) -> p c t", t=128)[:, :, 0:1], 0.0)
        cos_t = ar.alloc_top([NT, 64], F32)
        sin_t = ar.alloc_top([NT, 64], F32)
        gqk = ar.alloc_top([2, 128], F32)
        sk8 = ar.alloc_top([8], F32)
        se8 = ar.alloc_top([8], F32)
        sinkrow = ar.alloc_top([8, 128], BF16)
        negshift = ar.alloc_top([1], F32)
        tmpc = ar.alloc_top([128], F32)
        mx = ar.alloc_top([4], F32)
        Bc2 = Buf()
        dma("sp", cos_t.rearrange("p a b -> p (a b)"), cos_d, [], [Bc2])
        dma("sp", sin_t.rearrange("p a b -> p (a b)"), sin_d, [], [Bc2])
        dma("sp", gqk[:, 0, :], gq_d.partition_broadcast(128), [], [Bc2])
        dma("sp", gqk[:, 1, :], gk_d.partition_broadcast(128), [], [Bc2])
        dma("sp", sk8, sinks_d.partition_broadcast(128), [], [Bc2])
        tt("dve", tmpc, gqk[:, 0, :], gqk[:, 0, :], ALU.mult, [Bc2], [Bc2])
        fw.op("dve", lambda e: e.tensor_reduce(out=mx[:, 0:1], in_=tmpc, axis=AX.X, op=ALU.max), [Bc2], [Bc2])
        tt("dve", tmpc, gqk[:, 1, :], gqk[:, 1, :], ALU.mult, [Bc2], [Bc2])
        fw.op("dve", lambda e: e.tensor_reduce(out=mx[:, 1:2], in_=tmpc, axis=AX.X, op=ALU.max), [Bc2], [Bc2])
        tt("dve", mx[:, 2:3], mx[:, 0:1], mx[:, 1:2], ALU.mult, [Bc2], [Bc2])
        tt("pool", mx[:, 3:4], mx[:, 2:3], nhalf[:, 0:1], ALU.pow, [Bc2, Bconst], [Bc2])
        fw.op("dve", lambda e: e.reciprocal(out=mx[:, 2:3], in_=mx[:, 3:4]), [Bc2], [Bc2])
        ts("dve", negshift, mx[:, 2:3], -(128.0 ** 0.5), None, ALU.mult, None, [Bc2], [Bc2])
        act(se8, sk8, AF.Exp, [Bc2], [Bc2], bias=negshift)
        copy("dve", sinkrow[0:1, :, :], se8[0:1, :].unsqueeze(2).to_broadcast([1, 8, 128]), [Bc2], [Bc2])


        def rstd_act(ss, ms, rs, n, Bss, Bms, Brs):
            act(ms, ss, AF.Ln, [Bss], [Bms], scale=1.0 / n, bias=EPS)
            act(rs, ms, AF.Exp, [Bms], [Brs], scale=-0.5)

        def norm_a(xt, Bxt, junk, Bjunk, st, Bst, use_act=False):
            act(junk, xt, AF.Square, [Bxt], [Bjunk, Bst[0]], accum_out=st[0])
            if use_act:
                rstd_act(st[0], st[1], st[2], D, Bst[0], Bst[1], Bst[2])
            else:
                rstd_from_ss(st[0], st[1], st[2], D, 1, Bst[0], Bst[1], Bst[2], nhalf)

        def norm_b(xt, Bxt, st, Bst, xs, Bxs, tb_bank, stage, Bstage, slot, gcol):
            act(xs, xt, AF.Copy, [Bxt, Bst[2]], [Bxs], scale=st[2])
            tp = bank16(tb_bank).rearrange("p (c t) -> p c t", t=128)
            fns = [lambda e, c=c: e.transpose(out=tp[:, c, :], in_=xs[:, c * 128:(c + 1) * 128], identity=ident)
                   for c in range(8)]
            fw.op("pe", fns, [Bxs, Bconst], [PB[tb_bank]])
            tt("dve", stage[:, :, slot * 128:(slot + 1) * 128], tp,
               gcol.unsqueeze(2).to_broadcast([128, 8, 128]), ALU.mult, [PB[tb_bank], Bconst], [Bstage])

        m0 = ar.mark()
        wlr = wlr_top
        NX = 6
        xts = [ar.alloc([D], F32) for _ in range(NX)]
        Bxts = [Buf() for _ in range(NX)]
        xss = [ar.alloc([D], BF16) for _ in range(2)]
        Bxss = [Buf() for _ in range(2)]
        junk = ar.alloc([D], BF16)
        Bjunk = Buf()
        stats = [[ar.alloc([1], F32) for _ in range(3)] for _ in range(NX)]
        Bstats = [[Buf() for _ in range(3)] for _ in range(NX)]
        hst = [ar.alloc([8, 512], BF16) for _ in range(3)]
        Bhst = [Buf() for _ in range(3)]
        BhT = [Buf() for _ in range(NG)]

        def p1_load(t):
            dma("sp", xts[t % NX], x_d[t * 128:(t + 1) * 128, :], [], [Bxts[t % NX]])

        def p1_a(t):
            if t + 3 < NT:
                p1_load(t + 3)
            norm_a(xts[t % NX], Bxts[t % NX], junk, Bjunk, stats[t % NX], Bstats[t % NX], use_act=True)

        def p1_b(t):
            g, sl = divmod(t, 4)
            norm_b(xts[t % NX], Bxts[t % NX], stats[t % NX], Bstats[t % NX], xss[t % 2], Bxss[t % 2],
                   t % 2, hst[g % 3], Bhst[g % 3], sl, gmix)

        def p1_store(g):
            hs, Bhs = hst[g % 3], Bhst[g % 3]
            dma("sp", hT_d[:, :, g * 512:(g + 1) * 512], hs, [Bhs], [BhT[g]])
            mm(bank(2)[0:32, :], [(wlr[:, kc, :], hs[:, kc, :]) for kc in range(8)], [Bwlr, Bhs], [PB[2]])

        def p1_lrcopy(g):
            copy("act", lrT[0:32, g * 512:(g + 1) * 512], bank(2)[0:32, :], [PB[2]], [BlrT])
        for t_ in range(3):
            p1_load(t_)
        for t in range(NT + 8):
            if t < NT:
                p1_a(t)
            if 0 <= t - 2 < NT:
                p1_b(t - 2)
            if t >= 5 and (t - 5) % 4 == 3 and (t - 5) // 4 < NG:
                p1_store((t - 5) // 4)
            if t >= 7 and (t - 7) % 4 == 3 and (t - 7) // 4 < NG:
                p1_lrcopy((t - 7) // 4)
        fw.barrier()
        ar.release(m0)
        print("arena hw phase1", ar.hw, "top", ar.top)

        m0 = ar.mark()
        Bwhs = [Buf() for _ in range(2)]
        dma("pool", wh0_top.rearrange("p a b -> p (a b)"), w_gh_d[0], [], [Bwhs[0]])
        hgs = [ar.alloc([8, 512], BF16) for _ in range(2)]
        Bhgs = [Buf() for _ in range(2)]
        kT_all = ar.alloc([2, 8 * 128], BF16)
        BkT = [Buf() for _ in range(8)]
        v_all = ar.alloc([8, 256], BF16)
        Bv = [Buf() for _ in range(8)]
        qsb = [ar.alloc([10, 128], F32) for _ in range(2)]
        Bqsb = [Buf() for _ in range(2)]
        qT = ar.alloc([8, 8, 128], BF16)
        BqT = [Buf() for _ in range(8)]
        yst = [ar.alloc([8, 512], BF16) for _ in range(2)]
        Byst = [Buf() for _ in range(2)]
        Pt = [[[ar.alloc([512], BF16) for _ in range(3)] for _ in range(2)] for _ in range(2)]
        BPt = [[[Buf() for _ in range(3)] for _ in range(2)] for _ in range(2)]
        sq1 = ar.alloc([10, 128], F32)
        sq = [sq1, sq1]
        qn = [ar.alloc([10, 128], F32) for _ in range(2)]
        rA = [ar.alloc([10, 128], F32) for _ in range(2)]
        rB = [ar.alloc([10, 128], F32) for _ in range(2)]
        qr = [ar.alloc([10, 128], BF16) for _ in range(3)]
        cg = [ar.alloc([2, 128], F32) for _ in range(2)]
        sgn = [ar.alloc([2, 128], F32) for _ in range(2)]
        Bsq1 = Buf()
        Bsq = [Bsq1, Bsq1]
        Bqn = [Buf() for _ in range(2)]
        BrA = [Buf() for _ in range(2)]
        BrB = [Buf() for _ in range(2)]
        Bqr = [Buf() for _ in range(3)]
        Bcg = [Buf() for _ in range(2)]
        st10 = [[ar.alloc([10], F32) for _ in range(3)] for _ in range(2)]
        Bst10 = [[Buf() for _ in range(3)] for _ in range(2)]
        lnden = [ar.alloc([512], F32) for _ in range(2)]
        rden = lnden
        Blnden = [Buf() for _ in range(2)]
        Brden = Blnden
        Bys = [Buf() for _ in range(NG)]
        inv_sqrt_hd = 128.0 ** -0.5

        def swa_proj_a(t):
            g, sl = divmod(t, 4)
            p = t % 2
            hg, Bhg = hgs[g % 2], Bhgs[g % 2]
            if sl == 0:
                dma("sp", hg, hT_d[:, :, g * 512:(g + 1) * 512], [BhT[g]], [Bhg])
            lhs = [hg[:, kc, sl * 128:(sl + 1) * 128] for kc in range(8)]
            qf = qsb[p].rearrange("p a b -> p (a b)")
            mm(bank(0), [(lhs[kc], wsq[:, kc, 0:512]) for kc in range(8)], [Bhg, Bw2], [PB[0]])
            copy("act", qf[:, 0:512], bank(0), [PB[0]], [Bqsb[p]])
            mm(bank(1), [(lhs[kc], wsq[:, kc, 512:1024]) for kc in range(8)], [Bhg, Bw2], [PB[1]])
            copy("act", qf[:, 512:1024], bank(1), [PB[1]], [Bqsb[p]])
            mm(bank(0), [(lhs[kc], wskv[:, kc, :]) for kc in range(8)], [Bhg, Bw2], [PB[0]])
            copy("act", qf[:, 1024:1280], bank(0)[:, 0:256], [PB[0]], [Bqsb[p]])
            copy("act", v_all[:, t % 8, :], bank(0)[:, 256:512], [PB[0]], [Bv[t % 8]])
            act(sq[p], qsb[p], AF.Square, [Bqsb[p]], [Bsq[p]])
            st, Bst = st10[p], Bst10[p]
            fw.op("dve", lambda e: e.tensor_reduce(out=st[0], in_=sq[p], axis=AX.X, op=ALU.add), [Bsq[p]], [Bst[0]])
            c2 = cos_t[:, t, :].unsqueeze(1).unsqueeze(1).to_broadcast([128, 2, 2, 64])
            s2_ = sin_t[:, t, :].unsqueeze(1).unsqueeze(1).to_broadcast([128, 2, 2, 64])
            tt("pool", cg[p].rearrange("p a (h j) -> p a h j", h=2), gqk.rearrange("p a (h j) -> p a h j", h=2), c2,
               ALU.mult, [Bc2], [Bcg[p]])
            tt("pool", sgn[p].rearrange("p a (h j) -> p a h j", h=2), gqk.rearrange("p a (h j) -> p a h j", h=2), s2_,
               ALU.mult, [Bc2], [Bcg[p]])

        def swa_proj_b(t):
            p = t % 2
            rs = st10[p][2]
            Brs = Bst10[p][2]
            rstd_act(st10[p][0], st10[p][1], rs, 128, Bst10[p][0], Bst10[p][1], Brs)
            tt("dve", qn[p], qsb[p], rs.unsqueeze(2).to_broadcast([128, 10, 128]), ALU.mult,
               [Bqsb[p], Brs], [Bqn[p]])
            tt("pool", rA[p][:, 0:8, :], qn[p][:, 0:8, :], cg[p][:, 0:1, :].to_broadcast([128, 8, 128]),
               ALU.mult, [Bqn[p], Bcg[p]], [BrA[p]])
            tt("pool", rA[p][:, 8:10, :], qn[p][:, 8:10, :], cg[p][:, 1:2, :].to_broadcast([128, 2, 128]),
               ALU.mult, [Bqn[p], Bcg[p]], [BrA[p]])
            tt("dve", rB[p][:, 0:8, :], qn[p][:, 0:8, :], sgn[p][:, 0:1, :].to_broadcast([128, 8, 128]),
               ALU.mult, [Bqn[p], Bcg[p]], [BrB[p]])
            tt("dve", rB[p][:, 8:10, :], qn[p][:, 8:10, :], sgn[p][:, 1:2, :].to_broadcast([128, 2, 128]),
               ALU.mult, [Bqn[p], Bcg[p]], [BrB[p]])
            tt("dve", qr[t % 3][:, :, 0:64], rA[p][:, :, 0:64], rB[p][:, :, 64:128], ALU.subtract,
               [BrA[p], BrB[p]], [Bqr[t % 3]])
            tt("pool", qr[t % 3][:, :, 64:128], rA[p][:, :, 64:128], rB[p][:, :, 0:64], ALU.add,
               [BrA[p], BrB[p]], [Bqr[t % 3]])

        def swa_transpose_a(t):
            slot = t % 8
            p = t % 3
            tp = bank16(3).rearrange("p (c t) -> p c t", t=128)
            fns = [lambda e, c=c: e.transpose(out=tp[:, c, :], in_=qr[p][:, 8 + c, :], identity=ident) for c in range(2)]
            fns += [lambda e, c=c: e.transpose(out=tp[:, 2 + c, :], in_=qr[p][:, c, :], identity=ident) for c in range(6)]
            fw.op("pe", fns, [Bqr[p], Bconst], [PB[3]])
            copy("dve", kT_all[:, :, slot * 128:(slot + 1) * 128], tp[:, 0:2, :], [PB[3]], [BkT[slot]])
            copy("dve", qT[:, slot, 0:6, :], tp[:, 2:8, :], [PB[3]], [BqT[slot]])

        def swa_transpose_b(t):
            slot = t % 8
            p = t % 3
            tp = bank16(3).rearrange("p (c t) -> p c t", t=128)
            fns = [lambda e, c=c: e.transpose(out=tp[:, c, :], in_=qr[p][:, 6 + c, :], identity=ident) for c in range(2)]
            fw.op("pe", fns, [Bqr[p], Bconst], [PB[3]])
            copy("dve", qT[:, slot, 6:8, :], tp[:, 0:2, :], [PB[3]], [BqT[slot]])

        sbank = [5, 6]
        scnt = [0]

        def swa_scores(b):
            slot = b % 8
            for kvh in range(2):
                for oi, o in enumerate((b - 1, b, b + 1)):
                    if o < 0 or o >= NT:
                        continue
                    sbk = sbank[scnt[0] % 2]
                    scnt[0] += 1
                    pairs = [(kT_all[:, kvh, (o % 8) * 128:(o % 8 + 1) * 128],
                              qT[:, slot, kvh * 4:(kvh + 1) * 4, :].rearrange("p a b -> p (a b)"))]
                    if o != b:
                        m = mprev if o < b else mnext
                        pairs.append((ident, m.rearrange("p a b -> p (a b)")))
                    mm(bank(sbk), pairs, [BkT[o % 8], BqT[slot], Bconst], [PB[sbk]])
                    P, BP = Pt[b % 2][kvh][oi], BPt[b % 2][kvh][oi]
                    act(P, bank(sbk), AF.Exp, [PB[sbk], Bc2], [BP], bias=negshift, scale=inv_sqrt_hd)

        def swa_pv(b):
            g, sl = divmod(b, 4)
            ys, Bys_ = yst[g % 2], Byst[g % 2]
            for kvh in range(2):
                ob = 7 if kvh == 0 else 2
                valid = [(oi, o) for oi, o in enumerate((b - 1, b, b + 1)) if 0 <= o < NT]
                pairs = [(v_all[:, o % 8, kvh * 128:(kvh + 1) * 128], Pt[b % 2][kvh][oi]) for oi, o in valid]
                rd = [Bv[o % 8] for _, o in valid] + [BPt[b % 2][kvh][oi] for oi, _ in valid]
                mm(bank(ob), pairs, rd, [PB[ob]])
                pairs2 = [(ones_bf, Pt[b % 2][kvh][oi]) for oi, _ in valid]
                pairs2.append((ones_bf[0:1, :], sinkrow[0:1, kvh * 4:(kvh + 1) * 4, :].rearrange("p a b -> p (a b)")))
                mm(bank(4), pairs2, rd + [Bconst, Bc2], [PB[4]])
                act(lnden[kvh], bank(4), AF.Ln, [PB[4]], [Blnden[kvh]])
                act(rden[kvh], lnden[kvh], AF.Exp, [Blnden[kvh]], [Brden[kvh]], scale=-1.0)
                tt("dve", ys[:, kvh * 4:(kvh + 1) * 4, sl * 128:(sl + 1) * 128],
                   bank(ob).rearrange("p (a b) -> p a b", b=128), rden[kvh].rearrange("p (a b) -> p a b", b=128),
                   ALU.mult, [PB[ob], Brden[kvh]], [Bys_])
            if sl == 3:
                dma("sp", ys_d[:, :, g * 512:(g + 1) * 512], ys, [Bys_], [Bys[g]])

        for it in range(NT + 6):
            if it < NT:
                swa_proj_a(it)
            if 0 <= it - 3 < NT:
                swa_transpose_b(it - 3)
            if 0 <= it - 6 < NT:
                swa_pv(it - 6)
            if 0 <= it - 5 < NT:
                swa_scores(it - 5)
            if 0 <= it - 2 < NT:
                swa_transpose_a(it - 2)
            if it < NT:
                swa_proj_b(it)
        fw.barrier()
        ar.release(m0)
        ar.top = top_after_wh0
        print("arena hw phase2", ar.hw)

        m0 = ar.mark()
        whs = [wh0_top, ar.alloc([8, 768], BF16)]
        hgs = [ar.alloc([8, 512], BF16) for _ in range(2)]
        Bhgs = [Buf() for _ in range(2)]
        QD = [ar.alloc([S], BF16) for _ in range(2)]
        KI = [ar.alloc([S], BF16) for _ in range(2)]
        BQK = [[Buf() for _ in range(NG)] for _ in range(2)]
        KIt = [ar.alloc([NT, 128], BF16) for _ in range(2)]
        BKIt = [[Buf() for _ in range(NG)] for _ in range(2)]
        Vh = ar.alloc([NT, 256], BF16)
        BVh = [Buf() for _ in range(NT)]
        S2 = ar.alloc([NT, 256], BF16)
        BS2 = [Buf() for _ in range(NT)]
        SBh = ar.alloc([NT, 256], BF16)
        BSB = [Buf() for _ in range(NT)]
        eT = [ar.alloc([NT], F32) for _ in range(2)]
        BeT = [[Buf() for _ in range(NG)] for _ in range(2)]
        ar.off = (ar.off + 15) // 16 * 16
        tmp_off = ar.off
        Lg_ = [[ar.alloc([512], F32) for _ in range(2)] for _ in range(2)]
        Pp = [[ar.alloc([512], F32) for _ in range(2)] for _ in range(2)]
        Pex = [ar.alloc([512], F32) for _ in range(2)]
        Ep = [[ar.alloc([512], F32) for _ in range(2)] for _ in range(2)]
        En = [[ar.alloc([512], F32) for _ in range(2)] for _ in range(2)]
        BLg = [[Buf() for _ in range(2)] for _ in range(2)]
        BPp = [[Buf() for _ in range(2)] for _ in range(2)]
        BPex = [Buf() for _ in range(2)]
        BEp = [[Buf() for _ in range(2)] for _ in range(2)]
        BEn = [[Buf() for _ in range(2)] for _ in range(2)]
        tnh = [ar.alloc([256], F32) for _ in range(2)]
        Btnh = [Buf() for _ in range(2)]
        Sf = ar.alloc([256], F32)
        Sb = ar.alloc([256], F32)
        Sp = [ar.alloc([256], F32) for _ in range(2)]
        Se = [ar.alloc([256], F32) for _ in range(2)]
        BSf, BSb = Buf(), Buf()
        BSp = [Buf() for _ in range(2)]
        BSe = [Buf() for _ in range(2)]
        Sfb = [ar.alloc([256], BF16) for _ in range(2)]
        BSfb = [Buf() for _ in range(2)]
        Amat = [ar.alloc([256], BF16) for _ in range(2)]
        BA = [Buf() for _ in range(2)]
        on = [ar.alloc([256], F32) for _ in range(2)]
        Bon = [Buf() for _ in range(2)]
        yb = [ar.alloc([256], BF16) for _ in range(2)]
        Byb = [Buf() for _ in range(2)]
        ost = [[ar.alloc([1], F32) for _ in range(3)] for _ in range(3)]
        Bost = [[Buf() for _ in range(3)] for _ in range(3)]
        OBS = [7, 0, 3]
        ojunk = ar.alloc([256], BF16)
        Bojunk = Buf()
        ygst = [ar.alloc([2, 512], BF16) for _ in range(2)]
        Bygst = [Buf() for _ in range(2)]
        Byg = [[Buf() for _ in range(NG)] for _ in range(4)]
        dk_scale = 128.0 ** -0.5
        PB7h = [Buf(), Buf()]

        def load_wh(h):
            dma("pool", whs[h % 2].rearrange("p a b -> p (a b)"), w_gh_d[h], [], [Bwhs[h % 2]])

        for h in range(4):
            wh, Bwh = whs[h % 2], Bwhs[h % 2]
            if h + 1 < 4:
                load_wh(h + 1)

            def g1(g):
                p = g % 2
                for d in range(2):
                    up = upf if d == 0 else upb
                    mm(bank(0), [(up[0:33, h * 128:(h + 1) * 128], lrT[0:33, g * 512:(g + 1) * 512])],
                       [Bc3, BlrT], [PB[0]])
                    L = Lg_[p][d]
                    act(L, bank(0), AF.Exp, [PB[0]], [BLg[p][d]], scale=-1.0)
                    act(L, L, AF.Ln, [BLg[p][d]], [BLg[p][d]], bias=1.0)
                    ts("dve", L, L, -1.0 / 16.0, -0.5, ALU.mult, ALU.max, [BLg[p][d]], [BLg[p][d]])
                    fw.op("dve", lambda e, L=L, P=Pp[p][d]: e.tensor_tensor_scan(
                        out=P, data0=rm, data1=L, initial=0.0, op0=ALU.mult, op1=ALU.add),
                        [BLg[p][d], Bconst], [BPp[p][d]])
                tt("dve", Pex[p], Pp[p][1], Lg_[p][1], ALU.subtract, [BPp[p][1], BLg[p][1]], [BPex[p]])
                for d in range(2):
                    P4 = Pp[p][d].rearrange("p (c t) -> p c t", t=128)
                    act(eT[d][:, g * 4:(g + 1) * 4], P4[:, :, 127], AF.Exp, [BPp[p][d]], [BeT[d][g]])
                    src, Bsrc = (Pp[p][0], BPp[p][0]) if d == 0 else (Pex[p], BPex[p])
                    act(Ep[p][d], src, AF.Exp, [Bsrc], [BEp[p][d]])
                    act(En[p][d], src, AF.Exp, [Bsrc], [BEn[p][d]], scale=-1.0)

            def g2(g):
                p = g % 2
                hg, Bhg = hgs[p], Bhgs[p]
                if not (h > 0 and g < 2):
                    dma("sp", hg, hT_d[:, :, g * 512:(g + 1) * 512], [BhT[g]], [Bhg])
                mm(bank(1 + p), [(wh[:, kc, 0:128], hg[:, kc, :]) for kc in range(8)], [Bwh, Bhg], [PB[1 + p]])
                mm(bank(3 + p), [(wh[:, kc, 128:256], hg[:, kc, :]) for kc in range(8)], [Bwh, Bhg], [PB[3 + p]])

            def g3(g):
                p = g % 2
                gs = slice(g * 512, (g + 1) * 512)
                stt("dve", QD[0][:, gs], bank(1 + p), dk_scale, Ep[p][0], ALU.mult, ALU.mult,
                    [PB[1 + p], BEp[p][0]], [BQK[0][g]])
                stt("dve", QD[1][:, gs], bank(1 + p), dk_scale, En[p][1], ALU.mult, ALU.mult,
                    [PB[1 + p], BEn[p][1]], [BQK[1][g]])
                tt("dve", KI[0][:, gs], bank(3 + p), En[p][0], ALU.mult, [PB[3 + p], BEn[p][0]], [BQK[0][g]])
                tt("dve", KI[1][:, gs], bank(3 + p), Ep[p][1], ALU.mult, [PB[3 + p], BEp[p][1]], [BQK[1][g]])

            def g4(g):
                tp = bank16(5 + (g % 2)).rearrange("p (c t) -> p c t", t=128)
                fns = [lambda e, c=c, d=d: e.transpose(out=tp[:, d * 4 + c, :],
                                                       in_=KI[d][:, (g * 4 + c) * 128:(g * 4 + c + 1) * 128],
                                                       identity=ident) for d in range(2) for c in range(4)]
                fw.op("pe", fns, [BQK[0][g], BQK[1][g], Bconst], [PB[5 + (g % 2)]])
                for d in range(2):
                    copy("act", KIt[d][:, g * 4:(g + 1) * 4, :], tp[:, d * 4:(d + 1) * 4, :], [PB[5 + (g % 2)]],
                         [BKIt[d][g]])
            for it in range(NG + 2):
                if it < NG:
                    g1(it)
                    g2(it)
                if 0 <= it - 1 < NG:
                    g3(it - 1)
                if 0 <= it - 2 < NG:
                    g4(it - 2)

            if h == 3:
                tmp_bufs = [b for row in BLg for b in row] + [b for row in BPp for b in row] + BPex + \
                           [b for row in BEp for b in row] + [b for row in BEn for b in row]
                w4_pref = [ar.alloc_at(tmp_off + i * 8192, [4, 8, 256], BF16) for i in range(2)]
                Bw4_pref = [[Buf() for _ in range(4)] for _ in range(2)]
                for ob4 in range(2):
                    for wi, wd in enumerate((w_og_d, w_ga_d, w_os_d, w_gb_d)):
                        src = wd[:, ob4 * 2048:(ob4 + 1) * 2048].rearrange("p (a b) -> p a b", b=256)
                        extra = tmp_bufs if (ob4 == 0 and wi == 0) else []
                        dma("pool", w4_pref[ob4][:, wi, :, :], src, [], [Bw4_pref[ob4][wi]] + extra)

            fw.op("pool", lambda e: e.memset(Sb, 0.0), [], [BSb])

            def bwd_step(n):
                g = n // 4
                p = n % 2
                act(Sp[p], Sb, AF.Copy, [BSb, BeT[1][g]], [BSp[p]], scale=eT[1][:, n:n + 1])
                act(SBh[:, n, :], Sb, AF.Copy, [BSb, BeT[1][g]], [BSB[n]], scale=eT[1][:, n:n + 1])
                mm(bank(5 + p)[:, 0:256], [(KIt[1][:, n, :], Vh[:, n, :])],
                   [BKIt[1][g], BVh[n]], [PB[5 + p]])
                tt("dve", Sb, bank(5 + p)[:, 0:256], Sp[p], ALU.add, [PB[5 + p], BSp[p]], [BSb])
            pending = []
            for g in range(NG - 1, -1, -1):
                hg, Bhg = hgs[g % 2], Bhgs[g % 2]
                dma("sp", hg, hT_d[:, :, g * 512:(g + 1) * 512], [BhT[g]], [Bhg])
                for sl in range(3, -1, -1):
                    t = g * 4 + sl
                    lhs = [hg[:, kc, sl * 128:(sl + 1) * 128] for kc in range(8)]
                    bk = 1 + (t % 4)
                    fns = mmfns(bank(bk)[:, 0:256], [(lhs[kc], wh[:, kc, 256:512]) for kc in range(8)])
                    fns += mmfns(bank(bk)[:, 256:512], [(lhs[kc], wh[:, kc, 512:768]) for kc in range(8)])
                    fw.op("pe", fns, [Bhg, Bwh], [PB[bk]])
                    copy("act", Vh[:, t, :], bank(bk)[:, 0:256], [PB[bk]], [BVh[t]])
                    act(tnh[t % 2], bank(bk)[:, 256:512], AF.Tanh, [PB[bk]], [Btnh[t % 2]], scale=0.5)
                    stt("dve", S2[:, t, :], tnh[t % 2], 1.0, bank(bk)[:, 256:512], ALU.add, ALU.mult,
                        [Btnh[t % 2], PB[bk]], [BS2[t]])
                    pending.append(t)
                    if len(pending) > 2:
                        bwd_step(pending.pop(0))
            while pending:
                bwd_step(pending.pop(0))

            fw.op("pool", lambda e: e.memset(Sf, 0.0), [], [BSf])
            fw.op("pool", lambda e: e.memset(Sfb[0], 0.0), [], [BSfb[0]])

            def f_main(n):
                g, sl = divmod(n, 4)
                p = n % 2
                q3 = n % 3
                cs = slice(n * 128, (n + 1) * 128)
                A, BA_ = Amat[p], BA[p]
                sb_ = 5 + p
                fns = mmfns(bank(sb_)[:, 0:128], [(KI[0][:, cs], QD[0][:, cs])])
                fns += mmfns(bank(sb_)[:, 128:256], [(KI[1][:, cs], QD[1][:, cs])])
                fw.op("pe", fns, [BQK[0][g], BQK[1][g]], [PB[sb_]])
                tt("dve", A, bank(sb_)[:, 0:256], mask2, ALU.mult, [PB[sb_], Bconst], [BA_])
                bk = 1 + p
                mm(bank(bk)[:, 0:256], [(KIt[0][:, n, :], Vh[:, n, :])], [BKIt[0][g], BVh[n]], [PB[bk]])
                cur, nxt = Sfb[p], Sfb[1 - p]
                Bcur, Bnxt = BSfb[p], BSfb[1 - p]
                ob = OBS[q3]
                if 0 <= n - 4:
                    f_tr_pe(n - 4)
                mm(bank(ob)[:, 0:256],
                   [(A[:, 0:128], Vh[:, n, :]), (A[:, 128:256], Vh[:, n, :]),
                    (QD[0][:, cs], cur), (QD[1][:, cs], SBh[:, n, :])],
                   [BA_, BVh[n], BQK[0][g], BQK[1][g], Bcur, BSB[n]], [PB[ob]])
                act(Se[p], Sf, AF.Copy, [BSf, BeT[0][g]], [BSe[p]], scale=eT[0][:, n:n + 1])
                stt("dve", Sf, bank(bk)[:, 0:256], eT[0][:, n:n + 1], Se[p], ALU.mult, ALU.add,
                    [PB[bk], BeT[0][g], BSe[p]], [BSf])
                copy("dve", nxt, Sf, [BSf], [Bnxt])

            def f_sq(n):
                q3 = n % 3
                ob = OBS[q3]
                act(ojunk, bank(ob)[:, 0:256], AF.Square, [PB[ob]], [Bojunk, Bost[q3][0]], accum_out=ost[q3][0])

            def f_out(n):
                g, sl = divmod(n, 4)
                p = n % 2
                q3 = n % 3
                ob = OBS[q3]
                rstd_act(ost[q3][0], ost[q3][1], ost[q3][2], 256, Bost[q3][0], Bost[q3][1], Bost[q3][2])
                stt("dve", on[p], bank(ob)[:, 0:256], ost[q3][2], gout, ALU.mult, ALU.mult,
                    [PB[ob], Bost[q3][2], Bc3], [Bon[p]])
                tt("pool", yb[p], on[p], S2[:, n, :], ALU.mult, [Bon[p], BS2[n]], [Byb[p]])

            def f_tr_pe(n):
                p = n % 2
                tp = bank16(4).rearrange("p (c t) -> p c t", t=128)
                fns = [lambda e, c=c: e.transpose(out=tp[:, c, :], in_=yb[p][:, c * 128:(c + 1) * 128],
                                                  identity=ident) for c in range(2)]
                fw.op("pe", fns, [Byb[p], Bconst], [PB[4]])

            def f_tr(n):
                g, sl = divmod(n, 4)
                p = n % 2
                tp = bank16(4).rearrange("p (c t) -> p c t", t=128)
                ygs, Bygs = ygst[g % 2], Bygst[g % 2]
                copy("act", ygs[:, :, sl * 128:(sl + 1) * 128], tp[:, 0:2, :], [PB[4]], [Bygs])
                if sl == 3:
                    dma("sp", yg_d[:, 2 * h:2 * h + 2, g * 512:(g + 1) * 512], ygs, [Bygs], [Byg[h][g]])
            for it in range(NT + 4):
                if it < NT:
                    f_main(it)
                elif 0 <= it - 4 < NT:
                    f_tr_pe(it - 4)
                if 0 <= it - 2 < NT:
                    f_out(it - 2)
                if 0 <= it - 4 < NT:
                    f_tr(it - 4)
                if 0 <= it - 1 < NT:
                    f_sq(it - 1)
        fw.barrier()
        ar.release(m0)
        ar.top = ARENA_BF
        print("arena hw phase3", ar.hw)

        m0 = ar.mark()
        wblk = [None] * 4
        Bw4 = [None] * 4
        wblk[0], wblk[1] = w4_pref
        Bw4[0], Bw4[1] = Bw4_pref
        wblk[2] = ar.alloc([4, 8, 256], BF16)
        wblk[3] = ar.alloc([4, 8, 256], BF16)
        Bw4[2] = [Buf() for _ in range(4)]
        Bw4[3] = [Buf() for _ in range(4)]
        wout = ar.alloc([8, 1024], BF16)
        Bwout = Buf()
        for ob4 in range(2, 4):
            for wi, wd in enumerate((w_og_d, w_ga_d, w_os_d, w_gb_d)):
                src = wd[:, ob4 * 2048:(ob4 + 1) * 2048].rearrange("p (a b) -> p a b", b=256)
                dep = Bw4[2] if ob4 == 3 else []
                dma("pool", wblk[ob4][:, wi, :, :], src, dep, [Bw4[ob4][wi]])
        for kc in range(0, 8, 2):
            dma("pool", wout[:, kc:kc + 2, :].rearrange("p a b -> p (a b)"),
                w_out_d[:, kc * 1024:(kc + 2) * 1024], Bw4[3], [Bwout])
        ygg = [ar.alloc([8, 512], BF16) for _ in range(2)]
        ysg = [ar.alloc([8, 512], BF16) for _ in range(2)]
        hgg = [ar.alloc([8, 512], BF16) for _ in range(2)]
        Bygg = [Buf() for _ in range(2)]
        Bysg = [Buf() for _ in range(2)]
        Bhgg = [Buf() for _ in range(2)]
        Mg = [ar.alloc([8, 512], BF16)]
        BMg = [Buf()]
        tg = [ar.alloc([512], F32) for _ in range(2)]
        Btg = [Buf() for _ in range(2)]
        Ag = [ar.alloc([512], F32) for _ in range(2)]
        BAg = [Buf() for _ in range(2)]
        xts = [ar.alloc([D], F32) for _ in range(3)]
        Bxts = [Buf() for _ in range(3)]
        assert ar.off <= tmp_off, (ar.off, tmp_off)
        ar.off = tmp_off + 2 * 8192
        x1s = [ar.alloc([D], F32) for _ in range(3)]
        Bx1s = [Buf() for _ in range(3)]
        xss = [ar.alloc([D], BF16) for _ in range(2)]
        Bxss = [Buf() for _ in range(2)]
        junk = ar.alloc([D], BF16)
        Bjunk = Buf()
        stats = [[ar.alloc([1], F32) for _ in range(3)] for _ in range(3)]
        Bstats = [[Buf() for _ in range(3)] for _ in range(3)]
        h2st = [ar.alloc([8, 512], BF16) for _ in range(2)]
        Bh2st = [Buf() for _ in range(2)]
        Bx1 = [Buf() for _ in range(NT)]
        Bh2 = [Buf() for _ in range(NG)]

        def p4_load(g):
            gs = slice(g * 512, (g + 1) * 512)
            dma("sp", ygg[g % 2], yg_d[:, :, gs], [Byg[h_][g] for h_ in range(4)], [Bygg[g % 2]])
            dma("sp", ysg[g % 2], ys_d[:, :, gs], [Bys[g]], [Bysg[g % 2]])
            dma("sp", hgg[g % 2], hT_d[:, :, gs], [BhT[g]], [Bhgg[g % 2]])

        def p4_merge(g):
            yg_, ys_, hg_ = ygg[g % 2], ysg[g % 2], hgg[g % 2]
            M, BM = Mg[0], BMg[0]
            for oc in range(8):
                wb = oc // 2
                cs = slice((oc % 2) * 128, (oc % 2) * 128 + 128)
                W, BW = wblk[wb], Bw4[wb]
                mm(bank(0), [(W[:, 0, kc, cs], yg_[:, kc, :]) for kc in range(8)], [BW[0], Bygg[g % 2]], [PB[0]])
                mm(bank(1), [(W[:, 1, kc, cs], hg_[:, kc, :]) for kc in range(8)], [BW[1], Bhgg[g % 2]], [PB[1]])
                act(tg[0], bank(1), AF.Tanh, [PB[1]], [Btg[0]], scale=0.5)
                stt("dve", Ag[0], tg[0], 1.0, bank(0), ALU.add, ALU.mult, [Btg[0], PB[0]], [BAg[0]])
                mm(bank(2), [(W[:, 2, kc, cs], ys_[:, kc, :]) for kc in range(8)], [BW[2], Bysg[g % 2]], [PB[2]])
                mm(bank(3), [(W[:, 3, kc, cs], hg_[:, kc, :]) for kc in range(8)], [BW[3], Bhgg[g % 2]], [PB[3]])
                act(tg[1], bank(3), AF.Tanh, [PB[3]], [Btg[1]], scale=0.5)
                stt("dve", Ag[1], tg[1], 1.0, bank(2), ALU.add, ALU.mult, [Btg[1], PB[2]], [BAg[1]])
                tt("pool", M[:, oc, :], Ag[0], Ag[1], ALU.add, [BAg[0], BAg[1]], [BM])

        def p4_out_a(t):
            g, sl = divmod(t, 4)
            M, BM = Mg[0], BMg[0]
            xt, Bxt = xts[t % 3], Bxts[t % 3]
            x1, Bx1_ = x1s[t % 3], Bx1s[t % 3]
            if t + 2 < NT:
                dma("sp", xts[(t + 2) % 3], x_d[(t + 2) * 128:(t + 3) * 128, :], [], [Bxts[(t + 2) % 3]])
            lhs = [M[:, kc, sl * 128:(sl + 1) * 128] for kc in range(8)]
            for hf in range(2):
                bk = 4 + hf
                mm(bank(bk), [(lhs[kc], wout[:, kc, hf * 512:(hf + 1) * 512]) for kc in range(8)],
                   [BM, Bwout], [PB[bk]])
                stt("dve", x1[:, hf * 512:(hf + 1) * 512], bank(bk), 0.5, xt[:, hf * 512:(hf + 1) * 512],
                    ALU.mult, ALU.add, [PB[bk], Bxt], [Bx1_])
            norm_a(x1, Bx1_, junk, Bjunk, stats[t % 3], Bstats[t % 3])

        def p4_out_b1(t):
            dma("sp", out_d[t * 128:(t + 1) * 128, :], x1s[t % 3], [Bx1s[t % 3]], [Bx1[t]])
            act(xss[t % 2], x1s[t % 3], AF.Copy, [Bx1s[t % 3], Bstats[t % 3][2]], [Bxss[t % 2]],
                scale=stats[t % 3][2])

        def p4_out_b2(t):
            g, sl = divmod(t, 4)
            xs = xss[t % 2]
            tb_bank = 6 + (t % 2)
            tp = bank16(tb_bank).rearrange("p (c t) -> p c t", t=128)
            fns = [lambda e, c=c: e.transpose(out=tp[:, c, :], in_=xs[:, c * 128:(c + 1) * 128], identity=ident)
                   for c in range(8)]
            fw.op("pe", fns, [Bxss[t % 2], Bconst], [PB[tb_bank]])
            tt("dve", h2st[g % 2][:, :, sl * 128:(sl + 1) * 128], tp,
               gffn.unsqueeze(2).to_broadcast([128, 8, 128]), ALU.mult, [PB[tb_bank], Bconst], [Bh2st[g % 2]])
            if sl == 3:
                dma("sp", h2_d[:, :, g * 512:(g + 1) * 512], h2st[g % 2], [Bh2st[g % 2]], [Bh2[g]])
        p4_load(0)
        for t_ in range(2):
            dma("sp", xts[t_], x_d[t_ * 128:(t_ + 1) * 128, :], [], [Bxts[t_]])
        for g in range(NG):
            if g + 1 < NG:
                p4_load(g + 1)
            p4_merge(g)
            for sl in range(4):
                t = g * 4 + sl
                if t - 2 >= 0:
                    p4_out_b1(t - 2)
                p4_out_a(t)
                if t - 2 >= 0:
                    p4_out_b2(t - 2)
        JB = 2
        NJB = NJ // JB
        wfg_blk = lambda jb: w_fg_d[:, jb * 2048:(jb + 1) * 2048].rearrange("p (a b) -> p a b", b=256)
        wfu_blk = lambda jb: w_fu_d[:, jb * 2048:(jb + 1) * 2048].rearrange("p (a b) -> p a b", b=256)
        wfgb = [None] * NJB
        wfub = [None] * NJB
        Bwg = [Buf() for _ in range(NJB)]
        Bwu = [Buf() for _ in range(NJB)]
        NPRE = 4
        for jb in range(NPRE):
            wfgb[jb] = ar.alloc_at(m0 + jb * 4096, [8, 256], BF16)
            wfub[jb] = ar.alloc_at(m0 + jb * 4096 + 2048, [8, 256], BF16)
            cs = slice(jb * JB * 128, (jb + 1) * JB * 128)
            extra = (Bw4[2] + Bw4[3]) if jb == 0 else []
            dma("pool", wfgb[jb], wfg_blk(jb), [], [Bwg[jb]] + extra)
            dma("pool", wfub[jb], wfu_blk(jb), [], [Bwu[jb]])
        for t_ in (NT - 2, NT - 1):
            p4_out_b1(t_)
            p4_out_b2(t_)
        fw.barrier()
        ar.release(m0)
        print("arena hw phase4", ar.hw)

        m0 = ar.mark()
        ar.off = m0 + NPRE * 4096
        for jb in range(NPRE, NJB):
            wfgb[jb] = ar.alloc([8, 256], BF16)
            wfub[jb] = ar.alloc([8, 256], BF16)
        wfo = ar.alloc([NJ, 1024], BF16)
        Bwo = [Buf() for _ in range(NJB)]
        for jb in range(NPRE, NJB):
            cs = slice(jb * JB * 128, (jb + 1) * JB * 128)
            dep = [Bwu[jb - 3]] if jb - 3 >= NPRE else []
            dma("pool", wfgb[jb], wfg_blk(jb), dep, [Bwg[jb]])
            dma("pool", wfub[jb], wfu_blk(jb), [], [Bwu[jb]])
        for jb in range(NJB):
            dep = [Bwu[NJB - 1]] if jb == 0 else ([Bwo[jb - 3]] if jb >= 3 else [])
            dma("pool", wfo[:, jb * JB:(jb + 1) * JB, :].rearrange("p a b -> p (a b)"),
                w_fo_d[:, jb * JB * 1024:(jb + 1) * JB * 1024], dep, [Bwo[jb]])
        h2g = [ar.alloc([8, 512], BF16) for _ in range(2)]
        Bh2g = [Buf() for _ in range(2)]
        actT = ar.alloc([NJ, 512], BF16)
        Bact = [Buf() for _ in range(NJ)]
        sg = [ar.alloc([512], F32) for _ in range(2)]
        Bsg = [Buf() for _ in range(2)]
        x1r = [ar.alloc([D], F32) for _ in range(3)]
        Bx1r = [Buf() for _ in range(3)]
        fin = [ar.alloc([D], F32) for _ in range(2)]
        Bfin = [Buf() for _ in range(2)]
        Bout = [Buf() for _ in range(NT)]
        for g in range(NG):
            gs = slice(g * 512, (g + 1) * 512)
            h2, Bh2_ = h2g[g % 2], Bh2g[g % 2]
            dma("sp", h2, h2_d[:, :, gs], [Bh2[g]], [Bh2_])
            for j in range(NJ):
                js = slice((j % JB) * 128, (j % JB + 1) * 128)
                p = j % 2
                mm(bank(p), [(wfgb[j // JB][:, kc, js], h2[:, kc, :]) for kc in range(8)], [Bwg[j // JB], Bh2_], [PB[p]])
                mm(bank(2 + p), [(wfub[j // JB][:, kc, js], h2[:, kc, :]) for kc in range(8)], [Bwu[j // JB], Bh2_], [PB[2 + p]])
                act(sg[p], bank(p), AF.Silu, [PB[p]], [Bsg[p]])
                tt("dve", actT[:, j, :], sg[p], bank(2 + p), ALU.mult, [Bsg[p], PB[2 + p]], [Bact[j]])
            for sl in range(4):
                t = g * 4 + sl
                xr, Bxr = x1r[t % 3], Bx1r[t % 3]
                fo, Bfo = fin[t % 2], Bfin[t % 2]
                if t == 0:
                    for t_ in range(2):
                        dma("sp", x1r[t_], out_d[t_ * 128:(t_ + 1) * 128, :], [Bx1[t_]], [Bx1r[t_]])
                if t + 2 < NT:
                    dma("sp", x1r[(t + 2) % 3], out_d[(t + 2) * 128:(t + 3) * 128, :], [Bx1[t + 2]], [Bx1r[(t + 2) % 3]])
                if t >= 1:
                    dma("sp", out_d[(t - 1) * 128:t * 128, :], fin[(t - 1) % 2], [Bfin[(t - 1) % 2], Bx1[t - 1]], [Bout[t - 1]])
                for hf in range(2):
                    bk = 4 + 2 * (t % 2) + hf
                    mm(bank(bk), [(actT[:, j, sl * 128:(sl + 1) * 128], wfo[:, j, hf * 512:(hf + 1) * 512])
                                  for j in range(NJ)], Bact + Bwo, [PB[bk]])
                    tt("dve", fo[:, hf * 512:(hf + 1) * 512], bank(bk), xr[:, hf * 512:(hf + 1) * 512], ALU.add,
                       [PB[bk], Bxr], [Bfo])
        dma("sp", out_d[(NT - 1) * 128:NT * 128, :], fin[(NT - 1) % 2], [Bfin[(NT - 1) % 2], Bx1[NT - 1]], [Bout[NT - 1]])
        print("arena hw phase5", ar.hw)
        fw.barrier()
        fw.emit_all()
    return nc


def _kc_layout(w):
    k, n = w.shape
    c = k // 128
    return np.ascontiguousarray(w.reshape(c, 128, n).transpose(1, 0, 2).reshape(128, c * n))


def _kc_blocks(w, bc):
    k, n = w.shape
    c = k // 128
    nb = n // bc
    return np.ascontiguousarray(w.reshape(c, 128, nb, bc).transpose(1, 2, 0, 3).reshape(128, nb * c * bc))


_NC_CACHE = {}


def kernel(x, norm_mix_g, w_in, gla_gate_up_fwd, gla_gate_bias_fwd, gla_gate_up_bwd,
           gla_gate_bias_bwd, gla_out_norm_g, w_o_gla, swa_q_norm_g, swa_k_norm_g,
           swa_sinks, w_o_swa, w_out, norm_ffn_g, w_ffn_in, w_ffn_out):
    f32 = np.float32
    x = np.asarray(x, f32)
    w_in = np.asarray(w_in, f32)[0]
    shared = {}
    shared["w_lr"] = _kc_layout(w_in[:, 3072:3104])
    shared["w_sq"] = _kc_layout(w_in[:, 3104:4128])
    shared["w_skv"] = _kc_layout(w_in[:, 4128:4640])
    shared["w_ga"] = _kc_blocks(w_in[:, 4640:5664], 256)
    shared["w_gb"] = _kc_blocks(w_in[:, 5664:6688], 256)
    shared["w_os"] = _kc_blocks(np.asarray(w_o_swa, f32)[0], 256)
    shared["w_og"] = _kc_blocks(np.asarray(w_o_gla, f32)[0], 256)
    shared["w_out"] = _kc_layout(np.asarray(w_out, f32)[0])
    for h in range(4):
        blk = np.concatenate([w_in[:, h * 128:(h + 1) * 128], w_in[:, 512 + h * 128:512 + (h + 1) * 128],
                              w_in[:, 1024 + h * 256:1024 + (h + 1) * 256],
                              w_in[:, 2048 + h * 256:2048 + (h + 1) * 256]], axis=1)
        shared[f"w_gh{h}"] = _kc_layout(blk)
    wfi = np.asarray(w_ffn_in, f32)[0]
    shared["w_fg"] = _kc_blocks(wfi[:, :DFF], 256)
    shared["w_fu"] = _kc_blocks(wfi[:, DFF:], 256)
    shared["w_fo"] = _kc_layout(np.asarray(w_ffn_out, f32)[0])
    z16 = np.zeros((16, 512), f32)
    shared["upaug_f"] = np.ascontiguousarray(np.concatenate(
        [np.asarray(gla_gate_up_fwd, f32)[0], z16, np.asarray(gla_gate_bias_fwd, f32)[0][None, :]], axis=0))
    shared["upaug_b"] = np.ascontiguousarray(np.concatenate(
        [z16, np.asarray(gla_gate_up_bwd, f32)[0], np.asarray(gla_gate_bias_bwd, f32)[0][None, :]], axis=0))
    shared["gmix_col"] = np.ascontiguousarray(np.asarray(norm_mix_g, f32)[0].reshape(8, 128).T)
    shared["gffn_col"] = np.ascontiguousarray(np.asarray(norm_ffn_g, f32)[0].reshape(8, 128).T)
    shared["gout"] = np.asarray(gla_out_norm_g, f32).reshape(1, 256)
    shared["gq"] = np.asarray(swa_q_norm_g, f32).reshape(1, 128)
    shared["gk"] = np.asarray(swa_k_norm_g, f32).reshape(1, 128)
    shared["sinks"] = np.asarray(swa_sinks, f32).reshape(1, 8)
    half = 64
    inv_freq = (np.float32(10000.0) ** (-np.arange(half, dtype=f32) / f32(half))).astype(f32)
    ang = (np.arange(S, dtype=f32)[:, None] * inv_freq[None, :]).astype(f32)
    cos = np.cos(ang).astype(f32).reshape(NT, 128, half).transpose(1, 0, 2).reshape(128, NT * half)
    sin = np.sin(ang).astype(f32).reshape(NT, 128, half).transpose(1, 0, 2).reshape(128, NT * half)
    shared["cos_t"] = np.ascontiguousarray(cos)
    shared["sin_t"] = np.ascontiguousarray(sin)

    if "nc" not in _NC_CACHE:
        _NC_CACHE["nc"] = build_program()
    nc = _NC_CACHE["nc"]
    in_maps = []
    for c in range(8):
        m = dict(shared)
        m["x"] = np.ascontiguousarray(x[c])
        in_maps.append(m)
    res = run_bass_kernel_spmd(nc, in_maps, core_ids=list(range(8)))
    return np.stack([np.asarray(r["out"], dtype=f32) for r in res.results], axis=0)
```

```python
import numpy as np
from contextlib import ExitStack
import concourse.bass as bass
import concourse.mybir as mybir
from concourse.bass_utils import run_bass_kernel_spmd

F32 = mybir.dt.float32
BF16 = mybir.dt.bfloat16
AF = mybir.ActivationFunctionType
ALU = mybir.AluOpType
AX = mybir.AxisListType

S = 4096
D = 1024
NT = 32
NG = 8
DFF = 2816
NJ = 22
EPS = 1e-6
SEM_LIMIT = 30000


class Buf:
    __slots__ = ("w", "r")

    def __init__(self):
        self.w = None
        self.r = []


class SemObj:
    __slots__ = ("h", "val", "id")
    _n = 0

    def __init__(self, h):
        self.h = h
        self.val = 0
        SemObj._n += 1
        self.id = SemObj._n


class Eng:
    def __init__(self, fw, name):
        self.fw = fw
        self.name = name
        self.ops = []
        self.sem = None
        self.waited = {}

    def cur_sem(self):
        if self.sem is None or self.sem.val >= SEM_LIMIT:
            self.sem = self.fw.new_sem(self.name)
        return self.sem


class FW:
    def __init__(self, nc, n_dma_sems=48):
        self.nc = nc
        self.engs = {n: Eng(self, n) for n in ("pe", "act", "dve", "pool", "sp")}
        self.dma_pool = []
        self.dma_pools = {}
        self.dma_rrs = {}
        self.n_dma_sems = n_dma_sems
        self.sems = []

    def new_sem(self, name):
        h = self.nc.alloc_semaphore(name=f"s_{name}_{len(self.sems)}")
        s = SemObj(h)
        self.sems.append(s)
        return s

    def _waits_for(self, eng, reads, writes):
        deps = {}

        def add(tok):
            if tok is None:
                return
            s, v = tok
            if deps.get(s.id, (None, -1))[1] < v:
                deps[s.id] = (s, v)
        for b in reads:
            add(b.w)
        for b in writes:
            add(b.w)
            for t in b.r:
                add(t)
        out = []
        for sid, (s, v) in deps.items():
            if eng.name == "pe" and eng.sem is not None and sid == eng.sem.id:
                continue
            if eng.waited.get(sid, -1) >= v:
                continue
            eng.waited[sid] = v
            out.append((s, v))
        return out

    def _commit(self, tok, reads, writes):
        for b in writes:
            b.w = tok
            b.r = []
        for b in reads:
            if b in writes:
                continue
            b.r.append(tok)
            if len(b.r) > 48:
                best = {}
                for s, v in b.r:
                    if best.get(s.id, (None, -1))[1] < v:
                        best[s.id] = (s, v)
                b.r = list(best.values())

    def op(self, engname, fns, reads=(), writes=()):
        eng = self.engs[engname]
        if not isinstance(fns, (list, tuple)):
            fns = [fns]
        waits = self._waits_for(eng, reads, writes)
        sem = eng.cur_sem()
        sem.val += 1
        tok = (sem, sem.val)
        fns = list(fns)

        def emit(e, waits=waits, fns=fns, sem=sem):
            for s, v in waits:
                e.wait_ge(s.h, v)
            ins = None
            for f in fns:
                ins = f(e)
            ins.then_inc(sem.h, 1)
        eng.ops.append(emit)
        self._commit(tok, reads, writes)
        return tok

    def dma(self, qname, fn, reads=(), writes=()):
        eng = self.engs[qname]
        waits = self._waits_for(eng, reads, writes)
        pool = self.dma_pools.setdefault(qname, [])
        npool = self.n_dma_sems if qname == "sp" else 16
        if len(pool) < npool:
            s = self.new_sem("dma" + qname)
            pool.append(s)
            self.dma_pool.append(s)
        else:
            k = self.dma_rrs.get(qname, 0)
            s = pool[k % npool]
            self.dma_rrs[qname] = k + 1
        prev = s.val
        pre = []
        if prev > 0 and eng.waited.get(s.id, -1) < prev:
            pre.append((s, prev))
            eng.waited[s.id] = prev
        s.val += 16
        tok = (s, s.val)

        def emit(e, waits=waits + pre, fn=fn, s=s):
            for ss, v in waits:
                e.wait_ge(ss.h, v)
            fn(e).then_inc(s.h, 16)
        eng.ops.append(emit)
        self._commit(tok, reads, writes)
        return tok

    def barrier(self):
        toks = []
        for n in ("pe", "act", "dve", "pool"):
            s = self.engs[n].sem
            if s is not None and s.val > 0:
                toks.append((s, s.val))
        for s in self.dma_pool:
            if s.val > 0:
                toks.append((s, s.val))
        for n, eng in self.engs.items():
            ws = []
            for s, v in toks:
                if eng.waited.get(s.id, -1) >= v:
                    continue
                if eng.sem is not None and s.id == eng.sem.id:
                    continue
                eng.waited[s.id] = v
                ws.append((s, v))

            def emit(e, ws=ws):
                for s, v in ws:
                    e.wait_ge(s.h, v)
            eng.ops.append(emit)

    def emit_all(self):
        nc = self.nc
        with nc.Block() as block:
            @block.tensor
            def _(e):
                for f in self.engs["pe"].ops:
                    f(e)

            @block.scalar
            def _(e):
                for f in self.engs["act"].ops:
                    f(e)

            @block.vector
            def _(e):
                for f in self.engs["dve"].ops:
                    f(e)

            @block.gpsimd
            def _(e):
                for f in self.engs["pool"].ops:
                    f(e)

            @block.sync
            def _(e):
                for f in self.engs["sp"].ops:
                    f(e)


ARENA_BF = 106400


class Arena:
    def __init__(self, ap):
        self.ap = ap
        self.off = 0
        self.top = ARENA_BF
        self.hw = 0

    def mark(self):
        return self.off

    def alloc_at(self, off, free_shape, dt):
        n = 1
        for s_ in free_shape:
            n *= s_
        units = n * (2 if dt == F32 else 1)
        assert off % 16 == 0 and off + units <= self.top
        v = self.ap[:, off:off + units]
        if dt == F32:
            v = v.bitcast(F32)
        if len(free_shape) == 2:
            v = v.rearrange("p (a b) -> p a b", b=free_shape[1])
        elif len(free_shape) == 3:
            v = v.rearrange("p (a b c) -> p a b c", b=free_shape[1], c=free_shape[2])
        return v

    def alloc_top(self, free_shape, dt):
        n = 1
        for s in free_shape:
            n *= s
        units = n * (2 if dt == F32 else 1)
        self.top = (self.top - units) // 16 * 16
        assert self.top >= self.off, "arena overflow (top)"
        o = self.top
        v = self.ap[:, o:o + units]
        if dt == F32:
            v = v.bitcast(F32)
        if len(free_shape) == 2:
            v = v.rearrange("p (a b) -> p a b", b=free_shape[1])
        return v

    def release(self, m):
        self.off = m

    def alloc(self, free_shape, dt):
        n = 1
        for s in free_shape:
            n *= s
        units = n * (2 if dt == F32 else 1)
        self.off = (self.off + 15) // 16 * 16
        o = self.off
        self.off += units
        assert self.off <= self.top, f"arena overflow {self.off} > {self.top}"
        self.hw = max(self.hw, self.off)
        v = self.ap[:, o:o + units]
        if dt == F32:
            v = v.bitcast(F32)
        if len(free_shape) == 2:
            v = v.rearrange("p (a b) -> p a b", b=free_shape[1])
        elif len(free_shape) == 3:
            v = v.rearrange("p (a b c) -> p a b c", b=free_shape[1], c=free_shape[2])
        return v


def build_program():
    nc = bass.Bass("TRN2", target_bir_lowering=False)

    def din(name, shape, dt=F32):
        return nc.dram_tensor(name, list(shape), dt, kind="ExternalInput").ap()

    x_d = din("x", [S, D])
    w_lr_d = din("w_lr", [128, 8 * 32])
    w_sq_d = din("w_sq", [128, 8 * 1024])
    w_skv_d = din("w_skv", [128, 8 * 512])
    w_gb_d = din("w_gb", [128, 8 * 1024])
    w_ga_d = din("w_ga", [128, 8 * 1024])
    w_os_d = din("w_os", [128, 8 * 1024])
    w_og_d = din("w_og", [128, 8 * 1024])
    w_out_d = din("w_out", [128, 8 * 1024])
    w_gh_d = [din(f"w_gh{h}", [128, 8 * 768]) for h in range(4)]
    w_fg_d = din("w_fg", [128, 8 * DFF])
    w_fu_d = din("w_fu", [128, 8 * DFF])
    w_fo_d = din("w_fo", [128, NJ * 1024])
    upf_d = din("upaug_f", [33, 512])
    upb_d = din("upaug_b", [33, 512])
    gmix_d = din("gmix_col", [128, 8])
    gffn_d = din("gffn_col", [128, 8])
    gout_d = din("gout", [1, 256])
    gq_d = din("gq", [1, 128])
    gk_d = din("gk", [1, 128])
    sinks_d = din("sinks", [1, 8])
    cos_d = din("cos_t", [128, NT * 64])
    sin_d = din("sin_t", [128, NT * 64])
    out_d = nc.dram_tensor("out", [S, D], F32, kind="ExternalOutput").ap()
    hT_d = nc.dram_tensor("hT_scr", [128, 8, S], BF16, kind="Internal").ap()
    ys_d = nc.dram_tensor("ys_scr", [128, 8, S], BF16, kind="Internal").ap()
    yg_d = nc.dram_tensor("yg_scr", [128, 8, S], BF16, kind="Internal").ap()
    h2_d = nc.dram_tensor("h2_scr", [128, 8, S], BF16, kind="Internal").ap()

    fw = FW(nc)
    es = ExitStack()
    with es:
        arena_t = es.enter_context(nc.sbuf_tensor("arena", [128, ARENA_BF], BF16))
        pp = es.enter_context(nc.psum_tensor("pp", [128, 8, 512], F32))
        ar = Arena(arena_t)
        PB = [Buf() for _ in range(8)]

        def bank(b):
            return pp[:, b, :]

        def bank16(b):
            return pp[:, b, :].bitcast(BF16)

        def mm(out_ap, pairs, reads, writes, extra=None):
            fns = []
            n = len(pairs)
            for i, (l, r) in enumerate(pairs):
                fns.append(lambda e, l=l, r=r, i=i, n=n, o=out_ap: e.matmul(
                    o, lhsT=l, rhs=r, start=(i == 0), stop=(i == n - 1)))
            if extra:
                fns = fns + extra
            return fw.op("pe", fns, reads, writes)

        def mmfns(out_ap, pairs):
            fns = []
            n = len(pairs)
            for i, (l, r) in enumerate(pairs):
                fns.append(lambda e, l=l, r=r, i=i, n=n, o=out_ap: e.matmul(
                    o, lhsT=l, rhs=r, start=(i == 0), stop=(i == n - 1)))
            return fns

        def act(out, in_, func, reads, writes, **kw):
            return fw.op("act", lambda e: e.activation(out=out, in_=in_, func=func, **kw), reads, writes)

        def tt(eng, out, in0, in1, op, reads, writes):
            return fw.op(eng, lambda e: e.tensor_tensor(out=out, in0=in0, in1=in1, op=op), reads, writes)

        def ts(eng, out, in0, s1, s2, op0, op1, reads, writes):
            if s2 is None:
                return fw.op(eng, lambda e: e.tensor_scalar(out=out, in0=in0, scalar1=s1, scalar2=None, op0=op0), reads, writes)
            return fw.op(eng, lambda e: e.tensor_scalar(out=out, in0=in0, scalar1=s1, scalar2=s2, op0=op0, op1=op1), reads, writes)

        def stt(eng, out, in0, scalar, in1, op0, op1, reads, writes):
            return fw.op(eng, lambda e: e.scalar_tensor_tensor(out=out, in0=in0, scalar=scalar, in1=in1, op0=op0, op1=op1), reads, writes)

        def copy(eng, out, in_, reads, writes):
            if eng == "act":
                return act(out, in_, AF.Copy, reads, writes)
            return fw.op(eng, lambda e: e.tensor_copy(out=out, in_=in_), reads, writes)

        def dma(q, out, in_, reads, writes):
            return fw.dma(q, lambda e: e.dma_start(out=out, in_=in_), reads, writes)

        def rstd_from_ss(ss, ms, rs, n, width, Bss, Bms, Brs, nhalf):
            ts("dve", ms, ss, 1.0 / n, EPS, ALU.mult, ALU.add, [Bss], [Bms])
            tt("pool", rs, ms, nhalf[:, 0:width], ALU.pow, [Bms, Bconst], [Brs])

        ident = ar.alloc([128], BF16)
        ones_bf = ar.alloc([128], BF16)
        mask2 = ar.alloc([256], BF16)
        mprev = ar.alloc([4, 128], BF16)
        mnext = ar.alloc([4, 128], BF16)
        gmix = ar.alloc([8], F32)
        gffn = ar.alloc([8], F32)
        nhalf = ar.alloc([16], F32)
        rm = ar.alloc([512], F32)
        Bconst = Buf()
        pool_ms = lambda ap, v: fw.op("pool", lambda e: e.memset(ap, v), [], [Bconst])

        def asel(ap, pattern, cm, cmp, fill=0.0):
            fw.op("pool", lambda e: e.affine_select(out=ap, in_=ap, pattern=pattern, compare_op=cmp,
                                                    fill=fill, base=0, channel_multiplier=cm), [Bconst], [Bconst])
        pool_ms(ident, 1.0)
        asel(ident, [[-1, 128]], 1, ALU.is_equal)
        pool_ms(nhalf, -0.5)
        dma("sp", gmix, gmix_d, [], [Bconst])
        dma("sp", gffn, gffn_d, [], [Bconst])

        lrT = ar.alloc([S], BF16)
        BlrT = Buf()
        fw.op("pool", lambda e: e.memset(lrT[32:33, :], 1.0), [], [BlrT])

        wh0_top = ar.alloc_top([8, 768], BF16)
        upf = ar.alloc_top([512], BF16)
        upb = ar.alloc_top([512], BF16)
        gout = ar.alloc_top([256], F32)
        Bc3 = Buf()
        dma("pool", upf[0:33, :], upf_d, [], [Bc3])
        dma("pool", upb[0:33, :], upb_d, [], [Bc3])
        dma("sp", gout, gout_d.partition_broadcast(128), [], [Bc3])
        ts("dve", gout, gout, 0.5, None, ALU.mult, None, [Bc3], [Bc3])
        top_after_wh0 = ar.top
        wsq = ar.alloc_top([8, 1024], BF16)
        wskv = ar.alloc_top([8, 512], BF16)
        Bw2 = Buf()
        wlr_top = ar.alloc_top([8, 32], BF16)
        Bwlr = Buf()
        dma("pool", wlr_top.rearrange("p a b -> p (a b)"), w_lr_d, [], [Bwlr])
        dma("pool", wskv.rearrange("p a b -> p (a b)"), w_skv_d, [], [Bw2])
        for kc in range(0, 8, 2):
            dma("pool", wsq[:, kc:kc + 2, :].rearrange("p a b -> p (a b)"),
                w_sq_d[:, kc * 1024:(kc + 2) * 1024], [], [Bw2])
        pool_ms(ones_bf, 1.0)
        pool_ms(mask2, 1.0)
        asel(mask2[:, 0:128], [[1, 128]], -1, ALU.is_ge)
        asel(mask2[:, 128:256], [[-1, 128]], 1, ALU.is_gt)
        pool_ms(mprev, 0.0)
        asel(mprev, [[0, 4], [-1, 128]], 1, ALU.is_ge, fill=-30000.0)
        pool_ms(mnext, 0.0)
        asel(mnext, [[0, 4], [1, 128]], -1, ALU.is_ge, fill=-30000.0)
        pool_ms(rm, 1.0)
        pool_ms(rm.rearrange("p (c t) -> p c t", t=128)[:, :, 0:1], 0.0)
        cos_t = ar.alloc_top([NT, 64], F32)
        sin_t = ar.alloc_top([NT, 64], F32)
        gqk = ar.alloc_top([2, 128], F32)
        sk8 = ar.alloc_top([8], F32)
        se8 = ar.alloc_top([8], F32)
        sinkrow = ar.alloc_top([8, 128], BF16)
        negshift = ar.alloc_top([1], F32)
        tmpc = ar.alloc_top([128], F32)
        mx = ar.alloc_top([4], F32)
        Bc2 = Buf()
        dma("sp", cos_t.rearrange("p a b -> p (a b)"), cos_d, [], [Bc2])
        dma("sp", sin_t.rearrange("p a b -> p (a b)"), sin_d, [], [Bc2])
        dma("sp", gqk[:, 0, :], gq_d.partition_broadcast(128), [], [Bc2])
        dma("sp", gqk[:, 1, :], gk_d.partition_broadcast(128), [], [Bc2])
        dma("sp", sk8, sinks_d.partition_broadcast(128), [], [Bc2])
        tt("dve", tmpc, gqk[:, 0, :], gqk[:, 0, :], ALU.mult, [Bc2], [Bc2])
        fw.op("dve", lambda e: e.tensor_reduce(out=mx[:, 0:1], in_=tmpc, axis=AX.X, op=ALU.max), [Bc2], [Bc2])
        tt("dve", tmpc, gqk[:, 1, :], gqk[:, 1, :], ALU.mult, [Bc2], [Bc2])
        fw.op("dve", lambda e: e.tensor_reduce(out=mx[:, 1:2], in_=tmpc, axis=AX.X, op=ALU.max), [Bc2], [Bc2])
        tt("dve", mx[:, 2:3], mx[:, 0:1], mx[:, 1:2], ALU.mult, [Bc2], [Bc2])
        tt("pool", mx[:, 3:4], mx[:, 2:3], nhalf[:, 0:1], ALU.pow, [Bc2, Bconst], [Bc2])
        fw.op("dve", lambda e: e.reciprocal(out=mx[:, 2:3], in_=mx[:, 3:4]), [Bc2], [Bc2])
        ts("dve", negshift, mx[:, 2:3], -(128.0 ** 0.5), None, ALU.mult, None, [Bc2], [Bc2])
        act(se8, sk8, AF.Exp, [Bc2], [Bc2], bias=negshift)
        copy("dve", sinkrow[0:1, :, :], se8[0:1, :].unsqueeze(2).to_broadcast([1, 8, 128]), [Bc2], [Bc2])


        def rstd_act(ss, ms, rs, n, Bss, Bms, Brs):
            act(ms, ss, AF.Ln, [Bss], [Bms], scale=1.0 / n, bias=EPS)
            act(rs, ms, AF.Exp, [Bms], [Brs], scale=-0.5)

        def norm_a(xt, Bxt, junk, Bjunk, st, Bst, use_act=False):
            act(junk, xt, AF.Square, [Bxt], [Bjunk, Bst[0]], accum_out=st[0])
            if use_act:
                rstd_act(st[0], st[1], st[2], D, Bst[0], Bst[1], Bst[2])
            else:
                rstd_from_ss(st[0], st[1], st[2], D, 1, Bst[0], Bst[1], Bst[2], nhalf)

        def norm_b(xt, Bxt, st, Bst, xs, Bxs, tb_bank, stage, Bstage, slot, gcol):
            act(xs, xt, AF.Copy, [Bxt, Bst[2]], [Bxs], scale=st[2])
            tp = bank16(tb_bank).rearrange("p (c t) -> p c t", t=128)
            fns = [lambda e, c=c: e.transpose(out=tp[:, c, :], in_=xs[:, c * 128:(c + 1) * 128], identity=ident)
                   for c in range(8)]
            fw.op("pe", fns, [Bxs, Bconst], [PB[tb_bank]])
            tt("dve", stage[:, :, slot * 128:(slot + 1) * 128], tp,
               gcol.unsqueeze(2).to_broadcast([128, 8, 128]), ALU.mult, [PB[tb_bank], Bconst], [Bstage])

        m0 = ar.mark()
        wlr = wlr_top
        NX = 6
        xts = [ar.alloc([D], F32) for _ in range(NX)]
        Bxts = [Buf() for _ in range(NX)]
        xss = [ar.alloc([D], BF16) for _ in range(2)]
        Bxss = [Buf() for _ in range(2)]
        junk = ar.alloc([D], BF16)
        Bjunk = Buf()
        stats = [[ar.alloc([1], F32) for _ in range(3)] for _ in range(NX)]
        Bstats = [[Buf() for _ in range(3)] for _ in range(NX)]
        hst = [ar.alloc([8, 512], BF16) for _ in range(3)]
        Bhst = [Buf() for _ in range(3)]
        BhT = [Buf() for _ in range(NG)]

        def p1_load(t):
            dma("sp", xts[t % NX], x_d[t * 128:(t + 1) * 128, :], [], [Bxts[t % NX]])

        def p1_a(t):
            if t + 3 < NT:
                p1_load(t + 3)
            norm_a(xts[t % NX], Bxts[t % NX], junk, Bjunk, stats[t % NX], Bstats[t % NX], use_act=True)

        def p1_b(t):
            g, sl = divmod(t, 4)
            norm_b(xts[t % NX], Bxts[t % NX], stats[t % NX], Bstats[t % NX], xss[t % 2], Bxss[t % 2],
                   t % 2, hst[g % 3], Bhst[g % 3], sl, gmix)

        def p1_store(g):
            hs, Bhs = hst[g % 3], Bhst[g % 3]
            dma("sp", hT_d[:, :, g * 512:(g + 1) * 512], hs, [Bhs], [BhT[g]])
            mm(bank(2)[0:32, :], [(wlr[:, kc, :], hs[:, kc, :]) for kc in range(8)], [Bwlr, Bhs], [PB[2]])

        def p1_lrcopy(g):
            copy("act", lrT[0:32, g * 512:(g + 1) * 512], bank(2)[0:32, :], [PB[2]], [BlrT])
        for t_ in range(3):
            p1_load(t_)
        for t in range(NT + 8):
            if t < NT:
                p1_a(t)
            if 0 <= t - 2 < NT:
                p1_b(t - 2)
            if t >= 5 and (t - 5) % 4 == 3 and (t - 5) // 4 < NG:
                p1_store((t - 5) // 4)
            if t >= 7 and (t - 7) % 4 == 3 and (t - 7) // 4 < NG:
                p1_lrcopy((t - 7) // 4)
        fw.barrier()
        ar.release(m0)
        print("arena hw phase1", ar.hw, "top", ar.top)

        m0 = ar.mark()
        Bwhs = [Buf() for _ in range(2)]
        dma("pool", wh0_top.rearrange("p a b -> p (a b)"), w_gh_d[0], [], [Bwhs[0]])
        hgs = [ar.alloc([8, 512], BF16) for _ in range(2)]
        Bhgs = [Buf() for _ in range(2)]
        kT_all = ar.alloc([2, 8 * 128], BF16)
        BkT = [Buf() for _ in range(8)]
        v_all = ar.alloc([8, 256], BF16)
        Bv = [Buf() for _ in range(8)]
        qsb = [ar.alloc([10, 128], F32) for _ in range(2)]
        Bqsb = [Buf() for _ in range(2)]
        qT = ar.alloc([8, 8, 128], BF16)
        BqT = [Buf() for _ in range(8)]
        yst = [ar.alloc([8, 512], BF16) for _ in range(2)]
        Byst = [Buf() for _ in range(2)]
        Pt = [[[ar.alloc([512], BF16) for _ in range(3)] for _ in range(2)] for _ in range(2)]
        BPt = [[[Buf() for _ in range(3)] for _ in range(2)] for _ in range(2)]
        sq1 = ar.alloc([10, 128], F32)
        sq = [sq1, sq1]
        qn = [ar.alloc([10, 128], F32) for _ in range(2)]
        rA = [ar.alloc([10, 128], F32) for _ in range(2)]
        rB = [ar.alloc([10, 128], F32) for _ in range(2)]
        qr = [ar.alloc([10, 128], BF16) for _ in range(3)]
        cg = [ar.alloc([2, 128], F32) for _ in range(2)]
        sgn = [ar.alloc([2, 128], F32) for _ in range(2)]
        Bsq1 = Buf()
        Bsq = [Bsq1, Bsq1]
        Bqn = [Buf() for _ in range(2)]
        BrA = [Buf() for _ in range(2)]
        BrB = [Buf() for _ in range(2)]
        Bqr = [Buf() for _ in range(3)]
        Bcg = [Buf() for _ in range(2)]
        st10 = [[ar.alloc([10], F32) for _ in range(3)] for _ in range(2)]
        Bst10 = [[Buf() for _ in range(3)] for _ in range(2)]
        lnden = [ar.alloc([512], F32) for _ in range(2)]
        rden = lnden
        Blnden = [Buf() for _ in range(2)]
        Brden = Blnden
        Bys = [Buf() for _ in range(NG)]
        inv_sqrt_hd = 128.0 ** -0.5

        def swa_proj_a(t):
            g, sl = divmod(t, 4)
            p = t % 2
            hg, Bhg = hgs[g % 2], Bhgs[g % 2]
            if sl == 0:
                dma("sp", hg, hT_d[:, :, g * 512:(g + 1) * 512], [BhT[g]], [Bhg])
            lhs = [hg[:, kc, sl * 128:(sl + 1) * 128] for kc in range(8)]
            qf = qsb[p].rearrange("p a b -> p (a b)")
            mm(bank(0), [(lhs[kc], wsq[:, kc, 0:512]) for kc in range(8)], [Bhg, Bw2], [PB[0]])
            copy("act", qf[:, 0:512], bank(0), [PB[0]], [Bqsb[p]])
            mm(bank(1), [(lhs[kc], wsq[:, kc, 512:1024]) for kc in range(8)], [Bhg, Bw2], [PB[1]])
            copy("act", qf[:, 512:1024], bank(1), [PB[1]], [Bqsb[p]])
            mm(bank(0), [(lhs[kc], wskv[:, kc, :]) for kc in range(8)], [Bhg, Bw2], [PB[0]])
            copy("act", qf[:, 1024:1280], bank(0)[:, 0:256], [PB[0]], [Bqsb[p]])
            copy("act", v_all[:, t % 8, :], bank(0)[:, 256:512], [PB[0]], [Bv[t % 8]])
            act(sq[p], qsb[p], AF.Square, [Bqsb[p]], [Bsq[p]])
            st, Bst = st10[p], Bst10[p]
            fw.op("dve", lambda e: e.tensor_reduce(out=st[0], in_=sq[p], axis=AX.X, op=ALU.add), [Bsq[p]], [Bst[0]])
            c2 = cos_t[:, t, :].unsqueeze(1).unsqueeze(1).to_broadcast([128, 2, 2, 64])
            s2_ = sin_t[:, t, :].unsqueeze(1).unsqueeze(1).to_broadcast([128, 2, 2, 64])
            tt("pool", cg[p].rearrange("p a (h j) -> p a h j", h=2), gqk.rearrange("p a (h j) -> p a h j", h=2), c2,
               ALU.mult, [Bc2], [Bcg[p]])
            tt("pool", sgn[p].rearrange("p a (h j) -> p a h j", h=2), gqk.rearrange("p a (h j) -> p a h j", h=2), s2_,
               ALU.mult, [Bc2], [Bcg[p]])

        def swa_proj_b(t):
            p = t % 2
            rs = st10[p][2]
            Brs = Bst10[p][2]
            rstd_act(st10[p][0], st10[p][1], rs, 128, Bst10[p][0], Bst10[p][1], Brs)
            tt("dve", qn[p], qsb[p], rs.unsqueeze(2).to_broadcast([128, 10, 128]), ALU.mult,
               [Bqsb[p], Brs], [Bqn[p]])
            tt("pool", rA[p][:, 0:8, :], qn[p][:, 0:8, :], cg[p][:, 0:1, :].to_broadcast([128, 8, 128]),
               ALU.mult, [Bqn[p], Bcg[p]], [BrA[p]])
            tt("pool", rA[p][:, 8:10, :], qn[p][:, 8:10, :], cg[p][:, 1:2, :].to_broadcast([128, 2, 128]),
               ALU.mult, [Bqn[p], Bcg[p]], [BrA[p]])
            tt("dve", rB[p][:, 0:8, :], qn[p][:, 0:8, :], sgn[p][:, 0:1, :].to_broadcast([128, 8, 128]),
               ALU.mult, [Bqn[p], Bcg[p]], [BrB[p]])
            tt("dve", rB[p][:, 8:10, :], qn[p][:, 8:10, :], sgn[p][:, 1:2, :].to_broadcast([128, 2, 128]),
               ALU.mult, [Bqn[p], Bcg[p]], [BrB[p]])
            tt("dve", qr[t % 3][:, :, 0:64], rA[p][:, :, 0:64], rB[p][:, :, 64:128], ALU.subtract,
               [BrA[p], BrB[p]], [Bqr[t % 3]])
            tt("pool", qr[t % 3][:, :, 64:128], rA[p][:, :, 64:128], rB[p][:, :, 0:64], ALU.add,
               [BrA[p], BrB[p]], [Bqr[t % 3]])

        def swa_transpose_a(t):
            slot = t % 8
            p = t % 3
            tp = bank16(3).rearrange("p (c t) -> p c t", t=128)
            fns = [lambda e, c=c: e.transpose(out=tp[:, c, :], in_=qr[p][:, 8 + c, :], identity=ident) for c in range(2)]
            fns += [lambda e, c=c: e.transpose(out=tp[:, 2 + c, :], in_=qr[p][:, c, :], identity=ident) for c in range(6)]
            fw.op("pe", fns, [Bqr[p], Bconst], [PB[3]])
            copy("dve", kT_all[:, :, slot * 128:(slot + 1) * 128], tp[:, 0:2, :], [PB[3]], [BkT[slot]])
            copy("dve", qT[:, slot, 0:6, :], tp[:, 2:8, :], [PB[3]], [BqT[slot]])

        def swa_transpose_b(t):
            slot = t % 8
            p = t % 3
            tp = bank16(3).rearrange("p (c t) -> p c t", t=128)
            fns = [lambda e, c=c: e.transpose(out=tp[:, c, :], in_=qr[p][:, 6 + c, :], identity=ident) for c in range(2)]
            fw.op("pe", fns, [Bqr[p], Bconst], [PB[3]])
            copy("dve", qT[:, slot, 6:8, :], tp[:, 0:2, :], [PB[3]], [BqT[slot]])

        sbank = [5, 6]
        scnt = [0]

        def swa_scores(b):
            slot = b % 8
            for kvh in range(2):
                for oi, o in enumerate((b - 1, b, b + 1)):
                    if o < 0 or o >= NT:
                        continue
                    sbk = sbank[scnt[0] % 2]
                    scnt[0] += 1
                    pairs = [(kT_all[:, kvh, (o % 8) * 128:(o % 8 + 1) * 128],
                              qT[:, slot, kvh * 4:(kvh + 1) * 4, :].rearrange("p a b -> p (a b)"))]
                    if o != b:
                        m = mprev if o < b else mnext
                        pairs.append((ident, m.rearrange("p a b -> p (a b)")))
                    mm(bank(sbk), pairs, [BkT[o % 8], BqT[slot], Bconst], [PB[sbk]])
                    P, BP = Pt[b % 2][kvh][oi], BPt[b % 2][kvh][oi]
                    act(P, bank(sbk), AF.Exp, [PB[sbk], Bc2], [BP], bias=negshift, scale=inv_sqrt_hd)

        def swa_pv(b):
            g, sl = divmod(b, 4)
            ys, Bys_ = yst[g % 2], Byst[g % 2]
            for kvh in range(2):
                ob = 7 if kvh == 0 else 2
                valid = [(oi, o) for oi, o in enumerate((b - 1, b, b + 1)) if 0 <= o < NT]
                pairs = [(v_all[:, o % 8, kvh * 128:(kvh + 1) * 128], Pt[b % 2][kvh][oi]) for oi, o in valid]
                rd = [Bv[o % 8] for _, o in valid] + [BPt[b % 2][kvh][oi] for oi, _ in valid]
                mm(bank(ob), pairs, rd, [PB[ob]])
                pairs2 = [(ones_bf, Pt[b % 2][kvh][oi]) for oi, _ in valid]
                pairs2.append((ones_bf[0:1, :], sinkrow[0:1, kvh * 4:(kvh + 1) * 4, :].rearrange("p a b -> p (a b)")))
                mm(bank(4), pairs2, rd + [Bconst, Bc2], [PB[4]])
                act(lnden[kvh], bank(4), AF.Ln, [PB[4]], [Blnden[kvh]])
                act(rden[kvh], lnden[kvh], AF.Exp, [Blnden[kvh]], [Brden[kvh]], scale=-1.0)
                tt("dve", ys[:, kvh * 4:(kvh + 1) * 4, sl * 128:(sl + 1) * 128],
                   bank(ob).rearrange("p (a b) -> p a b", b=128), rden[kvh].rearrange("p (a b) -> p a b", b=128),
                   ALU.mult, [PB[ob], Brden[kvh]], [Bys_])
            if sl == 3:
                dma("sp", ys_d[:, :, g * 512:(g + 1) * 512], ys, [Bys_], [Bys[g]])

        for it in range(NT + 6):
            if it < NT:
                swa_proj_a(it)
            if 0 <= it - 3 < NT:
                swa_transpose_b(it - 3)
            if 0 <= it - 6 < NT:
                swa_pv(it - 6)
            if 0 <= it - 5 < NT:
                swa_scores(it - 5)
            if 0 <= it - 2 < NT:
                swa_transpose_a(it - 2)
            if it < NT:
                swa_proj_b(it)
        fw.barrier()
        ar.release(m0)
        ar.top = top_after_wh0
        print("arena hw phase2", ar.hw)

        m0 = ar.mark()
        whs = [wh0_top, ar.alloc([8, 768], BF16)]
        hgs = [ar.alloc([8, 512], BF16) for _ in range(2)]
        Bhgs = [Buf() for _ in range(2)]
        QD = [ar.alloc([S], BF16) for _ in range(2)]
        KI = [ar.alloc([S], BF16) for _ in range(2)]
        BQK = [[Buf() for _ in range(NG)] for _ in range(2)]
        KIt = [ar.alloc([NT, 128], BF16) for _ in range(2)]
        BKIt = [[Buf() for _ in range(NG)] for _ in range(2)]
        Vh = ar.alloc([NT, 256], BF16)
        BVh = [Buf() for _ in range(NT)]
        S2 = ar.alloc([NT, 256], BF16)
        BS2 = [Buf() for _ in range(NT)]
        SBh = ar.alloc([NT, 256], BF16)
        BSB = [Buf() for _ in range(NT)]
        eT = [ar.alloc([NT], F32) for _ in range(2)]
        BeT = [[Buf() for _ in range(NG)] for _ in range(2)]
        ar.off = (ar.off + 15) // 16 * 16
        tmp_off = ar.off
        Lg_ = [[ar.alloc([512], F32) for _ in range(2)] for _ in range(2)]
        Pp = [[ar.alloc([512], F32) for _ in range(2)] for _ in range(2)]
        Pex = [ar.alloc([512], F32) for _ in range(2)]
        Ep = [[ar.alloc([512], F32) for _ in range(2)] for _ in range(2)]
        En = [[ar.alloc([512], F32) for _ in range(2)] for _ in range(2)]
        BLg = [[Buf() for _ in range(2)] for _ in range(2)]
        BPp = [[Buf() for _ in range(2)] for _ in range(2)]
        BPex = [Buf() for _ in range(2)]
        BEp = [[Buf() for _ in range(2)] for _ in range(2)]
        BEn = [[Buf() for _ in range(2)] for _ in range(2)]
        tnh = [ar.alloc([256], F32) for _ in range(2)]
        Btnh = [Buf() for _ in range(2)]
        Sf = ar.alloc([256], F32)
        Sb = ar.alloc([256], F32)
        Sp = [ar.alloc([256], F32) for _ in range(2)]
        Se = [ar.alloc([256], F32) for _ in range(2)]
        BSf, BSb = Buf(), Buf()
        BSp = [Buf() for _ in range(2)]
        BSe = [Buf() for _ in range(2)]
        Sfb = [ar.alloc([256], BF16) for _ in range(2)]
        BSfb = [Buf() for _ in range(2)]
        Amat = [ar.alloc([256], BF16) for _ in range(2)]
        BA = [Buf() for _ in range(2)]
        on = [ar.alloc([256], F32) for _ in range(2)]
        Bon = [Buf() for _ in range(2)]
        yb = [ar.alloc([256], BF16) for _ in range(2)]
        Byb = [Buf() for _ in range(2)]
        ost = [[ar.alloc([1], F32) for _ in range(3)] for _ in range(3)]
        Bost = [[Buf() for _ in range(3)] for _ in range(3)]
        OBS = [7, 0, 3]
        ojunk = ar.alloc([256], BF16)
        Bojunk = Buf()
        ygst = [ar.alloc([2, 512], BF16) for _ in range(2)]
        Bygst = [Buf() for _ in range(2)]
        Byg = [[Buf() for _ in range(NG)] for _ in range(4)]
        dk_scale = 128.0 ** -0.5
        PB7h = [Buf(), Buf()]

        def load_wh(h):
            dma("pool", whs[h % 2].rearrange("p a b -> p (a b)"), w_gh_d[h], [], [Bwhs[h % 2]])

        for h in range(4):
            wh, Bwh = whs[h % 2], Bwhs[h % 2]
            if h + 1 < 4:
                load_wh(h + 1)

            def g1(g):
                p = g % 2
                for d in range(2):
                    up = upf if d == 0 else upb
                    mm(bank(0), [(up[0:33, h * 128:(h + 1) * 128], lrT[0:33, g * 512:(g + 1) * 512])],
                       [Bc3, BlrT], [PB[0]])
                    L = Lg_[p][d]
                    act(L, bank(0), AF.Exp, [PB[0]], [BLg[p][d]], scale=-1.0)
                    act(L, L, AF.Ln, [BLg[p][d]], [BLg[p][d]], bias=1.0)
                    ts("dve", L, L, -1.0 / 16.0, -0.5, ALU.mult, ALU.max, [BLg[p][d]], [BLg[p][d]])
                    fw.op("dve", lambda e, L=L, P=Pp[p][d]: e.tensor_tensor_scan(
                        out=P, data0=rm, data1=L, initial=0.0, op0=ALU.mult, op1=ALU.add),
                        [BLg[p][d], Bconst], [BPp[p][d]])
                tt("dve", Pex[p], Pp[p][1], Lg_[p][1], ALU.subtract, [BPp[p][1], BLg[p][1]], [BPex[p]])
                for d in range(2):
                    P4 = Pp[p][d].rearrange("p (c t) -> p c t", t=128)
                    act(eT[d][:, g * 4:(g + 1) * 4], P4[:, :, 127], AF.Exp, [BPp[p][d]], [BeT[d][g]])
                    src, Bsrc = (Pp[p][0], BPp[p][0]) if d == 0 else (Pex[p], BPex[p])
                    act(Ep[p][d], src, AF.Exp, [Bsrc], [BEp[p][d]])
                    act(En[p][d], src, AF.Exp, [Bsrc], [BEn[p][d]], scale=-1.0)

            def g2(g):
                p = g % 2
                hg, Bhg = hgs[p], Bhgs[p]
                if not (h > 0 and g < 2):
                    dma("sp", hg, hT_d[:, :, g * 512:(g + 1) * 512], [BhT[g]], [Bhg])
                mm(bank(1 + p), [(wh[:, kc, 0:128], hg[:, kc, :]) for kc in range(8)], [Bwh, Bhg], [PB[1 + p]])
                mm(bank(3 + p), [(wh[:, kc, 128:256], hg[:, kc, :]) for kc in range(8)], [Bwh, Bhg], [PB[3 + p]])

            def g3(g):
                p = g % 2
                gs = slice(g * 512, (g + 1) * 512)
                stt("dve", QD[0][:, gs], bank(1 + p), dk_scale, Ep[p][0], ALU.mult, ALU.mult,
                    [PB[1 + p], BEp[p][0]], [BQK[0][g]])
                stt("dve", QD[1][:, gs], bank(1 + p), dk_scale, En[p][1], ALU.mult, ALU.mult,
                    [PB[1 + p], BEn[p][1]], [BQK[1][g]])
                tt("dve", KI[0][:, gs], bank(3 + p), En[p][0], ALU.mult, [PB[3 + p], BEn[p][0]], [BQK[0][g]])
                tt("dve", KI[1][:, gs], bank(3 + p), Ep[p][1], ALU.mult, [PB[3 + p], BEp[p][1]], [BQK[1][g]])

            def g4(g):
                tp = bank16(5 + (g % 2)).rearrange("p (c t) -> p c t", t=128)
                fns = [lambda e, c=c, d=d: e.transpose(out=tp[:, d * 4 + c, :],
                                                       in_=KI[d][:, (g * 4 + c) * 128:(g * 4 + c + 1) * 128],
                                                       identity=ident) for d in range(2) for c in range(4)]
                fw.op("pe", fns, [BQK[0][g], BQK[1][g], Bconst], [PB[5 + (g % 2)]])
                for d in range(2):
                    copy("act", KIt[d][:, g * 4:(g + 1) * 4, :], tp[:, d * 4:(d + 1) * 4, :], [PB[5 + (g % 2)]],
                         [BKIt[d][g]])
            for it in range(NG + 2):
                if it < NG:
                    g1(it)
                    g2(it)
                if 0 <= it - 1 < NG:
                    g3(it - 1)
                if 0 <= it - 2 < NG:
                    g4(it - 2)

            if h == 3:
                tmp_bufs = [b for row in BLg for b in row] + [b for row in BPp for b in row] + BPex + \
                           [b for row in BEp for b in row] + [b for row in BEn for b in row]
                w4_pref = [ar.alloc_at(tmp_off + i * 8192, [4, 8, 256], BF16) for i in range(2)]
                Bw4_pref = [[Buf() for _ in range(4)] for _ in range(2)]
                for ob4 in range(2):
                    for wi, wd in enumerate((w_og_d, w_ga_d, w_os_d, w_gb_d)):
                        src = wd[:, ob4 * 2048:(ob4 + 1) * 2048].rearrange("p (a b) -> p a b", b=256)
                        extra = tmp_bufs if (ob4 == 0 and wi == 0) else []
                        dma("pool", w4_pref[ob4][:, wi, :, :], src, [], [Bw4_pref[ob4][wi]] + extra)

            fw.op("pool", lambda e: e.memset(Sb, 0.0), [], [BSb])

            def bwd_step(n):
                g = n // 4
                p = n % 2
                act(Sp[p], Sb, AF.Copy, [BSb, BeT[1][g]], [BSp[p]], scale=eT[1][:, n:n + 1])
                act(SBh[:, n, :], Sb, AF.Copy, [BSb, BeT[1][g]], [BSB[n]], scale=eT[1][:, n:n + 1])
                mm(bank(5 + p)[:, 0:256], [(KIt[1][:, n, :], Vh[:, n, :])],
                   [BKIt[1][g], BVh[n]], [PB[5 + p]])
                tt("dve", Sb, bank(5 + p)[:, 0:256], Sp[p], ALU.add, [PB[5 + p], BSp[p]], [BSb])
            pending = []
            for g in range(NG - 1, -1, -1):
                hg, Bhg = hgs[g % 2], Bhgs[g % 2]
                dma("sp", hg, hT_d[:, :, g * 512:(g + 1) * 512], [BhT[g]], [Bhg])
                for sl in range(3, -1, -1):
                    t = g * 4 + sl
                    lhs = [hg[:, kc, sl * 128:(sl + 1) * 128] for kc in range(8)]
                    bk = 1 + (t % 4)
                    fns = mmfns(bank(bk), [(lhs[kc], wh[:, kc, 256:768]) for kc in range(8)])
                    fw.op("pe", fns, [Bhg, Bwh], [PB[bk]])
                    copy("act", Vh[:, t, :], bank(bk)[:, 0:256], [PB[bk]], [BVh[t]])
                    act(tnh[t % 2], bank(bk)[:, 256:512], AF.Tanh, [PB[bk]], [Btnh[t % 2]], scale=0.5)
                    stt("dve", S2[:, t, :], tnh[t % 2], 1.0, bank(bk)[:, 256:512], ALU.add, ALU.mult,
                        [Btnh[t % 2], PB[bk]], [BS2[t]])
                    pending.append(t)
                    if len(pending) > 2:
                        bwd_step(pending.pop(0))
            while pending:
                bwd_step(pending.pop(0))

            fw.op("pool", lambda e: e.memset(Sf, 0.0), [], [BSf])
            fw.op("pool", lambda e: e.memset(Sfb[0], 0.0), [], [BSfb[0]])

            def f_main(n):
                g, sl = divmod(n, 4)
                p = n % 2
                q3 = n % 3
                cs = slice(n * 128, (n + 1) * 128)
                A, BA_ = Amat[p], BA[p]
                sb_ = 5 + p
                fns = mmfns(bank(sb_)[:, 0:128], [(KI[0][:, cs], QD[0][:, cs])])
                fns += mmfns(bank(sb_)[:, 128:256], [(KI[1][:, cs], QD[1][:, cs])])
                fw.op("pe", fns, [BQK[0][g], BQK[1][g]], [PB[sb_]])
                tt("dve", A, bank(sb_)[:, 0:256], mask2, ALU.mult, [PB[sb_], Bconst], [BA_])
                bk = 1 + p
                mm(bank(bk)[:, 0:256], [(KIt[0][:, n, :], Vh[:, n, :])], [BKIt[0][g], BVh[n]], [PB[bk]])
                cur, nxt = Sfb[p], Sfb[1 - p]
                Bcur, Bnxt = BSfb[p], BSfb[1 - p]
                ob = OBS[q3]
                if 0 <= n - 4:
                    f_tr_pe(n - 4)
                mm(bank(ob)[:, 0:256],
                   [(A[:, 0:128], Vh[:, n, :]), (A[:, 128:256], Vh[:, n, :]),
                    (QD[0][:, cs], cur), (QD[1][:, cs], SBh[:, n, :])],
                   [BA_, BVh[n], BQK[0][g], BQK[1][g], Bcur, BSB[n]], [PB[ob]])
                act(Se[p], Sf, AF.Copy, [BSf, BeT[0][g]], [BSe[p]], scale=eT[0][:, n:n + 1])
                stt("dve", Sf, bank(bk)[:, 0:256], eT[0][:, n:n + 1], Se[p], ALU.mult, ALU.add,
                    [PB[bk], BeT[0][g], BSe[p]], [BSf])
                copy("dve", nxt, Sf, [BSf], [Bnxt])

            def f_sq(n):
                q3 = n % 3
                ob = OBS[q3]
                act(ojunk, bank(ob)[:, 0:256], AF.Square, [PB[ob]], [Bojunk, Bost[q3][0]], accum_out=ost[q3][0])

            def f_out(n):
                g, sl = divmod(n, 4)
                p = n % 2
                q3 = n % 3
                ob = OBS[q3]
                rstd_act(ost[q3][0], ost[q3][1], ost[q3][2], 256, Bost[q3][0], Bost[q3][1], Bost[q3][2])
                stt("dve", on[p], bank(ob)[:, 0:256], ost[q3][2], gout, ALU.mult, ALU.mult,
                    [PB[ob], Bost[q3][2], Bc3], [Bon[p]])
                tt("pool", yb[p], on[p], S2[:, n, :], ALU.mult, [Bon[p], BS2[n]], [Byb[p]])

            def f_tr_pe(n):
                p = n % 2
                tp = bank16(4).rearrange("p (c t) -> p c t", t=128)
                fns = [lambda e, c=c: e.transpose(out=tp[:, c, :], in_=yb[p][:, c * 128:(c + 1) * 128],
                                                  identity=ident) for c in range(2)]
                fw.op("pe", fns, [Byb[p], Bconst], [PB[4]])

            def f_tr(n):
                g, sl = divmod(n, 4)
                p = n % 2
                tp = bank16(4).rearrange("p (c t) -> p c t", t=128)
                ygs, Bygs = ygst[g % 2], Bygst[g % 2]
                copy("act", ygs[:, :, sl * 128:(sl + 1) * 128], tp[:, 0:2, :], [PB[4]], [Bygs])
                if sl == 3:
                    dma("sp", yg_d[:, 2 * h:2 * h + 2, g * 512:(g + 1) * 512], ygs, [Bygs], [Byg[h][g]])
            for it in range(NT + 4):
                if it < NT:
                    f_main(it)
                elif 0 <= it - 4 < NT:
                    f_tr_pe(it - 4)
                if 0 <= it - 2 < NT:
                    f_out(it - 2)
                if 0 <= it - 4 < NT:
                    f_tr(it - 4)
                if 0 <= it - 1 < NT:
                    f_sq(it - 1)
        fw.barrier()
        ar.release(m0)
        ar.top = ARENA_BF
        print("arena hw phase3", ar.hw)

        m0 = ar.mark()
        wblk = [None] * 4
        Bw4 = [None] * 4
        wblk[0], wblk[1] = w4_pref
        Bw4[0], Bw4[1] = Bw4_pref
        wblk[2] = ar.alloc([4, 8, 256], BF16)
        wblk[3] = ar.alloc([4, 8, 256], BF16)
        Bw4[2] = [Buf() for _ in range(4)]
        Bw4[3] = [Buf() for _ in range(4)]
        wout = ar.alloc([8, 1024], BF16)
        Bwout = Buf()
        for ob4 in range(2, 4):
            for wi, wd in enumerate((w_og_d, w_ga_d, w_os_d, w_gb_d)):
                src = wd[:, ob4 * 2048:(ob4 + 1) * 2048].rearrange("p (a b) -> p a b", b=256)
                dep = Bw4[2] if ob4 == 3 else []
                dma("pool", wblk[ob4][:, wi, :, :], src, dep, [Bw4[ob4][wi]])
        for kc in range(0, 8, 2):
            dma("pool", wout[:, kc:kc + 2, :].rearrange("p a b -> p (a b)"),
                w_out_d[:, kc * 1024:(kc + 2) * 1024], Bw4[3], [Bwout])
        ygg = [ar.alloc([8, 512], BF16) for _ in range(2)]
        ysg = [ar.alloc([8, 512], BF16) for _ in range(2)]
        hgg = [ar.alloc([8, 512], BF16) for _ in range(2)]
        Bygg = [Buf() for _ in range(2)]
        Bysg = [Buf() for _ in range(2)]
        Bhgg = [Buf() for _ in range(2)]
        Mg = [ar.alloc([8, 512], BF16)]
        BMg = [Buf()]
        tg = [ar.alloc([512], F32) for _ in range(2)]
        Btg = [Buf() for _ in range(2)]
        Ag = [ar.alloc([512], F32) for _ in range(2)]
        BAg = [Buf() for _ in range(2)]
        xts = [ar.alloc([D], F32) for _ in range(3)]
        Bxts = [Buf() for _ in range(3)]
        assert ar.off <= tmp_off, (ar.off, tmp_off)
        ar.off = tmp_off + 2 * 8192
        x1s = [ar.alloc([D], F32) for _ in range(3)]
        Bx1s = [Buf() for _ in range(3)]
        xss = [ar.alloc([D], BF16) for _ in range(2)]
        Bxss = [Buf() for _ in range(2)]
        junk = ar.alloc([D], BF16)
        Bjunk = Buf()
        stats = [[ar.alloc([1], F32) for _ in range(3)] for _ in range(3)]
        Bstats = [[Buf() for _ in range(3)] for _ in range(3)]
        h2st = [ar.alloc([8, 512], BF16) for _ in range(2)]
        Bh2st = [Buf() for _ in range(2)]
        Bx1 = [Buf() for _ in range(NT)]
        Bh2 = [Buf() for _ in range(NG)]

        def p4_load(g):
            gs = slice(g * 512, (g + 1) * 512)
            dma("sp", ygg[g % 2], yg_d[:, :, gs], [Byg[h_][g] for h_ in range(4)], [Bygg[g % 2]])
            dma("sp", ysg[g % 2], ys_d[:, :, gs], [Bys[g]], [Bysg[g % 2]])
            dma("sp", hgg[g % 2], hT_d[:, :, gs], [BhT[g]], [Bhgg[g % 2]])

        def p4_merge(g):
            yg_, ys_, hg_ = ygg[g % 2], ysg[g % 2], hgg[g % 2]
            M, BM = Mg[0], BMg[0]
            for oc in range(8):
                wb = oc // 2
                cs = slice((oc % 2) * 128, (oc % 2) * 128 + 128)
                W, BW = wblk[wb], Bw4[wb]
                mm(bank(0), [(W[:, 0, kc, cs], yg_[:, kc, :]) for kc in range(8)], [BW[0], Bygg[g % 2]], [PB[0]])
                mm(bank(1), [(W[:, 1, kc, cs], hg_[:, kc, :]) for kc in range(8)], [BW[1], Bhgg[g % 2]], [PB[1]])
                act(tg[0], bank(1), AF.Tanh, [PB[1]], [Btg[0]], scale=0.5)
                stt("dve", Ag[0], tg[0], 1.0, bank(0), ALU.add, ALU.mult, [Btg[0], PB[0]], [BAg[0]])
                mm(bank(2), [(W[:, 2, kc, cs], ys_[:, kc, :]) for kc in range(8)], [BW[2], Bysg[g % 2]], [PB[2]])
                mm(bank(3), [(W[:, 3, kc, cs], hg_[:, kc, :]) for kc in range(8)], [BW[3], Bhgg[g % 2]], [PB[3]])
                act(tg[1], bank(3), AF.Tanh, [PB[3]], [Btg[1]], scale=0.5)
                stt("dve", Ag[1], tg[1], 1.0, bank(2), ALU.add, ALU.mult, [Btg[1], PB[2]], [BAg[1]])
                tt("pool", M[:, oc, :], Ag[0], Ag[1], ALU.add, [BAg[0], BAg[1]], [BM])

        def p4_out_a(t):
            g, sl = divmod(t, 4)
            M, BM = Mg[0], BMg[0]
            xt, Bxt = xts[t % 3], Bxts[t % 3]
            x1, Bx1_ = x1s[t % 3], Bx1s[t % 3]
            if t + 2 < NT:
                dma("sp", xts[(t + 2) % 3], x_d[(t + 2) * 128:(t + 3) * 128, :], [], [Bxts[(t + 2) % 3]])
            lhs = [M[:, kc, sl * 128:(sl + 1) * 128] for kc in range(8)]
            for hf in range(2):
                bk = 4 + hf
                mm(bank(bk), [(lhs[kc], wout[:, kc, hf * 512:(hf + 1) * 512]) for kc in range(8)],
                   [BM, Bwout], [PB[bk]])
                stt("dve", x1[:, hf * 512:(hf + 1) * 512], bank(bk), 0.5, xt[:, hf * 512:(hf + 1) * 512],
                    ALU.mult, ALU.add, [PB[bk], Bxt], [Bx1_])
            norm_a(x1, Bx1_, junk, Bjunk, stats[t % 3], Bstats[t % 3])

        def p4_out_b1(t):
            dma("sp", out_d[t * 128:(t + 1) * 128, :], x1s[t % 3], [Bx1s[t % 3]], [Bx1[t]])
            act(xss[t % 2], x1s[t % 3], AF.Copy, [Bx1s[t % 3], Bstats[t % 3][2]], [Bxss[t % 2]],
                scale=stats[t % 3][2])

        def p4_out_b2(t):
            g, sl = divmod(t, 4)
            xs = xss[t % 2]
            tb_bank = 6 + (t % 2)
            tp = bank16(tb_bank).rearrange("p (c t) -> p c t", t=128)
            fns = [lambda e, c=c: e.transpose(out=tp[:, c, :], in_=xs[:, c * 128:(c + 1) * 128], identity=ident)
                   for c in range(8)]
            fw.op("pe", fns, [Bxss[t % 2], Bconst], [PB[tb_bank]])
            tt("dve", h2st[g % 2][:, :, sl * 128:(sl + 1) * 128], tp,
               gffn.unsqueeze(2).to_broadcast([128, 8, 128]), ALU.mult, [PB[tb_bank], Bconst], [Bh2st[g % 2]])
            if sl == 3:
                dma("sp", h2_d[:, :, g * 512:(g + 1) * 512], h2st[g % 2], [Bh2st[g % 2]], [Bh2[g]])
        p4_load(0)
        for t_ in range(2):
            dma("sp", xts[t_], x_d[t_ * 128:(t_ + 1) * 128, :], [], [Bxts[t_]])
        for g in range(NG):
            if g + 1 < NG:
                p4_load(g + 1)
            p4_merge(g)
            for sl in range(4):
                t = g * 4 + sl
                if t - 2 >= 0:
                    p4_out_b1(t - 2)
                p4_out_a(t)
                if t - 2 >= 0:
                    p4_out_b2(t - 2)
        JB = 2
        NJB = NJ // JB
        wfg_blk = lambda jb: w_fg_d[:, jb * 2048:(jb + 1) * 2048].rearrange("p (a b) -> p a b", b=256)
        wfu_blk = lambda jb: w_fu_d[:, jb * 2048:(jb + 1) * 2048].rearrange("p (a b) -> p a b", b=256)
        wfgb = [None] * NJB
        wfub = [None] * NJB
        Bwg = [Buf() for _ in range(NJB)]
        Bwu = [Buf() for _ in range(NJB)]
        NPRE = 4
        for jb in range(NPRE):
            wfgb[jb] = ar.alloc_at(m0 + jb * 4096, [8, 256], BF16)
            wfub[jb] = ar.alloc_at(m0 + jb * 4096 + 2048, [8, 256], BF16)
            cs = slice(jb * JB * 128, (jb + 1) * JB * 128)
            extra = (Bw4[2] + Bw4[3]) if jb == 0 else []
            dma("pool", wfgb[jb], wfg_blk(jb), [], [Bwg[jb]] + extra)
            dma("pool", wfub[jb], wfu_blk(jb), [], [Bwu[jb]])
        for t_ in (NT - 2, NT - 1):
            p4_out_b1(t_)
            p4_out_b2(t_)
        fw.barrier()
        ar.release(m0)
        print("arena hw phase4", ar.hw)

        m0 = ar.mark()
        ar.off = m0 + NPRE * 4096
        for jb in range(NPRE, NJB):
            wfgb[jb] = ar.alloc([8, 256], BF16)
            wfub[jb] = ar.alloc([8, 256], BF16)
        wfo = ar.alloc([NJ, 1024], BF16)
        Bwo = [Buf() for _ in range(NJB)]
        for jb in range(NPRE, NJB):
            cs = slice(jb * JB * 128, (jb + 1) * JB * 128)
            dep = [Bwu[jb - 3]] if jb - 3 >= NPRE else []
            dma("pool", wfgb[jb], wfg_blk(jb), dep, [Bwg[jb]])
            dma("pool", wfub[jb], wfu_blk(jb), [], [Bwu[jb]])
        for jb in range(NJB):
            dep = [Bwu[NJB - 1]] if jb == 0 else ([Bwo[jb - 3]] if jb >= 3 else [])
            dma("pool", wfo[:, jb * JB:(jb + 1) * JB, :].rearrange("p a b -> p (a b)"),
                w_fo_d[:, jb * JB * 1024:(jb + 1) * JB * 1024], dep, [Bwo[jb]])
        h2g = [ar.alloc([8, 512], BF16) for _ in range(2)]
        Bh2g = [Buf() for _ in range(2)]
        actT = ar.alloc([NJ, 512], BF16)
        Bact = [Buf() for _ in range(NJ)]
        sg = [ar.alloc([512], F32) for _ in range(2)]
        Bsg = [Buf() for _ in range(2)]
        x1r = [ar.alloc([D], F32) for _ in range(3)]
        Bx1r = [Buf() for _ in range(3)]
        fin = [ar.alloc([D], F32) for _ in range(2)]
        Bfin = [Buf() for _ in range(2)]
        Bout = [Buf() for _ in range(NT)]
        for g in range(NG):
            gs = slice(g * 512, (g + 1) * 512)
            h2, Bh2_ = h2g[g % 2], Bh2g[g % 2]
            dma("sp", h2, h2_d[:, :, gs], [Bh2[g]], [Bh2_])
            for j in range(NJ):
                js = slice((j % JB) * 128, (j % JB + 1) * 128)
                p = j % 2
                mm(bank(p), [(wfgb[j // JB][:, kc, js], h2[:, kc, :]) for kc in range(8)], [Bwg[j // JB], Bh2_], [PB[p]])
                mm(bank(2 + p), [(wfub[j // JB][:, kc, js], h2[:, kc, :]) for kc in range(8)], [Bwu[j // JB], Bh2_], [PB[2 + p]])
                act(sg[p], bank(p), AF.Silu, [PB[p]], [Bsg[p]])
                tt("dve", actT[:, j, :], sg[p], bank(2 + p), ALU.mult, [Bsg[p], PB[2 + p]], [Bact[j]])
            for sl in range(4):
                t = g * 4 + sl
                xr, Bxr = x1r[t % 3], Bx1r[t % 3]
                fo, Bfo = fin[t % 2], Bfin[t % 2]
                if t == 0:
                    for t_ in range(2):
                        dma("sp", x1r[t_], out_d[t_ * 128:(t_ + 1) * 128, :], [Bx1[t_]], [Bx1r[t_]])
                if t + 2 < NT:
                    dma("sp", x1r[(t + 2) % 3], out_d[(t + 2) * 128:(t + 3) * 128, :], [Bx1[t + 2]], [Bx1r[(t + 2) % 3]])
                if t >= 1:
                    dma("sp", out_d[(t - 1) * 128:t * 128, :], fin[(t - 1) % 2], [Bfin[(t - 1) % 2], Bx1[t - 1]], [Bout[t - 1]])
                for hf in range(2):
                    bk = 4 + 2 * (t % 2) + hf
                    mm(bank(bk), [(actT[:, j, sl * 128:(sl + 1) * 128], wfo[:, j, hf * 512:(hf + 1) * 512])
                                  for j in range(NJ)], Bact + Bwo, [PB[bk]])
                    tt("dve", fo[:, hf * 512:(hf + 1) * 512], bank(bk), xr[:, hf * 512:(hf + 1) * 512], ALU.add,
                       [PB[bk], Bxr], [Bfo])
        dma("sp", out_d[(NT - 1) * 128:NT * 128, :], fin[(NT - 1) % 2], [Bfin[(NT - 1) % 2], Bx1[NT - 1]], [Bout[NT - 1]])
        print("arena hw phase5", ar.hw)
        fw.barrier()
        fw.emit_all()
    return nc


def _kc_layout(w):
    k, n = w.shape
    c = k // 128
    return np.ascontiguousarray(w.reshape(c, 128, n).transpose(1, 0, 2).reshape(128, c * n))


def _kc_blocks(w, bc):
    k, n = w.shape
    c = k // 128
    nb = n // bc
    return np.ascontiguousarray(w.reshape(c, 128, nb, bc).transpose(1, 2, 0, 3).reshape(128, nb * c * bc))


_NC_CACHE = {}


def kernel(x, norm_mix_g, w_in, gla_gate_up_fwd, gla_gate_bias_fwd, gla_gate_up_bwd,
           gla_gate_bias_bwd, gla_out_norm_g, w_o_gla, swa_q_norm_g, swa_k_norm_g,
           swa_sinks, w_o_swa, w_out, norm_ffn_g, w_ffn_in, w_ffn_out):
    f32 = np.float32
    x = np.asarray(x, f32)
    w_in = np.asarray(w_in, f32)[0]
    shared = {}
    shared["w_lr"] = _kc_layout(w_in[:, 3072:3104])
    shared["w_sq"] = _kc_layout(w_in[:, 3104:4128])
    shared["w_skv"] = _kc_layout(w_in[:, 4128:4640])
    shared["w_ga"] = _kc_blocks(w_in[:, 4640:5664], 256)
    shared["w_gb"] = _kc_blocks(w_in[:, 5664:6688], 256)
    shared["w_os"] = _kc_blocks(np.asarray(w_o_swa, f32)[0], 256)
    shared["w_og"] = _kc_blocks(np.asarray(w_o_gla, f32)[0], 256)
    shared["w_out"] = _kc_layout(np.asarray(w_out, f32)[0])
    for h in range(4):
        blk = np.concatenate([w_in[:, h * 128:(h + 1) * 128], w_in[:, 512 + h * 128:512 + (h + 1) * 128],
                              w_in[:, 1024 + h * 256:1024 + (h + 1) * 256],
                              w_in[:, 2048 + h * 256:2048 + (h + 1) * 256]], axis=1)
        shared[f"w_gh{h}"] = _kc_layout(blk)
    wfi = np.asarray(w_ffn_in, f32)[0]
    shared["w_fg"] = _kc_blocks(wfi[:, :DFF], 256)
    shared["w_fu"] = _kc_blocks(wfi[:, DFF:], 256)
    shared["w_fo"] = _kc_layout(np.asarray(w_ffn_out, f32)[0])
    z16 = np.zeros((16, 512), f32)
    shared["upaug_f"] = np.ascontiguousarray(np.concatenate(
        [np.asarray(gla_gate_up_fwd, f32)[0], z16, np.asarray(gla_gate_bias_fwd, f32)[0][None, :]], axis=0))
    shared["upaug_b"] = np.ascontiguousarray(np.concatenate(
        [z16, np.asarray(gla_gate_up_bwd, f32)[0], np.asarray(gla_gate_bias_bwd, f32)[0][None, :]], axis=0))
    shared["gmix_col"] = np.ascontiguousarray(np.asarray(norm_mix_g, f32)[0].reshape(8, 128).T)
    shared["gffn_col"] = np.ascontiguousarray(np.asarray(norm_ffn_g, f32)[0].reshape(8, 128).T)
    shared["gout"] = np.asarray(gla_out_norm_g, f32).reshape(1, 256)
    shared["gq"] = np.asarray(swa_q_norm_g, f32).reshape(1, 128)
    shared["gk"] = np.asarray(swa_k_norm_g, f32).reshape(1, 128)
    shared["sinks"] = np.asarray(swa_sinks, f32).reshape(1, 8)
    half = 64
    inv_freq = (np.float32(10000.0) ** (-np.arange(half, dtype=f32) / f32(half))).astype(f32)
    ang = (np.arange(S, dtype=f32)[:, None] * inv_freq[None, :]).astype(f32)
    cos = np.cos(ang).astype(f32).reshape(NT, 128, half).transpose(1, 0, 2).reshape(128, NT * half)
    sin = np.sin(ang).astype(f32).reshape(NT, 128, half).transpose(1, 0, 2).reshape(128, NT * half)
    shared["cos_t"] = np.ascontiguousarray(cos)
    shared["sin_t"] = np.ascontiguousarray(sin)

    if "nc" not in _NC_CACHE:
        _NC_CACHE["nc"] = build_program()
    nc = _NC_CACHE["nc"]
    in_maps = []
    for c in range(8):
        m = dict(shared)
        m["x"] = np.ascontiguousarray(x[c])
        in_maps.append(m)
    res = run_bass_kernel_spmd(nc, in_maps, core_ids=list(range(8)))
    return np.stack([np.asarray(r["out"], dtype=f32) for r in res.results], axis=0)
```

```python
import numpy as np
from contextlib import ExitStack
import concourse.bass as bass
import concourse.mybir as mybir
from concourse.bass_utils import run_bass_kernel_spmd

F32 = mybir.dt.float32
BF16 = mybir.dt.bfloat16
AF = mybir.ActivationFunctionType
ALU = mybir.AluOpType
AX = mybir.AxisListType

S = 4096
D = 1024
NT = 32
NG = 8
DFF = 2816
NJ = 22
EPS = 1e-6
SEM_LIMIT = 30000


class Buf:
    __slots__ = ("w", "r")

    def __init__(self):
        self.w = None
        self.r = []


class SemObj:
    __slots__ = ("h", "val", "id")
    _n = 0

    def __init__(self, h):
        self.h = h
        self.val = 0
        SemObj._n += 1
        self.id = SemObj._n


class Eng:
    def __init__(self, fw, name):
        self.fw = fw
        self.name = name
        self.ops = []
        self.sem = None
        self.waited = {}

    def cur_sem(self):
        if self.sem is None or self.sem.val >= SEM_LIMIT:
            self.sem = self.fw.new_sem(self.name)
        return self.sem


class FW:
    def __init__(self, nc, n_dma_sems=48):
        self.nc = nc
        self.engs = {n: Eng(self, n) for n in ("pe", "act", "dve", "pool", "sp")}
        self.dma_pool = []
        self.dma_pools = {}
        self.dma_rrs = {}
        self.n_dma_sems = n_dma_sems
        self.sems = []

    def new_sem(self, name):
        h = self.nc.alloc_semaphore(name=f"s_{name}_{len(self.sems)}")
        s = SemObj(h)
        self.sems.append(s)
        return s

    def _waits_for(self, eng, reads, writes):
        deps = {}

        def add(tok):
            if tok is None:
                return
            s, v = tok
            if deps.get(s.id, (None, -1))[1] < v:
                deps[s.id] = (s, v)
        for b in reads:
            add(b.w)
        for b in writes:
            add(b.w)
            for t in b.r:
                add(t)
        out = []
        for sid, (s, v) in deps.items():
            if eng.name == "pe" and eng.sem is not None and sid == eng.sem.id:
                continue
            if eng.waited.get(sid, -1) >= v:
                continue
            eng.waited[sid] = v
            out.append((s, v))
        return out

    def _commit(self, tok, reads, writes):
        for b in writes:
            b.w = tok
            b.r = []
        for b in reads:
            if b in writes:
                continue
            b.r.append(tok)
            if len(b.r) > 48:
                best = {}
                for s, v in b.r:
                    if best.get(s.id, (None, -1))[1] < v:
                        best[s.id] = (s, v)
                b.r = list(best.values())

    def op(self, engname, fns, reads=(), writes=()):
        eng = self.engs[engname]
        if not isinstance(fns, (list, tuple)):
            fns = [fns]
        waits = self._waits_for(eng, reads, writes)
        sem = eng.cur_sem()
        sem.val += 1
        tok = (sem, sem.val)
        fns = list(fns)

        def emit(e, waits=waits, fns=fns, sem=sem):
            for s, v in waits:
                e.wait_ge(s.h, v)
            ins = None
            for f in fns:
                ins = f(e)
            ins.then_inc(sem.h, 1)
        eng.ops.append(emit)
        self._commit(tok, reads, writes)
        return tok

    def dma(self, qname, fn, reads=(), writes=()):
        eng = self.engs[qname]
        waits = self._waits_for(eng, reads, writes)
        pool = self.dma_pools.setdefault(qname, [])
        npool = self.n_dma_sems if qname == "sp" else 16
        if len(pool) < npool:
            s = self.new_sem("dma" + qname)
            pool.append(s)
            self.dma_pool.append(s)
        else:
            k = self.dma_rrs.get(qname, 0)
            s = pool[k % npool]
            self.dma_rrs[qname] = k + 1
        prev = s.val
        pre = []
        if prev > 0 and eng.waited.get(s.id, -1) < prev:
            pre.append((s, prev))
            eng.waited[s.id] = prev
        s.val += 16
        tok = (s, s.val)

        def emit(e, waits=waits + pre, fn=fn, s=s):
            for ss, v in waits:
                e.wait_ge(ss.h, v)
            fn(e).then_inc(s.h, 16)
        eng.ops.append(emit)
        self._commit(tok, reads, writes)
        return tok

    def barrier(self):
        toks = []
        for n in ("pe", "act", "dve", "pool"):
            s = self.engs[n].sem
            if s is not None and s.val > 0:
                toks.append((s, s.val))
        for s in self.dma_pool:
            if s.val > 0:
                toks.append((s, s.val))
        for n, eng in self.engs.items():
            ws = []
            for s, v in toks:
                if eng.waited.get(s.id, -1) >= v:
                    continue
                if eng.sem is not None and s.id == eng.sem.id:
                    continue
                eng.waited[s.id] = v
                ws.append((s, v))

            def emit(e, ws=ws):
                for s, v in ws:
                    e.wait_ge(s.h, v)
            eng.ops.append(emit)

    def emit_all(self):
        nc = self.nc
        with nc.Block() as block:
            @block.tensor
            def _(e):
                for f in self.engs["pe"].ops:
                    f(e)

            @block.scalar
            def _(e):
                for f in self.engs["act"].ops:
                    f(e)

            @block.vector
            def _(e):
                for f in self.engs["dve"].ops:
                    f(e)

            @block.gpsimd
            def _(e):
                for f in self.engs["pool"].ops:
                    f(e)

            @block.sync
            def _(e):
                for f in self.engs["sp"].ops:
                    f(e)


ARENA_BF = 106400


class Arena:
    def __init__(self, ap):
        self.ap = ap
        self.off = 0
        self.top = ARENA_BF
        self.hw = 0

    def mark(self):
        return self.off

    def alloc_at(self, off, free_shape, dt):
        n = 1
        for s_ in free_shape:
            n *= s_
        units = n * (2 if dt == F32 else 1)
        assert off % 16 == 0 and off + units <= self.top
        v = self.ap[:, off:off + units]
        if dt == F32:
            v = v.bitcast(F32)
        if len(free_shape) == 2:
            v = v.rearrange("p (a b) -> p a b", b=free_shape[1])
        elif len(free_shape) == 3:
            v = v.rearrange("p (a b c) -> p a b c", b=free_shape[1], c=free_shape[2])
        return v

    def alloc_top(self, free_shape, dt):
        n = 1
        for s in free_shape:
            n *= s
        units = n * (2 if dt == F32 else 1)
        self.top = (self.top - units) // 16 * 16
        assert self.top >= self.off, "arena overflow (top)"
        o = self.top
        v = self.ap[:, o:o + units]
        if dt == F32:
            v = v.bitcast(F32)
        if len(free_shape) == 2:
            v = v.rearrange("p (a b) -> p a b", b=free_shape[1])
        return v

    def release(self, m):
        self.off = m

    def alloc(self, free_shape, dt):
        n = 1
        for s in free_shape:
            n *= s
        units = n * (2 if dt == F32 else 1)
        self.off = (self.off + 15) // 16 * 16
        o = self.off
        self.off += units
        assert self.off <= self.top, f"arena overflow {self.off} > {self.top}"
        self.hw = max(self.hw, self.off)
        v = self.ap[:, o:o + units]
        if dt == F32:
            v = v.bitcast(F32)
        if len(free_shape) == 2:
            v = v.rearrange("p (a b) -> p a b", b=free_shape[1])
        elif len(free_shape) == 3:
            v = v.rearrange("p (a b c) -> p a b c", b=free_shape[1], c=free_shape[2])
        return v


def build_program():
    nc = bass.Bass("TRN2", target_bir_lowering=False)

    def din(name, shape, dt=F32):
        return nc.dram_tensor(name, list(shape), dt, kind="ExternalInput").ap()

    x_d = din("x", [S, D])
    w_lr_d = din("w_lr", [128, 8 * 32])
    w_sq_d = din("w_sq", [128, 8 * 1024])
    w_skv_d = din("w_skv", [128, 8 * 512])
    w_gb_d = din("w_gb", [128, 8 * 1024])
    w_ga_d = din("w_ga", [128, 8 * 1024])
    w_os_d = din("w_os", [128, 8 * 1024])
    w_og_d = din("w_og", [128, 8 * 1024])
    w_out_d = din("w_out", [128, 8 * 1024])
    w_gh_d = [din(f"w_gh{h}", [128, 8 * 768]) for h in range(4)]
    w_fg_d = din("w_fg", [128, 8 * DFF])
    w_fu_d = din("w_fu", [128, 8 * DFF])
    w_fo_d = din("w_fo", [128, NJ * 1024])
    upf_d = din("upaug_f", [33, 512])
    upb_d = din("upaug_b", [33, 512])
    gmix_d = din("gmix_col", [128, 8])
    gffn_d = din("gffn_col", [128, 8])
    gout_d = din("gout", [1, 256])
    gq_d = din("gq", [1, 128])
    gk_d = din("gk", [1, 128])
    sinks_d = din("sinks", [1, 8])
    cos_d = din("cos_t", [128, NT * 64])
    sin_d = din("sin_t", [128, NT * 64])
    out_d = nc.dram_tensor("out", [S, D], F32, kind="ExternalOutput").ap()
    hT_d = nc.dram_tensor("hT_scr", [128, 8, S], BF16, kind="Internal").ap()
    ys_d = nc.dram_tensor("ys_scr", [128, 8, S], BF16, kind="Internal").ap()
    yg_d = nc.dram_tensor("yg_scr", [128, 8, S], BF16, kind="Internal").ap()
    h2_d = nc.dram_tensor("h2_scr", [128, 8, S], BF16, kind="Internal").ap()

    fw = FW(nc)
    es = ExitStack()
    with es:
        arena_t = es.enter_context(nc.sbuf_tensor("arena", [128, ARENA_BF], BF16))
        pp = es.enter_context(nc.psum_tensor("pp", [128, 8, 512], F32))
        ar = Arena(arena_t)
        PB = [Buf() for _ in range(8)]

        def bank(b):
            return pp[:, b, :]

        def bank16(b):
            return pp[:, b, :].bitcast(BF16)

        def mm(out_ap, pairs, reads, writes, extra=None):
            fns = []
            n = len(pairs)
            for i, (l, r) in enumerate(pairs):
                fns.append(lambda e, l=l, r=r, i=i, n=n, o=out_ap: e.matmul(
                    o, lhsT=l, rhs=r, start=(i == 0), stop=(i == n - 1)))
            if extra:
                fns = fns + extra
            return fw.op("pe", fns, reads, writes)

        def mmfns(out_ap, pairs):
            fns = []
            n = len(pairs)
            for i, (l, r) in enumerate(pairs):
                fns.append(lambda e, l=l, r=r, i=i, n=n, o=out_ap: e.matmul(
                    o, lhsT=l, rhs=r, start=(i == 0), stop=(i == n - 1)))
            return fns

        def act(out, in_, func, reads, writes, **kw):
            return fw.op("act", lambda e: e.activation(out=out, in_=in_, func=func, **kw), reads, writes)

        def tt(eng, out, in0, in1, op, reads, writes):
            return fw.op(eng, lambda e: e.tensor_tensor(out=out, in0=in0, in1=in1, op=op), reads, writes)

        def ts(eng, out, in0, s1, s2, op0, op1, reads, writes):
            if s2 is None:
                return fw.op(eng, lambda e: e.tensor_scalar(out=out, in0=in0, scalar1=s1, scalar2=None, op0=op0), reads, writes)
            return fw.op(eng, lambda e: e.tensor_scalar(out=out, in0=in0, scalar1=s1, scalar2=s2, op0=op0, op1=op1), reads, writes)

        def stt(eng, out, in0, scalar, in1, op0, op1, reads, writes):
            return fw.op(eng, lambda e: e.scalar_tensor_tensor(out=out, in0=in0, scalar=scalar, in1=in1, op0=op0, op1=op1), reads, writes)

        def copy(eng, out, in_, reads, writes):
            if eng == "act":
                return act(out, in_, AF.Copy, reads, writes)
            return fw.op(eng, lambda e: e.tensor_copy(out=out, in_=in_), reads, writes)

        def dma(q, out, in_, reads, writes):
            return fw.dma(q, lambda e: e.dma_start(out=out, in_=in_), reads, writes)

        def rstd_from_ss(ss, ms, rs, n, width, Bss, Bms, Brs, nhalf):
            ts("dve", ms, ss, 1.0 / n, EPS, ALU.mult, ALU.add, [Bss], [Bms])
            tt("pool", rs, ms, nhalf[:, 0:width], ALU.pow, [Bms, Bconst], [Brs])

        ident = ar.alloc([128], BF16)
        ones_bf = ar.alloc([128], BF16)
        mask2 = ar.alloc([256], BF16)
        mprev = ar.alloc([4, 128], BF16)
        mnext = ar.alloc([4, 128], BF16)
        gmix = ar.alloc([8], F32)
        gffn = ar.alloc([8], F32)
        nhalf = ar.alloc([16], F32)
        rm = ar.alloc([512], F32)
        Bconst = Buf()
        pool_ms = lambda ap, v: fw.op("pool", lambda e: e.memset(ap, v), [], [Bconst])

        def asel(ap, pattern, cm, cmp, fill=0.0):
            fw.op("pool", lambda e: e.affine_select(out=ap, in_=ap, pattern=pattern, compare_op=cmp,
                                                    fill=fill, base=0, channel_multiplier=cm), [Bconst], [Bconst])
        pool_ms(ident, 1.0)
        asel(ident, [[-1, 128]], 1, ALU.is_equal)
        pool_ms(nhalf, -0.5)
        dma("sp", gmix, gmix_d, [], [Bconst])
        dma("sp", gffn, gffn_d, [], [Bconst])

        lrT = ar.alloc([S], BF16)
        BlrT = Buf()
        fw.op("pool", lambda e: e.memset(lrT[32:33, :], 1.0), [], [BlrT])

        wh0_top = ar.alloc_top([8, 768], BF16)
        upf = ar.alloc_top([512], BF16)
        upb = ar.alloc_top([512], BF16)
        gout = ar.alloc_top([256], F32)
        Bc3 = Buf()
        dma("pool", upf[0:33, :], upf_d, [], [Bc3])
        dma("pool", upb[0:33, :], upb_d, [], [Bc3])
        dma("sp", gout, gout_d.partition_broadcast(128), [], [Bc3])
        ts("dve", gout, gout, 0.5, None, ALU.mult, None, [Bc3], [Bc3])
        top_after_wh0 = ar.top
        wsq = ar.alloc_top([8, 1024], BF16)
        wskv = ar.alloc_top([8, 512], BF16)
        Bw2 = Buf()
        wlr_top = ar.alloc_top([8, 32], BF16)
        Bwlr = Buf()
        dma("pool", wlr_top.rearrange("p a b -> p (a b)"), w_lr_d, [], [Bwlr])
        dma("pool", wskv.rearrange("p a b -> p (a b)"), w_skv_d, [], [Bw2])
        for kc in range(0, 8, 2):
            dma("pool", wsq[:, kc:kc + 2, :].rearrange("p a b -> p (a b)"),
                w_sq_d[:, kc * 1024:(kc + 2) * 1024], [], [Bw2])
        pool_ms(ones_bf, 1.0)
        pool_ms(mask2, 1.0)
        asel(mask2[:, 0:128], [[1, 128]], -1, ALU.is_ge)
        asel(mask2[:, 128:256], [[-1, 128]], 1, ALU.is_gt)
        pool_ms(mprev, 0.0)
        asel(mprev, [[0, 4], [-1, 128]], 1, ALU.is_ge, fill=-30000.0)
        pool_ms(mnext, 0.0)
        asel(mnext, [[0, 4], [1, 128]], -1, ALU.is_ge, fill=-30000.0)
        pool_ms(rm, 1.0)
        pool_ms(rm.rearrange("p (c t) -> p c t", t=128)[:, :, 0:1], 0.0)
        cos_t = ar.alloc_top([NT, 64], F32)
        sin_t = ar.alloc_top([NT, 64], F32)
        gqk = ar.alloc_top([2, 128], F32)
        sk8 = ar.alloc_top([8], F32)
        se8 = ar.alloc_top([8], F32)
        sinkrow = ar.alloc_top([8, 128], BF16)
        negshift = ar.alloc_top([1], F32)
        tmpc = ar.alloc_top([128], F32)
        mx = ar.alloc_top([4], F32)
        Bc2 = Buf()
        dma("sp", cos_t.rearrange("p a b -> p (a b)"), cos_d, [], [Bc2])
        dma("sp", sin_t.rearrange("p a b -> p (a b)"), sin_d, [], [Bc2])
        dma("sp", gqk[:, 0, :], gq_d.partition_broadcast(128), [], [Bc2])
        dma("sp", gqk[:, 1, :], gk_d.partition_broadcast(128), [], [Bc2])
        dma("sp", sk8, sinks_d.partition_broadcast(128), [], [Bc2])
        tt("dve", tmpc, gqk[:, 0, :], gqk[:, 0, :], ALU.mult, [Bc2], [Bc2])
        fw.op("dve", lambda e: e.tensor_reduce(out=mx[:, 0:1], in_=tmpc, axis=AX.X, op=ALU.max), [Bc2], [Bc2])
        tt("dve", tmpc, gqk[:, 1, :], gqk[:, 1, :], ALU.mult, [Bc2], [Bc2])
        fw.op("dve", lambda e: e.tensor_reduce(out=mx[:, 1:2], in_=tmpc, axis=AX.X, op=ALU.max), [Bc2], [Bc2])
        tt("dve", mx[:, 2:3], mx[:, 0:1], mx[:, 1:2], ALU.mult, [Bc2], [Bc2])
        tt("pool", mx[:, 3:4], mx[:, 2:3], nhalf[:, 0:1], ALU.pow, [Bc2, Bconst], [Bc2])
        fw.op("dve", lambda e: e.reciprocal(out=mx[:, 2:3], in_=mx[:, 3:4]), [Bc2], [Bc2])
        ts("dve", negshift, mx[:, 2:3], -(128.0 ** 0.5), None, ALU.mult, None, [Bc2], [Bc2])
        act(se8, sk8, AF.Exp, [Bc2], [Bc2], bias=negshift)
        copy("dve", sinkrow[0:1, :, :], se8[0:1, :].unsqueeze(2).to_broadcast([1, 8, 128]), [Bc2], [Bc2])


        def rstd_act(ss, ms, rs, n, Bss, Bms, Brs):
            act(ms, ss, AF.Ln, [Bss], [Bms], scale=1.0 / n, bias=EPS)
            act(rs, ms, AF.Exp, [Bms], [Brs], scale=-0.5)

        def norm_a(xt, Bxt, junk, Bjunk, st, Bst, use_act=False):
            act(junk, xt, AF.Square, [Bxt], [Bjunk, Bst[0]], accum_out=st[0])
            if use_act:
                rstd_act(st[0], st[1], st[2], D, Bst[0], Bst[1], Bst[2])
            else:
                rstd_from_ss(st[0], st[1], st[2], D, 1, Bst[0], Bst[1], Bst[2], nhalf)

        def norm_b(xt, Bxt, st, Bst, xs, Bxs, tb_bank, stage, Bstage, slot, gcol):
            act(xs, xt, AF.Copy, [Bxt, Bst[2]], [Bxs], scale=st[2])
            tp = bank16(tb_bank).rearrange("p (c t) -> p c t", t=128)
            fns = [lambda e, c=c: e.transpose(out=tp[:, c, :], in_=xs[:, c * 128:(c + 1) * 128], identity=ident)
                   for c in range(8)]
            fw.op("pe", fns, [Bxs, Bconst], [PB[tb_bank]])
            tt("dve", stage[:, :, slot * 128:(slot + 1) * 128], tp,
               gcol.unsqueeze(2).to_broadcast([128, 8, 128]), ALU.mult, [PB[tb_bank], Bconst], [Bstage])

        m0 = ar.mark()
        wlr = wlr_top
        NX = 6
        xts = [ar.alloc([D], F32) for _ in range(NX)]
        Bxts = [Buf() for _ in range(NX)]
        xss = [ar.alloc([D], BF16) for _ in range(2)]
        Bxss = [Buf() for _ in range(2)]
        junk = ar.alloc([D], BF16)
        Bjunk = Buf()
        stats = [[ar.alloc([1], F32) for _ in range(3)] for _ in range(NX)]
        Bstats = [[Buf() for _ in range(3)] for _ in range(NX)]
        hst = [ar.alloc([8, 512], BF16) for _ in range(3)]
        Bhst = [Buf() for _ in range(3)]
        BhT = [Buf() for _ in range(NG)]

        def p1_load(t):
            dma("sp", xts[t % NX], x_d[t * 128:(t + 1) * 128, :], [], [Bxts[t % NX]])

        def p1_a(t):
            if t + 3 < NT:
                p1_load(t + 3)
            norm_a(xts[t % NX], Bxts[t % NX], junk, Bjunk, stats[t % NX], Bstats[t % NX], use_act=True)

        def p1_b(t):
            g, sl = divmod(t, 4)
            norm_b(xts[t % NX], Bxts[t % NX], stats[t % NX], Bstats[t % NX], xss[t % 2], Bxss[t % 2],
                   t % 2, hst[g % 3], Bhst[g % 3], sl, gmix)

        def p1_store(g):
            hs, Bhs = hst[g % 3], Bhst[g % 3]
            dma("sp", hT_d[:, :, g * 512:(g + 1) * 512], hs, [Bhs], [BhT[g]])
            mm(bank(2)[0:32, :], [(wlr[:, kc, :], hs[:, kc, :]) for kc in range(8)], [Bwlr, Bhs], [PB[2]])

        def p1_lrcopy(g):
            copy("act", lrT[0:32, g * 512:(g + 1) * 512], bank(2)[0:32, :], [PB[2]], [BlrT])
        for t_ in range(3):
            p1_load(t_)
        for t in range(NT + 8):
            if t < NT:
                p1_a(t)
            if 0 <= t - 2 < NT:
                p1_b(t - 2)
            if t >= 5 and (t - 5) % 4 == 3 and (t - 5) // 4 < NG:
                p1_store((t - 5) // 4)
            if t >= 7 and (t - 7) % 4 == 3 and (t - 7) // 4 < NG:
                p1_lrcopy((t - 7) // 4)
        fw.barrier()
        ar.release(m0)
        print("arena hw phase1", ar.hw, "top", ar.top)

        m0 = ar.mark()
        Bwhs = [Buf() for _ in range(2)]
        dma("pool", wh0_top.rearrange("p a b -> p (a b)"), w_gh_d[0], [], [Bwhs[0]])
        hgs = [ar.alloc([8, 512], BF16) for _ in range(2)]
        Bhgs = [Buf() for _ in range(2)]
        kT_all = ar.alloc([2, 8 * 128], BF16)
        BkT = [Buf() for _ in range(8)]
        v_all = ar.alloc([8, 256], BF16)
        Bv = [Buf() for _ in range(8)]
        qsb = [ar.alloc([10, 128], F32) for _ in range(2)]
        Bqsb = [Buf() for _ in range(2)]
        qT = ar.alloc([8, 8, 128], BF16)
        BqT = [Buf() for _ in range(8)]
        yst = [ar.alloc([8, 512], BF16) for _ in range(2)]
        Byst = [Buf() for _ in range(2)]
        Pt = [[[ar.alloc([512], BF16) for _ in range(3)] for _ in range(2)] for _ in range(2)]
        BPt = [[[Buf() for _ in range(3)] for _ in range(2)] for _ in range(2)]
        sq1 = ar.alloc([10, 128], F32)
        sq = [sq1, sq1]
        qn = [ar.alloc([10, 128], F32) for _ in range(2)]
        rA = [ar.alloc([10, 128], F32) for _ in range(2)]
        rB = [ar.alloc([10, 128], F32) for _ in range(2)]
        qr = [ar.alloc([10, 128], BF16) for _ in range(3)]
        cg = [ar.alloc([2, 128], F32) for _ in range(2)]
        sgn = [ar.alloc([2, 128], F32) for _ in range(2)]
        Bsq1 = Buf()
        Bsq = [Bsq1, Bsq1]
        Bqn = [Buf() for _ in range(2)]
        BrA = [Buf() for _ in range(2)]
        BrB = [Buf() for _ in range(2)]
        Bqr = [Buf() for _ in range(3)]
        Bcg = [Buf() for _ in range(2)]
        st10 = [[ar.alloc([10], F32) for _ in range(3)] for _ in range(2)]
        Bst10 = [[Buf() for _ in range(3)] for _ in range(2)]
        lnden = [ar.alloc([512], F32) for _ in range(2)]
        rden = lnden
        Blnden = [Buf() for _ in range(2)]
        Brden = Blnden
        Bys = [Buf() for _ in range(NG)]
        inv_sqrt_hd = 128.0 ** -0.5

        def swa_proj_a(t):
            g, sl = divmod(t, 4)
            p = t % 2
            hg, Bhg = hgs[g % 2], Bhgs[g % 2]
            if sl == 0:
                dma("sp", hg, hT_d[:, :, g * 512:(g + 1) * 512], [BhT[g]], [Bhg])
            lhs = [hg[:, kc, sl * 128:(sl + 1) * 128] for kc in range(8)]
            qf = qsb[p].rearrange("p a b -> p (a b)")
            mm(bank(0), [(lhs[kc], wsq[:, kc, 0:512]) for kc in range(8)], [Bhg, Bw2], [PB[0]])
            copy("act", qf[:, 0:512], bank(0), [PB[0]], [Bqsb[p]])
            mm(bank(1), [(lhs[kc], wsq[:, kc, 512:1024]) for kc in range(8)], [Bhg, Bw2], [PB[1]])
            copy("act", qf[:, 512:1024], bank(1), [PB[1]], [Bqsb[p]])
            mm(bank(0), [(lhs[kc], wskv[:, kc, :]) for kc in range(8)], [Bhg, Bw2], [PB[0]])
            copy("act", qf[:, 1024:1280], bank(0)[:, 0:256], [PB[0]], [Bqsb[p]])
            copy("act", v_all[:, t % 8, :], bank(0)[:, 256:512], [PB[0]], [Bv[t % 8]])
            act(sq[p], qsb[p], AF.Square, [Bqsb[p]], [Bsq[p]])
            st, Bst = st10[p], Bst10[p]
            fw.op("dve", lambda e: e.tensor_reduce(out=st[0], in_=sq[p], axis=AX.X, op=ALU.add), [Bsq[p]], [Bst[0]])
            c2 = cos_t[:, t, :].unsqueeze(1).unsqueeze(1).to_broadcast([128, 2, 2, 64])
            s2_ = sin_t[:, t, :].unsqueeze(1).unsqueeze(1).to_broadcast([128, 2, 2, 64])
            tt("pool", cg[p].rearrange("p a (h j) -> p a h j", h=2), gqk.rearrange("p a (h j) -> p a h j", h=2), c2,
               ALU.mult, [Bc2], [Bcg[p]])
            tt("pool", sgn[p].rearrange("p a (h j) -> p a h j", h=2), gqk.rearrange("p a (h j) -> p a h j", h=2), s2_,
               ALU.mult, [Bc2], [Bcg[p]])

        def swa_proj_b(t):
            p = t % 2
            rs = st10[p][2]
            Brs = Bst10[p][2]
            rstd_act(st10[p][0], st10[p][1], rs, 128, Bst10[p][0], Bst10[p][1], Brs)
            tt("dve", qn[p], qsb[p], rs.unsqueeze(2).to_broadcast([128, 10, 128]), ALU.mult,
               [Bqsb[p], Brs], [Bqn[p]])
            tt("pool", rA[p][:, 0:8, :], qn[p][:, 0:8, :], cg[p][:, 0:1, :].to_broadcast([128, 8, 128]),
               ALU.mult, [Bqn[p], Bcg[p]], [BrA[p]])
            tt("pool", rA[p][:, 8:10, :], qn[p][:, 8:10, :], cg[p][:, 1:2, :].to_broadcast([128, 2, 128]),
               ALU.mult, [Bqn[p], Bcg[p]], [BrA[p]])
            tt("dve", rB[p][:, 0:8, :], qn[p][:, 0:8, :], sgn[p][:, 0:1, :].to_broadcast([128, 8, 128]),
               ALU.mult, [Bqn[p], Bcg[p]], [BrB[p]])
            tt("dve", rB[p][:, 8:10, :], qn[p][:, 8:10, :], sgn[p][:, 1:2, :].to_broadcast([128, 2, 128]),
               ALU.mult, [Bqn[p], Bcg[p]], [BrB[p]])
            tt("dve", qr[t % 3][:, :, 0:64], rA[p][:, :, 0:64], rB[p][:, :, 64:128], ALU.subtract,
               [BrA[p], BrB[p]], [Bqr[t % 3]])
            tt("pool", qr[t % 3][:, :, 64:128], rA[p][:, :, 64:128], rB[p][:, :, 0:64], ALU.add,
               [BrA[p], BrB[p]], [Bqr[t % 3]])

        def swa_transpose_a(t):
            slot = t % 8
            p = t % 3
            tp = bank16(3).rearrange("p (c t) -> p c t", t=128)
            fns = [lambda e, c=c: e.transpose(out=tp[:, c, :], in_=qr[p][:, 8 + c, :], identity=ident) for c in range(2)]
            fns += [lambda e, c=c: e.transpose(out=tp[:, 2 + c, :], in_=qr[p][:, c, :], identity=ident) for c in range(6)]
            fw.op("pe", fns, [Bqr[p], Bconst], [PB[3]])
            copy("dve", kT_all[:, :, slot * 128:(slot + 1) * 128], tp[:, 0:2, :], [PB[3]], [BkT[slot]])
            copy("dve", qT[:, slot, 0:6, :], tp[:, 2:8, :], [PB[3]], [BqT[slot]])

        def swa_transpose_b(t):
            slot = t % 8
            p = t % 3
            tp = bank16(3).rearrange("p (c t) -> p c t", t=128)
            fns = [lambda e, c=c: e.transpose(out=tp[:, c, :], in_=qr[p][:, 6 + c, :], identity=ident) for c in range(2)]
            fw.op("pe", fns, [Bqr[p], Bconst], [PB[3]])
            copy("dve", qT[:, slot, 6:8, :], tp[:, 0:2, :], [PB[3]], [BqT[slot]])

        sbank = [5, 6]
        scnt = [0]

        def swa_scores(b):
            slot = b % 8
            for kvh in range(2):
                for oi, o in enumerate((b - 1, b, b + 1)):
                    if o < 0 or o >= NT:
                        continue
                    sbk = sbank[scnt[0] % 2]
                    scnt[0] += 1
                    pairs = [(kT_all[:, kvh, (o % 8) * 128:(o % 8 + 1) * 128],
                              qT[:, slot, kvh * 4:(kvh + 1) * 4, :].rearrange("p a b -> p (a b)"))]
                    if o != b:
                        m = mprev if o < b else mnext
                        pairs.append((ident, m.rearrange("p a b -> p (a b)")))
                    mm(bank(sbk), pairs, [BkT[o % 8], BqT[slot], Bconst], [PB[sbk]])
                    P, BP = Pt[b % 2][kvh][oi], BPt[b % 2][kvh][oi]
                    act(P, bank(sbk), AF.Exp, [PB[sbk], Bc2], [BP], bias=negshift, scale=inv_sqrt_hd)

        def swa_pv(b):
            g, sl = divmod(b, 4)
            ys, Bys_ = yst[g % 2], Byst[g % 2]
            for kvh in range(2):
                ob = 7 if kvh == 0 else 2
                valid = [(oi, o) for oi, o in enumerate((b - 1, b, b + 1)) if 0 <= o < NT]
                pairs = [(v_all[:, o % 8, kvh * 128:(kvh + 1) * 128], Pt[b % 2][kvh][oi]) for oi, o in valid]
                rd = [Bv[o % 8] for _, o in valid] + [BPt[b % 2][kvh][oi] for oi, _ in valid]
                mm(bank(ob), pairs, rd, [PB[ob]])
                pairs2 = [(ones_bf, Pt[b % 2][kvh][oi]) for oi, _ in valid]
                pairs2.append((ones_bf[0:1, :], sinkrow[0:1, kvh * 4:(kvh + 1) * 4, :].rearrange("p a b -> p (a b)")))
                mm(bank(4), pairs2, rd + [Bconst, Bc2], [PB[4]])
                act(lnden[kvh], bank(4), AF.Ln, [PB[4]], [Blnden[kvh]])
                act(rden[kvh], lnden[kvh], AF.Exp, [Blnden[kvh]], [Brden[kvh]], scale=-1.0)
                tt("dve", ys[:, kvh * 4:(kvh + 1) * 4, sl * 128:(sl + 1) * 128],
                   bank(ob).rearrange("p (a b) -> p a b", b=128), rden[kvh].rearrange("p (a b) -> p a b", b=128),
                   ALU.mult, [PB[ob], Brden[kvh]], [Bys_])
            if sl == 3:
                dma("sp", ys_d[:, :, g * 512:(g + 1) * 512], ys, [Bys_], [Bys[g]])

        for it in range(NT + 6):
            if it < NT:
                swa_proj_a(it)
            if 0 <= it - 3 < NT:
                swa_transpose_b(it - 3)
            if 0 <= it - 6 < NT:
                swa_pv(it - 6)
            if 0 <= it - 5 < NT:
                swa_scores(it - 5)
            if 0 <= it - 2 < NT:
                swa_transpose_a(it - 2)
            if it < NT:
                swa_proj_b(it)
        fw.barrier()
        ar.release(m0)
        ar.top = top_after_wh0
        print("arena hw phase2", ar.hw)

        m0 = ar.mark()
        whs = [wh0_top, ar.alloc([8, 768], BF16)]
        hgs = [ar.alloc([8, 512], BF16) for _ in range(2)]
        Bhgs = [Buf() for _ in range(2)]
        QD = [ar.alloc([S], BF16) for _ in range(2)]
        KI = [ar.alloc([S], BF16) for _ in range(2)]
        BQK = [[Buf() for _ in range(NG)] for _ in range(2)]
        KIt = [ar.alloc([NT, 128], BF16) for _ in range(2)]
        BKIt = [[Buf() for _ in range(NG)] for _ in range(2)]
        Vh = ar.alloc([NT, 256], BF16)
        BVh = [Buf() for _ in range(NT)]
        S2 = ar.alloc([NT, 256], BF16)
        BS2 = [Buf() for _ in range(NT)]
        SBh = ar.alloc([NT, 256], BF16)
        BSB = [Buf() for _ in range(NT)]
        eT = [ar.alloc([NT], F32) for _ in range(2)]
        BeT = [[Buf() for _ in range(NG)] for _ in range(2)]
        ar.off = (ar.off + 15) // 16 * 16
        tmp_off = ar.off
        Lg_ = [[ar.alloc([512], F32) for _ in range(2)] for _ in range(2)]
        Pp = [[ar.alloc([512], F32) for _ in range(2)] for _ in range(2)]
        Pex = [ar.alloc([512], F32) for _ in range(2)]
        Ep = [[ar.alloc([512], F32) for _ in range(2)] for _ in range(2)]
        En = [[ar.alloc([512], F32) for _ in range(2)] for _ in range(2)]
        BLg = [[Buf() for _ in range(2)] for _ in range(2)]
        BPp = [[Buf() for _ in range(2)] for _ in range(2)]
        BPex = [Buf() for _ in range(2)]
        BEp = [[Buf() for _ in range(2)] for _ in range(2)]
        BEn = [[Buf() for _ in range(2)] for _ in range(2)]
        tnh = [ar.alloc([256], F32) for _ in range(2)]
        Btnh = [Buf() for _ in range(2)]
        Sf = ar.alloc([256], F32)
        Sb = ar.alloc([256], F32)
        Sp = [ar.alloc([256], F32) for _ in range(2)]
        Se = [ar.alloc([256], F32) for _ in range(2)]
        BSf, BSb = Buf(), Buf()
        BSp = [Buf() for _ in range(2)]
        BSe = [Buf() for _ in range(2)]
        Sfb = [ar.alloc([256], BF16) for _ in range(2)]
        BSfb = [Buf() for _ in range(2)]
        Amat = [ar.alloc([256], BF16) for _ in range(2)]
        BA = [Buf() for _ in range(2)]
        on = [ar.alloc([256], F32) for _ in range(2)]
        Bon = [Buf() for _ in range(2)]
        yb = [ar.alloc([256], BF16) for _ in range(2)]
        Byb = [Buf() for _ in range(2)]
        ost = [[ar.alloc([1], F32) for _ in range(3)] for _ in range(3)]
        Bost = [[Buf() for _ in range(3)] for _ in range(3)]
        OBS = [7, 0, 3]
        ojunk = ar.alloc([256], BF16)
        Bojunk = Buf()
        ygst = [ar.alloc([2, 512], BF16) for _ in range(2)]
        Bygst = [Buf() for _ in range(2)]
        Byg = [[Buf() for _ in range(NG)] for _ in range(4)]
        dk_scale = 128.0 ** -0.5
        PB7h = [Buf(), Buf()]

        def load_wh(h):
            dma("pool", whs[h % 2].rearrange("p a b -> p (a b)"), w_gh_d[h], [], [Bwhs[h % 2]])

        for h in range(4):
            wh, Bwh = whs[h % 2], Bwhs[h % 2]
            if h + 1 < 4:
                load_wh(h + 1)

            def g1(g):
                p = g % 2
                for d in range(2):
                    up = upf if d == 0 else upb
                    mm(bank(0), [(up[0:33, h * 128:(h + 1) * 128], lrT[0:33, g * 512:(g + 1) * 512])],
                       [Bc3, BlrT], [PB[0]])
                    L = Lg_[p][d]
                    act(L, bank(0), AF.Exp, [PB[0]], [BLg[p][d]], scale=-1.0)
                    act(L, L, AF.Ln, [BLg[p][d]], [BLg[p][d]], bias=1.0)
                    ts("dve", L, L, -1.0 / 16.0, -0.5, ALU.mult, ALU.max, [BLg[p][d]], [BLg[p][d]])
                    fw.op("dve", lambda e, L=L, P=Pp[p][d]: e.tensor_tensor_scan(
                        out=P, data0=rm, data1=L, initial=0.0, op0=ALU.mult, op1=ALU.add),
                        [BLg[p][d], Bconst], [BPp[p][d]])
                tt("dve", Pex[p], Pp[p][1], Lg_[p][1], ALU.subtract, [BPp[p][1], BLg[p][1]], [BPex[p]])
                for d in range(2):
                    P4 = Pp[p][d].rearrange("p (c t) -> p c t", t=128)
                    act(eT[d][:, g * 4:(g + 1) * 4], P4[:, :, 127], AF.Exp, [BPp[p][d]], [BeT[d][g]])
                    src, Bsrc = (Pp[p][0], BPp[p][0]) if d == 0 else (Pex[p], BPex[p])
                    act(Ep[p][d], src, AF.Exp, [Bsrc], [BEp[p][d]])
                    act(En[p][d], src, AF.Exp, [Bsrc], [BEn[p][d]], scale=-1.0)

            def g2(g):
                p = g % 2
                hg, Bhg = hgs[p], Bhgs[p]
                if not (h > 0 and g < 2):
                    dma("sp", hg, hT_d[:, :, g * 512:(g + 1) * 512], [BhT[g]], [Bhg])
                mm(bank(1 + p), [(wh[:, kc, 0:128], hg[:, kc, :]) for kc in range(8)], [Bwh, Bhg], [PB[1 + p]])
                mm(bank(3 + p), [(wh[:, kc, 128:256], hg[:, kc, :]) for kc in range(8)], [Bwh, Bhg], [PB[3 + p]])

            def g3(g):
                p = g % 2
                gs = slice(g * 512, (g + 1) * 512)
                stt("dve", QD[0][:, gs], bank(1 + p), dk_scale, Ep[p][0], ALU.mult, ALU.mult,
                    [PB[1 + p], BEp[p][0]], [BQK[0][g]])
                stt("dve", QD[1][:, gs], bank(1 + p), dk_scale, En[p][1], ALU.mult, ALU.mult,
                    [PB[1 + p], BEn[p][1]], [BQK[1][g]])
                tt("dve", KI[0][:, gs], bank(3 + p), En[p][0], ALU.mult, [PB[3 + p], BEn[p][0]], [BQK[0][g]])
                tt("dve", KI[1][:, gs], bank(3 + p), Ep[p][1], ALU.mult, [PB[3 + p], BEp[p][1]], [BQK[1][g]])

            def g4(g):
                tp = bank16(5 + (g % 2)).rearrange("p (c t) -> p c t", t=128)
                fns = [lambda e, c=c, d=d: e.transpose(out=tp[:, d * 4 + c, :],
                                                       in_=KI[d][:, (g * 4 + c) * 128:(g * 4 + c + 1) * 128],
                                                       identity=ident) for d in range(2) for c in range(4)]
                fw.op("pe", fns, [BQK[0][g], BQK[1][g], Bconst], [PB[5 + (g % 2)]])
                for d in range(2):
                    copy("act", KIt[d][:, g * 4:(g + 1) * 4, :], tp[:, d * 4:(d + 1) * 4, :], [PB[5 + (g % 2)]],
                         [BKIt[d][g]])
            for it in range(NG + 2):
                if it < NG:
                    g1(it)
                    g2(it)
                if 0 <= it - 1 < NG:
                    g3(it - 1)
                if 0 <= it - 2 < NG:
                    g4(it - 2)

            if h == 3:
                tmp_bufs = [b for row in BLg for b in row] + [b for row in BPp for b in row] + BPex + \
                           [b for row in BEp for b in row] + [b for row in BEn for b in row]
                w4_pref = [ar.alloc_at(tmp_off + i * 8192, [4, 8, 256], BF16) for i in range(2)]
                Bw4_pref = [[Buf() for _ in range(4)] for _ in range(2)]
                for ob4 in range(2):
                    for wi, wd in enumerate((w_og_d, w_ga_d, w_os_d, w_gb_d)):
                        src = wd[:, ob4 * 2048:(ob4 + 1) * 2048].rearrange("p (a b) -> p a b", b=256)
                        extra = tmp_bufs if (ob4 == 0 and wi == 0) else []
                        dma("pool", w4_pref[ob4][:, wi, :, :], src, [], [Bw4_pref[ob4][wi]] + extra)

            fw.op("pool", lambda e: e.memset(Sb, 0.0), [], [BSb])

            def bwd_step(n):
                g = n // 4
                p = n % 2
                act(Sp[p], Sb, AF.Copy, [BSb, BeT[1][g]], [BSp[p]], scale=eT[1][:, n:n + 1])
                ts("dve", SBh[:, n, :], Sb, eT[1][:, n:n + 1], None, ALU.mult, None, [BSb, BeT[1][g]], [BSB[n]])
                mm(bank(5 + p)[:, 0:256], [(KIt[1][:, n, :], Vh[:, n, :])],
                   [BKIt[1][g], BVh[n]], [PB[5 + p]])
                tt("dve", Sb, bank(5 + p)[:, 0:256], Sp[p], ALU.add, [PB[5 + p], BSp[p]], [BSb])
            pending = []
            for g in range(NG - 1, -1, -1):
                hg, Bhg = hgs[g % 2], Bhgs[g % 2]
                dma("sp", hg, hT_d[:, :, g * 512:(g + 1) * 512], [BhT[g]], [Bhg])
                for sl in range(3, -1, -1):
                    t = g * 4 + sl
                    lhs = [hg[:, kc, sl * 128:(sl + 1) * 128] for kc in range(8)]
                    bk = 1 + (t % 4)
                    fns = mmfns(bank(bk), [(lhs[kc], wh[:, kc, 256:768]) for kc in range(8)])
                    fw.op("pe", fns, [Bhg, Bwh], [PB[bk]])
                    copy("act", Vh[:, t, :], bank(bk)[:, 0:256], [PB[bk]], [BVh[t]])
                    act(tnh[t % 2], bank(bk)[:, 256:512], AF.Tanh, [PB[bk]], [Btnh[t % 2]], scale=0.5)
                    stt("dve", S2[:, t, :], tnh[t % 2], 1.0, bank(bk)[:, 256:512], ALU.add, ALU.mult,
                        [Btnh[t % 2], PB[bk]], [BS2[t]])
                    pending.append(t)
                    if len(pending) > 2:
                        bwd_step(pending.pop(0))
            while pending:
                bwd_step(pending.pop(0))

            fw.op("pool", lambda e: e.memset(Sf, 0.0), [], [BSf])
            fw.op("pool", lambda e: e.memset(Sfb[0], 0.0), [], [BSfb[0]])

            def f_main(n):
                g, sl = divmod(n, 4)
                p = n % 2
                q3 = n % 3
                cs = slice(n * 128, (n + 1) * 128)
                A, BA_ = Amat[p], BA[p]
                sb_ = 5 + p
                fns = mmfns(bank(sb_)[:, 0:128], [(KI[0][:, cs], QD[0][:, cs])])
                fns += mmfns(bank(sb_)[:, 128:256], [(KI[1][:, cs], QD[1][:, cs])])
                fw.op("pe", fns, [BQK[0][g], BQK[1][g]], [PB[sb_]])
                tt("dve", A, bank(sb_)[:, 0:256], mask2, ALU.mult, [PB[sb_], Bconst], [BA_])
                bk = 1 + p
                mm(bank(bk)[:, 0:256], [(KIt[0][:, n, :], Vh[:, n, :])], [BKIt[0][g], BVh[n]], [PB[bk]])
                cur, nxt = Sfb[p], Sfb[1 - p]
                Bcur, Bnxt = BSfb[p], BSfb[1 - p]
                ob = OBS[q3]
                if 0 <= n - 4:
                    f_tr_pe(n - 4)
                mm(bank(ob)[:, 0:256],
                   [(A[:, 0:128], Vh[:, n, :]), (A[:, 128:256], Vh[:, n, :]),
                    (QD[0][:, cs], cur), (QD[1][:, cs], SBh[:, n, :])],
                   [BA_, BVh[n], BQK[0][g], BQK[1][g], Bcur, BSB[n]], [PB[ob]])
                act(Se[p], Sf, AF.Copy, [BSf, BeT[0][g]], [BSe[p]], scale=eT[0][:, n:n + 1])
                stt("dve", Sf, bank(bk)[:, 0:256], eT[0][:, n:n + 1], Se[p], ALU.mult, ALU.add,
                    [PB[bk], BeT[0][g], BSe[p]], [BSf])
                copy("dve", nxt, Sf, [BSf], [Bnxt])

            def f_sq(n):
                q3 = n % 3
                ob = OBS[q3]
                act(ojunk, bank(ob)[:, 0:256], AF.Square, [PB[ob]], [Bojunk, Bost[q3][0]], accum_out=ost[q3][0])

            def f_out(n):
                g, sl = divmod(n, 4)
                p = n % 2
                q3 = n % 3
                ob = OBS[q3]
                rstd_act(ost[q3][0], ost[q3][1], ost[q3][2], 256, Bost[q3][0], Bost[q3][1], Bost[q3][2])
                stt("dve", on[p], bank(ob)[:, 0:256], ost[q3][2], gout, ALU.mult, ALU.mult,
                    [PB[ob], Bost[q3][2], Bc3], [Bon[p]])
                tt("pool", yb[p], on[p], S2[:, n, :], ALU.mult, [Bon[p], BS2[n]], [Byb[p]])

            def f_tr_pe(n):
                p = n % 2
                tp = bank16(4).rearrange("p (c t) -> p c t", t=128)
                fns = [lambda e, c=c: e.transpose(out=tp[:, c, :], in_=yb[p][:, c * 128:(c + 1) * 128],
                                                  identity=ident) for c in range(2)]
                fw.op("pe", fns, [Byb[p], Bconst], [PB[4]])

            def f_tr(n):
                g, sl = divmod(n, 4)
                p = n % 2
                tp = bank16(4).rearrange("p (c t) -> p c t", t=128)
                ygs, Bygs = ygst[g % 2], Bygst[g % 2]
                copy("act", ygs[:, :, sl * 128:(sl + 1) * 128], tp[:, 0:2, :], [PB[4]], [Bygs])
                if sl == 3:
                    dma("sp", yg_d[:, 2 * h:2 * h + 2, g * 512:(g + 1) * 512], ygs, [Bygs], [Byg[h][g]])
            for it in range(NT + 4):
                if it < NT:
                    f_main(it)
                elif 0 <= it - 4 < NT:
                    f_tr_pe(it - 4)
                if 0 <= it - 2 < NT:
                    f_out(it - 2)
                if 0 <= it - 4 < NT:
                    f_tr(it - 4)
                if 0 <= it - 1 < NT:
                    f_sq(it - 1)
        fw.barrier()
        ar.release(m0)
        ar.top = ARENA_BF
        print("arena hw phase3", ar.hw)

        m0 = ar.mark()
        wblk = [None] * 4
        Bw4 = [None] * 4
        wblk[0], wblk[1] = w4_pref
        Bw4[0], Bw4[1] = Bw4_pref
        wblk[2] = ar.alloc([4, 8, 256], BF16)
        wblk[3] = ar.alloc([4, 8, 256], BF16)
        Bw4[2] = [Buf() for _ in range(4)]
        Bw4[3] = [Buf() for _ in range(4)]
        wout = ar.alloc([8, 1024], BF16)
        Bwout = Buf()
        for ob4 in range(2, 4):
            for wi, wd in enumerate((w_og_d, w_ga_d, w_os_d, w_gb_d)):
                src = wd[:, ob4 * 2048:(ob4 + 1) * 2048].rearrange("p (a b) -> p a b", b=256)
                dep = Bw4[2] if ob4 == 3 else []
                dma("pool", wblk[ob4][:, wi, :, :], src, dep, [Bw4[ob4][wi]])
        for kc in range(0, 8, 2):
            dma("pool", wout[:, kc:kc + 2, :].rearrange("p a b -> p (a b)"),
                w_out_d[:, kc * 1024:(kc + 2) * 1024], Bw4[3], [Bwout])
        ygg = [ar.alloc([8, 512], BF16) for _ in range(2)]
        ysg = [ar.alloc([8, 512], BF16) for _ in range(2)]
        hgg = [ar.alloc([8, 512], BF16) for _ in range(2)]
        Bygg = [Buf() for _ in range(2)]
        Bysg = [Buf() for _ in range(2)]
        Bhgg = [Buf() for _ in range(2)]
        Mg = [ar.alloc([8, 512], BF16)]
        BMg = [Buf()]
        tg = [ar.alloc([512], F32) for _ in range(2)]
        Btg = [Buf() for _ in range(2)]
        Ag = [ar.alloc([512], F32) for _ in range(2)]
        BAg = [Buf() for _ in range(2)]
        xts = [ar.alloc([D], F32) for _ in range(3)]
        Bxts = [Buf() for _ in range(3)]
        assert ar.off <= tmp_off, (ar.off, tmp_off)
        ar.off = tmp_off + 2 * 8192
        x1s = [ar.alloc([D], F32) for _ in range(3)]
        Bx1s = [Buf() for _ in range(3)]
        xss = [ar.alloc([D], BF16) for _ in range(2)]
        Bxss = [Buf() for _ in range(2)]
        junk = ar.alloc([D], BF16)
        Bjunk = Buf()
        stats = [[ar.alloc([1], F32) for _ in range(3)] for _ in range(3)]
        Bstats = [[Buf() for _ in range(3)] for _ in range(3)]
        h2st = [ar.alloc([8, 512], BF16) for _ in range(2)]
        Bh2st = [Buf() for _ in range(2)]
        Bx1 = [Buf() for _ in range(NT)]
        Bh2 = [Buf() for _ in range(NG)]

        def p4_load(g):
            gs = slice(g * 512, (g + 1) * 512)
            dma("sp", ygg[g % 2], yg_d[:, :, gs], [Byg[h_][g] for h_ in range(4)], [Bygg[g % 2]])
            dma("sp", ysg[g % 2], ys_d[:, :, gs], [Bys[g]], [Bysg[g % 2]])
            dma("sp", hgg[g % 2], hT_d[:, :, gs], [BhT[g]], [Bhgg[g % 2]])

        def p4_merge(g):
            yg_, ys_, hg_ = ygg[g % 2], ysg[g % 2], hgg[g % 2]
            M, BM = Mg[0], BMg[0]
            for oc in range(8):
                wb = oc // 2
                cs = slice((oc % 2) * 128, (oc % 2) * 128 + 128)
                W, BW = wblk[wb], Bw4[wb]
                mm(bank(0), [(W[:, 0, kc, cs], yg_[:, kc, :]) for kc in range(8)], [BW[0], Bygg[g % 2]], [PB[0]])
                mm(bank(1), [(W[:, 1, kc, cs], hg_[:, kc, :]) for kc in range(8)], [BW[1], Bhgg[g % 2]], [PB[1]])
                act(tg[0], bank(1), AF.Tanh, [PB[1]], [Btg[0]], scale=0.5)
                stt("dve", Ag[0], tg[0], 1.0, bank(0), ALU.add, ALU.mult, [Btg[0], PB[0]], [BAg[0]])
                mm(bank(2), [(W[:, 2, kc, cs], ys_[:, kc, :]) for kc in range(8)], [BW[2], Bysg[g % 2]], [PB[2]])
                mm(bank(3), [(W[:, 3, kc, cs], hg_[:, kc, :]) for kc in range(8)], [BW[3], Bhgg[g % 2]], [PB[3]])
                act(tg[1], bank(3), AF.Tanh, [PB[3]], [Btg[1]], scale=0.5)
                stt("dve", Ag[1], tg[1], 1.0, bank(2), ALU.add, ALU.mult, [Btg[1], PB[2]], [BAg[1]])
                tt("pool", M[:, oc, :], Ag[0], Ag[1], ALU.add, [BAg[0], BAg[1]], [BM])

        def p4_out_a(t):
            g, sl = divmod(t, 4)
            M, BM = Mg[0], BMg[0]
            xt, Bxt = xts[t % 3], Bxts[t % 3]
            x1, Bx1_ = x1s[t % 3], Bx1s[t % 3]
            if t + 2 < NT:
                dma("sp", xts[(t + 2) % 3], x_d[(t + 2) * 128:(t + 3) * 128, :], [], [Bxts[(t + 2) % 3]])
            lhs = [M[:, kc, sl * 128:(sl + 1) * 128] for kc in range(8)]
            for hf in range(2):
                bk = 4 + hf
                mm(bank(bk), [(lhs[kc], wout[:, kc, hf * 512:(hf + 1) * 512]) for kc in range(8)],
                   [BM, Bwout], [PB[bk]])
                stt("dve", x1[:, hf * 512:(hf + 1) * 512], bank(bk), 0.5, xt[:, hf * 512:(hf + 1) * 512],
                    ALU.mult, ALU.add, [PB[bk], Bxt], [Bx1_])
            norm_a(x1, Bx1_, junk, Bjunk, stats[t % 3], Bstats[t % 3])

        def p4_out_b1(t):
            dma("sp", out_d[t * 128:(t + 1) * 128, :], x1s[t % 3], [Bx1s[t % 3]], [Bx1[t]])
            act(xss[t % 2], x1s[t % 3], AF.Copy, [Bx1s[t % 3], Bstats[t % 3][2]], [Bxss[t % 2]],
                scale=stats[t % 3][2])

        def p4_out_b2(t):
            g, sl = divmod(t, 4)
            xs = xss[t % 2]
            tb_bank = 6 + (t % 2)
            tp = bank16(tb_bank).rearrange("p (c t) -> p c t", t=128)
            fns = [lambda e, c=c: e.transpose(out=tp[:, c, :], in_=xs[:, c * 128:(c + 1) * 128], identity=ident)
                   for c in range(8)]
            fw.op("pe", fns, [Bxss[t % 2], Bconst], [PB[tb_bank]])
            tt("dve", h2st[g % 2][:, :, sl * 128:(sl + 1) * 128], tp,
               gffn.unsqueeze(2).to_broadcast([128, 8, 128]), ALU.mult, [PB[tb_bank], Bconst], [Bh2st[g % 2]])
            if sl == 3:
                dma("sp", h2_d[:, :, g * 512:(g + 1) * 512], h2st[g % 2], [Bh2st[g % 2]], [Bh2[g]])
        p4_load(0)
        for t_ in range(2):
            dma("sp", xts[t_], x_d[t_ * 128:(t_ + 1) * 128, :], [], [Bxts[t_]])
        for g in range(NG):
            if g + 1 < NG:
                p4_load(g + 1)
            p4_merge(g)
            for sl in range(4):
                t = g * 4 + sl
                if t - 2 >= 0:
                    p4_out_b1(t - 2)
                p4_out_a(t)
                if t - 2 >= 0:
                    p4_out_b2(t - 2)
        JB = 2
        NJB = NJ // JB
        wfg_blk = lambda jb: w_fg_d[:, jb * 2048:(jb + 1) * 2048].rearrange("p (a b) -> p a b", b=256)
        wfu_blk = lambda jb: w_fu_d[:, jb * 2048:(jb + 1) * 2048].rearrange("p (a b) -> p a b", b=256)
        wfgb = [None] * NJB
        wfub = [None] * NJB
        Bwg = [Buf() for _ in range(NJB)]
        Bwu = [Buf() for _ in range(NJB)]
        NPRE = 4
        for jb in range(NPRE):
            wfgb[jb] = ar.alloc_at(m0 + jb * 4096, [8, 256], BF16)
            wfub[jb] = ar.alloc_at(m0 + jb * 4096 + 2048, [8, 256], BF16)
            cs = slice(jb * JB * 128, (jb + 1) * JB * 128)
            extra = (Bw4[2] + Bw4[3]) if jb == 0 else []
            dma("pool", wfgb[jb], wfg_blk(jb), [], [Bwg[jb]] + extra)
            dma("pool", wfub[jb], wfu_blk(jb), [], [Bwu[jb]])
        for t_ in (NT - 2, NT - 1):
            p4_out_b1(t_)
            p4_out_b2(t_)
        fw.barrier()
        ar.release(m0)
        print("arena hw phase4", ar.hw)

        m0 = ar.mark()
        ar.off = m0 + NPRE * 4096
        for jb in range(NPRE, NJB):
            wfgb[jb] = ar.alloc([8, 256], BF16)
            wfub[jb] = ar.alloc([8, 256], BF16)
        wfo = ar.alloc([NJ, 1024], BF16)
        Bwo = [Buf() for _ in range(NJB)]
        for jb in range(NPRE, NJB):
            cs = slice(jb * JB * 128, (jb + 1) * JB * 128)
            dep = [Bwu[jb - 3]] if jb - 3 >= NPRE else []
            dma("pool", wfgb[jb], wfg_blk(jb), dep, [Bwg[jb]])
            dma("pool", wfub[jb], wfu_blk(jb), [], [Bwu[jb]])
        for jb in range(NJB):
            dep = [Bwu[NJB - 1]] if jb == 0 else ([Bwo[jb - 3]] if jb >= 3 else [])
            dma("pool", wfo[:, jb * JB:(jb + 1) * JB, :].rearrange("p a b -> p (a b)"),
                w_fo_d[:, jb * JB * 1024:(jb + 1) * JB * 1024], dep, [Bwo[jb]])
        h2g = [ar.alloc([8, 512], BF16) for _ in range(2)]
        Bh2g = [Buf() for _ in range(2)]
        actT = ar.alloc([NJ, 512], BF16)
        Bact = [Buf() for _ in range(NJ)]
        sg = [ar.alloc([512], F32) for _ in range(2)]
        Bsg = [Buf() for _ in range(2)]
        x1r = [ar.alloc([D], F32) for _ in range(3)]
        Bx1r = [Buf() for _ in range(3)]
        fin = [ar.alloc([D], F32) for _ in range(2)]
        Bfin = [Buf() for _ in range(2)]
        Bout = [Buf() for _ in range(NT)]
        for g in range(NG):
            gs = slice(g * 512, (g + 1) * 512)
            h2, Bh2_ = h2g[g % 2], Bh2g[g % 2]
            dma("sp", h2, h2_d[:, :, gs], [Bh2[g]], [Bh2_])
            for j in range(NJ):
                js = slice((j % JB) * 128, (j % JB + 1) * 128)
                p = j % 2
                mm(bank(p), [(wfgb[j // JB][:, kc, js], h2[:, kc, :]) for kc in range(8)], [Bwg[j // JB], Bh2_], [PB[p]])
                mm(bank(2 + p), [(wfub[j // JB][:, kc, js], h2[:, kc, :]) for kc in range(8)], [Bwu[j // JB], Bh2_], [PB[2 + p]])
                act(sg[p], bank(p), AF.Silu, [PB[p]], [Bsg[p]])
                tt("dve", actT[:, j, :], sg[p], bank(2 + p), ALU.mult, [Bsg[p], PB[2 + p]], [Bact[j]])
            for sl in range(4):
                t = g * 4 + sl
                xr, Bxr = x1r[t % 3], Bx1r[t % 3]
                fo, Bfo = fin[t % 2], Bfin[t % 2]
                if t == 0:
                    for t_ in range(2):
                        dma("sp", x1r[t_], out_d[t_ * 128:(t_ + 1) * 128, :], [Bx1[t_]], [Bx1r[t_]])
                if t + 2 < NT:
                    dma("sp", x1r[(t + 2) % 3], out_d[(t + 2) * 128:(t + 3) * 128, :], [Bx1[t + 2]], [Bx1r[(t + 2) % 3]])
                if t >= 1:
                    dma("sp", out_d[(t - 1) * 128:t * 128, :], fin[(t - 1) % 2], [Bfin[(t - 1) % 2], Bx1[t - 1]], [Bout[t - 1]])
                for hf in range(2):
                    bk = 4 + 2 * (t % 2) + hf
                    mm(bank(bk), [(actT[:, j, sl * 128:(sl + 1) * 128], wfo[:, j, hf * 512:(hf + 1) * 512])
                                  for j in range(NJ)], Bact + Bwo, [PB[bk]])
                    tt("dve", fo[:, hf * 512:(hf + 1) * 512], bank(bk), xr[:, hf * 512:(hf + 1) * 512], ALU.add,
                       [PB[bk], Bxr], [Bfo])
        dma("sp", out_d[(NT - 1) * 128:NT * 128, :], fin[(NT - 1) % 2], [Bfin[(NT - 1) % 2], Bx1[NT - 1]], [Bout[NT - 1]])
        print("arena hw phase5", ar.hw)
        fw.barrier()
        fw.emit_all()
    return nc


def _kc_layout(w):
    k, n = w.shape
    c = k // 128
    return np.ascontiguousarray(w.reshape(c, 128, n).transpose(1, 0, 2).reshape(128, c * n))


def _kc_blocks(w, bc):
    k, n = w.shape
    c = k // 128
    nb = n // bc
    return np.ascontiguousarray(w.reshape(c, 128, nb, bc).transpose(1, 2, 0, 3).reshape(128, nb * c * bc))


_NC_CACHE = {}


def kernel(x, norm_mix_g, w_in, gla_gate_up_fwd, gla_gate_bias_fwd, gla_gate_up_bwd,
           gla_gate_bias_bwd, gla_out_norm_g, w_o_gla, swa_q_norm_g, swa_k_norm_g,
           swa_sinks, w_o_swa, w_out, norm_ffn_g, w_ffn_in, w_ffn_out):
    f32 = np.float32
    x = np.asarray(x, f32)
    w_in = np.asarray(w_in, f32)[0]
    shared = {}
    shared["w_lr"] = _kc_layout(w_in[:, 3072:3104])
    shared["w_sq"] = _kc_layout(w_in[:, 3104:4128])
    shared["w_skv"] = _kc_layout(w_in[:, 4128:4640])
    shared["w_ga"] = _kc_blocks(w_in[:, 4640:5664], 256)
    shared["w_gb"] = _kc_blocks(w_in[:, 5664:6688], 256)
    shared["w_os"] = _kc_blocks(np.asarray(w_o_swa, f32)[0], 256)
    shared["w_og"] = _kc_blocks(np.asarray(w_o_gla, f32)[0], 256)
    shared["w_out"] = _kc_layout(np.asarray(w_out, f32)[0])
    for h in range(4):
        blk = np.concatenate([w_in[:, h * 128:(h + 1) * 128], w_in[:, 512 + h * 128:512 + (h + 1) * 128],
                              w_in[:, 1024 + h * 256:1024 + (h + 1) * 256],
                              w_in[:, 2048 + h * 256:2048 + (h + 1) * 256]], axis=1)
        shared[f"w_gh{h}"] = _kc_layout(blk)
    wfi = np.asarray(w_ffn_in, f32)[0]
    shared["w_fg"] = _kc_blocks(wfi[:, :DFF], 256)
    shared["w_fu"] = _kc_blocks(wfi[:, DFF:], 256)
    shared["w_fo"] = _kc_layout(np.asarray(w_ffn_out, f32)[0])
    z16 = np.zeros((16, 512), f32)
    shared["upaug_f"] = np.ascontiguousarray(np.concatenate(
        [np.asarray(gla_gate_up_fwd, f32)[0], z16, np.asarray(gla_gate_bias_fwd, f32)[0][None, :]], axis=0))
    shared["upaug_b"] = np.ascontiguousarray(np.concatenate(
        [z16, np.asarray(gla_gate_up_bwd, f32)[0], np.asarray(gla_gate_bias_bwd, f32)[0][None, :]], axis=0))
    shared["gmix_col"] = np.ascontiguousarray(np.asarray(norm_mix_g, f32)[0].reshape(8, 128).T)
    shared["gffn_col"] = np.ascontiguousarray(np.asarray(norm_ffn_g, f32)[0].reshape(8, 128).T)
    shared["gout"] = np.asarray(gla_out_norm_g, f32).reshape(1, 256)
    shared["gq"] = np.asarray(swa_q_norm_g, f32).reshape(1, 128)
    shared["gk"] = np.asarray(swa_k_norm_g, f32).reshape(1, 128)
    shared["sinks"] = np.asarray(swa_sinks, f32).reshape(1, 8)
    half = 64
    inv_freq = (np.float32(10000.0) ** (-np.arange(half, dtype=f32) / f32(half))).astype(f32)
    ang = (np.arange(S, dtype=f32)[:, None] * inv_freq[None, :]).astype(f32)
    cos = np.cos(ang).astype(f32).reshape(NT, 128, half).transpose(1, 0, 2).reshape(128, NT * half)
    sin = np.sin(ang).astype(f32).reshape(NT, 128, half).transpose(1, 0, 2).reshape(128, NT * half)
    shared["cos_t"] = np.ascontiguousarray(cos)
    shared["sin_t"] = np.ascontiguousarray(sin)

    if "nc" not in _NC_CACHE:
        _NC_CACHE["nc"] = build_program()
    nc = _NC_CACHE["nc"]
    in_maps = []
    for c in range(8):
        m = dict(shared)
        m["x"] = np.ascontiguousarray(x[c])
        in_maps.append(m)
    res = run_bass_kernel_spmd(nc, in_maps, core_ids=list(range(8)))
    return np.stack([np.asarray(r["out"], dtype=f32) for r in res.results], axis=0)
```
